# Optimizing a Trainium2 kernel written in Bass

```python
import math
import jax, jax.numpy as jnp
from jax import lax
import numpy as np

D_MODEL = 1024
BATCH = 8
SEQ = 2048
DEPTH = 1

GLA_HEADS = 4
GLA_DK = D_MODEL // 2
GLA_DV = D_MODEL
GLA_HEAD_DK = GLA_DK // GLA_HEADS
GLA_HEAD_DV = GLA_DV // GLA_HEADS
GLA_GATE_RANK = 16
GLA_GATE_TAU = 16.0
GLA_CHUNK = 64
POOL_WIDTH = D_MODEL
POOL_WINDOWS = (2, 4, 8, 16)
POOL_GROUPS = len(POOL_WINDOWS)
POOL_GROUP_DIM = POOL_WIDTH // POOL_GROUPS
NORM_EPS = 1e-5
DEEPNORM_ALPHA = (2.0 * DEPTH) ** 0.25
DEEPNORM_BETA = (8.0 * DEPTH) ** -0.25

SPLIT_SIZES = (GLA_DK, GLA_DK, GLA_DV, GLA_DV, GLA_GATE_RANK,
               POOL_WIDTH, POOL_WIDTH, D_MODEL, D_MODEL)
D_IN = int(sum(SPLIT_SIZES))
SPLIT_POINTS = tuple(int(p) for p in np.cumsum(SPLIT_SIZES)[:-1])

kernel_name = "hybrid_gla_pool_gated_deepnorm"


def gla_chunked(q, k, v, log_a):
    B, S, H, dk = q.shape
    dv = v.shape[-1]
    n = S // GLA_CHUNK

    def to_chunks(t):
        return t.reshape(B, n, GLA_CHUNK, H, t.shape[-1]).transpose(1, 0, 3, 2, 4)

    qc, kc, vc, ac = to_chunks(q), to_chunks(k), to_chunks(v), to_chunks(log_a)
    causal = jnp.tril(jnp.ones((GLA_CHUNK, GLA_CHUNK), dtype=bool))[None, None, :, :, None]

    def step(state, inp):
        qi, ki, vi, ai = inp
        qi = qi.astype(jnp.float32)
        ki = ki.astype(jnp.float32)
        vi = vi.astype(jnp.float32)
        b = jnp.cumsum(ai.astype(jnp.float32), axis=2)
        o_inter = jnp.einsum('bhck,bhkv->bhcv', qi * jnp.exp(b), state)
        diff = b[:, :, :, None, :] - b[:, :, None, :, :]
        decay = jnp.exp(jnp.where(causal, diff, -jnp.inf))
        scores = jnp.einsum('bhik,bhjk,bhijk->bhij', qi, ki, decay)
        o_intra = jnp.einsum('bhij,bhjv->bhiv', scores, vi)
        b_last = b[:, :, -1:, :]
        k_dec = ki * jnp.exp(b_last - b)
        new_state = (jnp.exp(b_last[:, :, 0, :])[..., None] * state
                     + jnp.einsum('bhck,bhcv->bhkv', k_dec, vi))
        return new_state, o_inter + o_intra

    state0 = jnp.zeros((B, H, dk, dv), jnp.float32)
    _, o = lax.scan(step, state0, (qc, kc, vc, ac))
    return o.transpose(1, 0, 3, 2, 4).reshape(B, S, H, dv)


def causal_multiscale_pool(u):
    B, S, G, Cg = u.shape
    u32 = u.astype(jnp.float32)
    cs = jnp.cumsum(u32, axis=1)
    cs = jnp.concatenate([jnp.zeros_like(cs[:, :1]), cs], axis=1)
    t = jnp.arange(S)
    win = jnp.array(POOL_WINDOWS, dtype=jnp.int32)
    start = jnp.maximum(t[:, None] + 1 - win[None, :], 0)
    lower = cs[:, start, jnp.arange(G)[None, :]]
    count = jnp.minimum(t[:, None] + 1, win[None, :]).astype(jnp.float32)
    pooled = (cs[:, 1:] - lower) / count[None, :, :, None]
    return pooled - u32


def rms_norm_heads(o, w):
    o32 = o.astype(jnp.float32)
    o32 = o32 * lax.rsqrt(jnp.mean(jnp.square(o32), axis=-1, keepdims=True) + NORM_EPS)
    B, S = o.shape[:2]
    return o32.reshape(B, S, -1) * w


def layer_norm(r, w, b):
    r32 = r.astype(jnp.float32)
    mu = jnp.mean(r32, axis=-1, keepdims=True)
    var = jnp.mean(jnp.square(r32 - mu), axis=-1, keepdims=True)
    return (r32 - mu) * lax.rsqrt(var + NORM_EPS) * w + b


def hybrid_layer(x, w_in, w_gate_up, b_gate, gn_w, pool_w, pool_b, pool_scale,
                 w_a, w_b, w_o, ln_w, ln_b):
    B, S, _ = x.shape
    h = jnp.einsum('bsd,de->bse', x, w_in)
    q, k, v, g, a_low, u, z, gate_a, gate_b = jnp.split(h, SPLIT_POINTS, axis=-1)

    log_a = jax.nn.log_sigmoid(
        (jnp.einsum('bsr,rk->bsk', a_low, w_gate_up) + b_gate).astype(jnp.float32)) / GLA_GATE_TAU
    q = q.reshape(B, S, GLA_HEADS, GLA_HEAD_DK) * (GLA_HEAD_DK ** -0.5)
    k = k.reshape(B, S, GLA_HEADS, GLA_HEAD_DK)
    v = v.reshape(B, S, GLA_HEADS, GLA_HEAD_DV)
    log_a = log_a.reshape(B, S, GLA_HEADS, GLA_HEAD_DK)
    o = gla_chunked(q, k, v, log_a)
    y_a = rms_norm_heads(o, gn_w) * jax.nn.silu(g.astype(jnp.float32))

    p = causal_multiscale_pool(u.reshape(B, S, POOL_GROUPS, POOL_GROUP_DIM))
    p = jnp.einsum('bsgc,gcd->bsgd', p, pool_w).reshape(B, S, POOL_WIDTH) + pool_b
    y_b = p * pool_scale * jax.nn.silu(z.astype(jnp.float32))

    merged = (jax.nn.sigmoid(gate_a.astype(jnp.float32)) * jnp.einsum('bse,ed->bsd', y_a, w_a)
              + jax.nn.sigmoid(gate_b.astype(jnp.float32)) * jnp.einsum('bse,ed->bsd', y_b, w_b))
    out = jnp.einsum('bsd,de->bse', merged, w_o)

    return layer_norm(DEEPNORM_ALPHA * x.astype(jnp.float32) + out, ln_w, ln_b).astype(x.dtype)


def setup_inputs(seed: int = 0) -> dict:
    key = jax.random.key(seed)
    ks = jax.random.split(key, 14)
    L = DEPTH
    f32 = jnp.float32
    x = jax.random.normal(ks[0], (BATCH, SEQ, D_MODEL), f32)
    w_in = jax.random.normal(ks[1], (L, D_MODEL, D_IN), f32) * D_MODEL ** -0.5
    w_gate_up = jax.random.normal(ks[2], (L, GLA_GATE_RANK, GLA_DK), f32) * GLA_GATE_RANK ** -0.5
    b_gate = 0.1 * jax.random.normal(ks[3], (L, GLA_DK), f32)
    gn_w = 1.0 + 0.02 * jax.random.normal(ks[4], (L, GLA_DV), f32)
    pool_w = jax.random.normal(ks[5], (L, POOL_GROUPS, POOL_GROUP_DIM, POOL_GROUP_DIM), f32) * POOL_GROUP_DIM ** -0.5
    pool_b = 0.02 * jax.random.normal(ks[6], (L, POOL_WIDTH), f32)
    pool_scale = 1.0 + 0.02 * jax.random.normal(ks[7], (L, POOL_WIDTH), f32)
    w_a = jax.random.normal(ks[8], (L, GLA_DV, D_MODEL), f32) * (GLA_DV ** -0.5) * DEEPNORM_BETA
    w_b = jax.random.normal(ks[9], (L, POOL_WIDTH, D_MODEL), f32) * (POOL_WIDTH ** -0.5) * DEEPNORM_BETA
    w_o = jax.random.normal(ks[10], (L, D_MODEL, D_MODEL), f32) * (D_MODEL ** -0.5) * DEEPNORM_BETA
    ln_w = 1.0 + 0.02 * jax.random.normal(ks[11], (L, D_MODEL), f32)
    ln_b = 0.02 * jax.random.normal(ks[12], (L, D_MODEL), f32)
    return {"x": x, "w_in": w_in, "w_gate_up": w_gate_up, "b_gate": b_gate, "gn_w": gn_w,
            "pool_w": pool_w, "pool_b": pool_b, "pool_scale": pool_scale,
            "w_a": w_a, "w_b": w_b, "w_o": w_o, "ln_w": ln_w, "ln_b": ln_b}


def reference(x, w_in, w_gate_up, b_gate, gn_w, pool_w, pool_b, pool_scale,
              w_a, w_b, w_o, ln_w, ln_b):
    for l in range(DEPTH):
        x = hybrid_layer(x, w_in[l], w_gate_up[l], b_gate[l], gn_w[l], pool_w[l], pool_b[l],
                         pool_scale[l], w_a[l], w_b[l], w_o[l], ln_w[l], ln_b[l])
    return x
```

```python
import numpy as np
import ml_dtypes
from contextlib import ExitStack
import concourse.bass as bass
import concourse.mybir as mybir
from concourse.bass_utils import run_bass_kernel_spmd

F32 = mybir.dt.float32
BF16 = mybir.dt.bfloat16
AF = mybir.ActivationFunctionType
ALU = mybir.AluOpType

T = 2048
NT = 16
D = 1024
KC = 8
DIN = 7184
Q0, K0, V0, G0, AL0, U0, Z0, GA0, GB0 = 0, 512, 1024, 2048, 3072, 3088, 4112, 5136, 6160
DK = 128
EPS = 1e-5
ALPHA = 2.0 ** 0.25
NW = 3
N_CORES = 8

ENGS = ("pe", "act", "dve", "pool", "sp")


class Op:
    __slots__ = ("eng", "fn", "deps", "sig", "lane", "ticket", "is_dma")

    def __init__(self, eng, fn, deps, lane, is_dma):
        self.eng = eng
        self.fn = fn
        self.deps = deps
        self.sig = False
        self.lane = lane
        self.ticket = None
        self.is_dma = is_dma


class Buf:
    __slots__ = ("w", "r", "excl")

    def __init__(self, excl=False):
        self.w = None
        self.r = []
        self.excl = excl


class Prog:
    def __init__(self):
        self.ops = {e: [] for e in ENGS}
        self.last = {e: None for e in ENGS}

    def add(self, eng, fn, reads=(), writes=(), lane=None, extra=()):
        is_dma = lane is not None
        deps = []
        writes = list(writes) + [b for b in reads if b.excl]
        reads = [b for b in reads if not b.excl]
        for b in reads:
            if b.w is not None:
                deps.append(b.w)
        for b in writes:
            if b.w is not None:
                deps.append(b.w)
            deps.extend(b.r)
        deps.extend([d for d in extra if d is not None])
        op = Op(eng, fn, deps, lane if is_dma else eng, is_dma)
        for b in reads:
            b.r.append(op)
        for b in writes:
            b.w = op
            b.r = []
        self.ops[eng].append(op)
        if not is_dma:
            self.last[eng] = op
        return op

    def barrier(self, engs=("pe", "act", "dve")):
        lasts = [self.last[e] for e in engs if self.last[e] is not None]
        saved = dict(self.last)
        for e in engs:
            self.add(e, lambda eng: None, extra=[l for l in lasts if l.eng != e])
        self.last = saved
        return lasts

    def finalize(self):
        for e in ENGS:
            for op in self.ops[e]:
                keep = []
                for d in op.deps:
                    if d is op:
                        continue
                    if (not d.is_dma) and d.eng == op.eng and op.eng in ("pe", "sp"):
                        continue
                    keep.append(d)
                op.deps = keep
                for d in keep:
                    d.sig = True
        cnt = {}
        for e in ENGS:
            for op in self.ops[e]:
                if op.is_dma:
                    op.sig = True
                if op.sig:
                    inc = 16 if op.is_dma else 1
                    cnt[op.lane] = cnt.get(op.lane, 0) + inc
                    op.ticket = cnt[op.lane]
        return sorted(cnt.keys())

    def replay(self, eng_name, eng, sems):
        waited = {}
        for op in self.ops[eng_name]:
            need = {}
            for d in op.deps:
                if need.get(d.lane, 0) < d.ticket:
                    need[d.lane] = d.ticket
            for lane, val in need.items():
                if waited.get(lane, 0) < val:
                    eng.wait_ge(sems[lane], val)
                    waited[lane] = val
            ins = op.fn(eng)
            if op.sig:
                assert ins is not None
                ins.then_inc(sems[op.lane], 16 if op.is_dma else 1)


def _pool_mats():
    pm = np.zeros((128, 12, 128), np.float32)
    for g in range(4):
        w = 2 ** (g + 1)
        for t in range(128):
            for s in range(t - w + 1, t + 1):
                if s >= 0:
                    pm[s, g * 3 + 0, t] += 1.0 / w
                else:
                    pm[s + 128, g * 3 + 1, t] += 1.0 / w
            pm[t, g * 3 + 0, t] -= 1.0
            cnt = min(t + 1, w)
            for s in range(max(0, t - w + 1), t + 1):
                pm[s, g * 3 + 2, t] += 1.0 / cnt
            pm[t, g * 3 + 2, t] -= 1.0
    return pm.astype(ml_dtypes.bfloat16)


def _consts():
    ident = np.eye(128, dtype=np.float32)
    lt = np.triu(np.ones((128, 128), np.float32))
    return {
        "ident_bf": ident.astype(ml_dtypes.bfloat16),
        "ident_f": ident,
        "lt_bf": lt.astype(ml_dtypes.bfloat16),
        "pmat": _pool_mats(),
        "invc": np.tile((1.0 / np.arange(1, 17, dtype=np.float32))[None, :], (128, 1)).astype(np.float32),
    }


def build_program(dbg=(), stop_after='E'):
    nc = bass.Bass("TRN2", target_bir_lowering=False)
    _order = ['A', 'B', 'C', 'D', 'E']

    def _en(n):
        return _order.index(n) <= _order.index(stop_after)

    lasts = []
    stores = []

    def din(name, shape, dt=F32):
        return nc.dram_tensor(name, shape, dt, kind="ExternalInput").ap()

    x = din("x", [T, D])
    w_in = din("w_in", [D, DIN])
    w_gate_up = din("w_gate_up", [16, 512])
    b_gate = din("b_gate", [1, 512])
    gn_w = din("gn_w", [8, 128])
    pool_w = din("pool_w", [4, 256, 256])
    pool_b = din("pool_b", [8, 128])
    pool_scale = din("pool_scale", [8, 128])
    w_a = din("w_a", [D, D])
    w_b = din("w_b", [D, D])
    w_o = din("w_o", [D, D])
    ln_w = din("ln_w", [1, D])
    ln_b = din("ln_b", [1, D])
    ident_bf_d = din("ident_bf", [128, 128], BF16)
    ident_f_d = din("ident_f", [128, 128], F32)
    lt_bf_d = din("lt_bf", [128, 128], BF16)
    pmat_d = din("pmat", [128, 12, 128], BF16)
    invc_d = din("invc", [128, 16], F32)
    out = nc.dram_tensor("out", [T, D], F32, kind="ExternalOutput").ap()
    dbg_out = {}

    P = Prog()

    def PE(fn, r=(), w=(), extra=()):
        return P.add("pe", fn, r, w, extra=extra)

    def ACT(fn, r=(), w=(), extra=()):
        return P.add("act", fn, r, w, extra=extra)

    def DVE(fn, r=(), w=(), extra=()):
        return P.add("dve", fn, r, w, extra=extra)

    def POOL(fn, r=(), w=(), extra=()):
        return P.add("pool", fn, r, w, extra=extra)

    with ExitStack() as es:
        def sb(name, shape, dt):
            return es.enter_context(nc.sbuf_tensor(name, shape, dt))

        def ps(name, shape, dt):
            return es.enter_context(nc.psum_tensor(name, shape, dt))

        xT = sb("xT", [128, KC, T], BF16)
        b_xT = Buf()
        wslot = [sb(f"ws{i}", [128, KC, 512], BF16) for i in range(NW)]
        wbuf = [Buf() for _ in range(NW)]
        A1 = sb("A1", [128, 16384], BF16)
        A2 = sb("A2", [128, 16384], BF16)
        A3 = sb("A3", [128, 16384], BF16)
        A4 = sb("A4", [128, 16384], BF16)
        ident = sb("ident", [128, 128], BF16)
        lt = sb("lt", [128, 128], BF16)
        pmat = sb("pmat_s", [128, 12, 128], BF16)
        wg_aug = sb("wg_aug", [32, 512], BF16)
        wal = sb("wal", [128, KC, 16], BF16)
        poolw = sb("poolw", [128, 8, 256], BF16)
        colsT = sb("colsT", [128, 24], F32)
        cols16 = sb("cols16", [128, 8], F32)
        blast = sb("blast", [128, NT, 4], F32)
        eblast = sb("eblast", [128, NT, 4], F32)
        small = sb("small", [128, 64], F32)
        cpow = sb("cpow", [128, 16], F32)
        ones_t = sb("ones_t", [128, 128], F32)
        invc = sb("invc_s", [128, 16], F32)
        pfix = sb("pfix", [128, 16], F32)
        b_invc = Buf()
        b_ones = Buf()
        junk2 = sb("junk2", [128, D], BF16)
        small2 = sb("small2", [128, 32], F32)
        xt3 = sb("xt3", [128, D], F32)
        sgm_t = sb("sgm_t", [128, 2, 512], F32)
        b_cpow = Buf()
        b_ident, b_lt, b_pmat, b_wg, b_wal, b_poolw, b_cols, b_cols16 = [Buf() for _ in range(8)]

        pb = [ps(f"pb{i}", [128, 512], F32) for i in range(8)]
        pbT = [p[:].bitcast(BF16) for p in pb]
        bb = [Buf(excl=True) for _ in range(8)]
        bank_ctr = [0]

        NB = [4]

        def next_bank(n=None, base=0):
            n = NB[0] if n is None else n
            i = base + bank_ctr[0] % n
            bank_ctr[0] += 1
            return i

        wctr = [0]

        def load_w(src, c0):
            i = wctr[0] % NW
            wctr[0] += 1
            srcv = src[:, c0:c0 + 512].rearrange("(kc p) n -> p kc n", p=128)
            P.add("pool", lambda e, i=i, srcv=srcv: e.dma_start(out=wslot[i][:], in_=srcv),
                  writes=[wbuf[i]], lane=f"w{i}")
            return i

        def mkslot(ap, lane, buf=None):
            return (ap, buf if buf is not None else Buf(), lane)

        R = [mkslot(wslot[i][:], f"w{i}", wbuf[i]) for i in range(NW)]
        XC = mkslot(A2[:, 4096:8192].rearrange("p (k n) -> p k n", k=KC), "wxC")
        XA = mkslot(A2[:, 8192:12288].rearrange("p (k n) -> p k n", k=KC), "wxA")
        XB = mkslot(A2[:, 12288:16384].rearrange("p (k n) -> p k n", k=KC), "wxB")
        X123 = [mkslot(A1[:, i * 4096:(i + 1) * 4096].rearrange("p (k n) -> p k n", k=KC), f"wx{i}")
                for i in range(3)]

        def ld(slot, src, c0, extra=()):
            ap, buf, lane = slot
            srcv = src[:, c0:c0 + 512].rearrange("(kc p) n -> p kc n", p=128)
            P.add("pool", lambda e, ap=ap, srcv=srcv: e.dma_start(out=ap, in_=srcv),
                  writes=[buf], lane=lane, extra=extra)
            return (ap, buf)

        def scale_w(wt, cols):
            W, B = wt
            DVE(lambda e, W=W, cols=cols: e.tensor_tensor(
                W, W, cols.unsqueeze(2).broadcast_to([128, KC, 512]), ALU.mult),
                r=[B, b_cols, b_cols16], w=[B])

        P.add("sp", lambda e: e.dma_start(out=ident[:], in_=ident_bf_d[:, :]), writes=[b_ident], lane="c_id")
        P.add("sp", lambda e: e.dma_start(out=lt[:], in_=lt_bf_d[:, :]), writes=[b_lt], lane="c_lt")
        P.add("sp", lambda e: e.dma_start(out=invc[:], in_=invc_d[:, :]), writes=[b_invc], lane="c_pm")


        NXB = 8
        xb = [A2[:, s * 1024:(s + 1) * 1024] for s in range(NXB)]
        b_xb = [Buf() for _ in range(NXB)]
        b_xTb = [Buf() for _ in range(4)]
        p0_last = [None] * NT
        def x_load(t):
            P.add("pool", lambda e, t=t: e.dma_start(out=xb[t % NXB], in_=x[t * 128:(t + 1) * 128, :]),
                  writes=[b_xb[t % NXB]], lane=f"xb{t % NXB}")

        b_wg1 = Buf()
        for t in range(4):
            x_load(t)
        P.add("pool", lambda e: e.dma_start(
            out=wal[:], in_=w_in[:, AL0:AL0 + 16].rearrange("(kc p) n -> p kc n", p=128)),
            writes=[b_wal], lane="c_wal")
        P.add("pool", lambda e: e.dma_start(out=wg_aug[0:16, :], in_=w_gate_up[:, :]), writes=[b_wg], lane="c_wg")
        P.add("pool", lambda e: e.dma_start(out=wg_aug[16:17, :], in_=b_gate[:, :]), writes=[b_wg1], lane="c_wg1")
        wq = ld(R[0], w_in, Q0)
        for t in range(4, 8):
            x_load(t)
        wk = ld(R[1], w_in, K0)
        vW = [None, None]
        gW = [None, None]
        P.add("pool", lambda e: e.dma_start(
            out=poolw[:], in_=pool_w.rearrange("g (cc p) d -> p (g cc) d", p=128)),
            writes=[b_poolw], lane="c_pw")
        POOL(lambda e: e.memset(cpow[:, 0:4], -0.5), w=[b_cpow])
        POOL(lambda e: e.memset(cpow[:, 4:6], 256.0 * EPS), w=[b_cpow])
        POOL(lambda e: e.memset(cpow[:, 6:7], EPS), w=[b_cpow])
        POOL(lambda e: e.memset(cpow[:, 7:8], -1.0), w=[b_cpow])
        POOL(lambda e: e.memset(cpow[:, 8:9], -1.0 / D), w=[b_cpow])
        POOL(lambda e: e.memset(cpow[:, 9:10], 1.0 / D), w=[b_cpow])
        POOL(lambda e: e.memset(cpow[:, 10:11], -1.0 / D), w=[b_cpow])
        POOL(lambda e: e.memset(cpow[:, 11:12], 1.0 / D), w=[b_cpow])

        def p0_tile(t):
            s = t % NXB
            bank = next_bank()
            for k in range(KC):
                p0_last[t] = PE(lambda e, k=k, s=s, bank=bank: e.transpose(
                    pbT[bank][:, k * 128:(k + 1) * 128], xb[s][:, k * 128:(k + 1) * 128], ident[:]),
                   r=[b_xb[s], b_ident], w=[bb[bank]])
            src = pbT[bank].rearrange("p (k t) -> p k t", k=KC)
            dst = xT[:, :, t * 128:(t + 1) * 128]
            if t % 2 == 0:
                DVE(lambda e, dst=dst, src=src: e.tensor_copy(dst, src), r=[bb[bank]], w=[b_xT, b_xTb[t // 4]])
            else:
                ACT(lambda e, dst=dst, src=src: e.copy(dst, src), r=[bb[bank]], w=[b_xT, b_xTb[t // 4]])

        for t in range(4):
            p0_tile(t)

        identf = A3[:, 4096:4096 + 256].bitcast(F32)
        rows = A3[:, 4352:4352 + 256].bitcast(F32)
        b_identf, b_rows = Buf(), Buf()
        P.add("sp", lambda e: e.dma_start(out=identf, in_=ident_f_d[:, :]), writes=[b_identf], lane="c_if")
        b_rows1, b_rows2 = Buf(), Buf()
        P.add("sp", lambda e: e.dma_start(out=rows[0:8, :], in_=gn_w[:, :]), writes=[b_rows], lane="c_rows")
        P.add("sp", lambda e: e.dma_start(out=rows[8:16, :], in_=pool_b[:, :]), writes=[b_rows1], lane="c_rows1")
        P.add("sp", lambda e: e.dma_start(out=rows[16:24, :], in_=pool_scale[:, :]), writes=[b_rows2], lane="c_rows2")
        PE(lambda e: e.transpose(pb[2][:, 0:24], rows[0:24, :], identf[0:24, 0:24]),
           r=[b_rows, b_rows1, b_rows2, b_identf], w=[bb[2]])
        DVE(lambda e: e.tensor_copy(colsT[:], pb[2][:, 0:24]), r=[bb[2]], w=[b_cols])
        DVE(lambda e: e.tensor_scalar(cols16[:], colsT[:, 0:8], 16.0, None, ALU.mult), r=[b_cols], w=[b_cols16])

        qbT = A1[:, 0:8192].rearrange("p (h t) -> p h t", h=4)
        kbT = A1[:, 8192:16384].rearrange("p (h t) -> p h t", h=4)
        EpT = A3[:, 0:8192].rearrange("p (h t) -> p h t", h=4)
        EmT = A3[:, 8192:16384].rearrange("p (h t) -> p h t", h=4)
        y_aT = A3[:].rearrange("p (k t) -> p k t", k=KC)
        kdT = A4[:, 0:8192].rearrange("p (h t) -> p h t", h=4)
        alT = A4[0:32, 8192:10240]
        la_hi = [A4[:, 10240 + s * 512:10240 + (s + 1) * 512] for s in range(2)]
        la_lo = [A4[:, 11264 + s * 512:11264 + (s + 1) * 512] for s in range(2)]
        etmp = [A4[:, 12288 + s * 1024:12288 + (s + 1) * 1024].bitcast(F32) for s in range(2)]
        b_alT, b_Ep, b_Em, b_qb, b_kb, b_kd = [Buf() for _ in range(6)]
        b_la = [Buf() for _ in range(2)]
        b_et = [Buf() for _ in range(2)]
        b_EpT = [Buf() for _ in range(NT)]
        b_EmT = [Buf() for _ in range(NT)]

        b_alTb = [Buf() for _ in range(4)]
        DVE(lambda e: e.memset(alT[0:32, :], 1.0), w=b_alTb)
        DVE(lambda e: e.memset(ones_t[:, :], 1.0), w=[b_ones])

        def a0_blk(tb):
            bank = next_bank()
            for k in range(KC):
                PE(lambda e, k=k, tb=tb, bank=bank: e.matmul(
                    pb[bank][0:16, :], wal[:, k, :], xT[:, k, tb * 512:(tb + 1) * 512],
                    start=(k == 0), stop=(k == KC - 1)), r=[b_wal, b_xTb[tb]], w=[bb[bank]])
            DVE(lambda e, tb=tb, bank=bank: e.tensor_copy(alT[0:16, tb * 512:(tb + 1) * 512], pb[bank][0:16, :]),
                r=[bb[bank]], w=[b_alTb[tb]])

        a0_blk(0)

        cs = [A4[:, 10240 + s * 1024:10240 + (s + 1) * 1024].bitcast(F32) for s in range(2)]

        def a1_pre(u):
            s = u % 2
            tb, h = u // 4, u % 4
            bpre = 4 + s
            PE(lambda e, tb=tb, h=h, bpre=bpre: e.matmul(
                pb[bpre][:, :], wg_aug[0:17, h * 128:(h + 1) * 128], alT[0:17, tb * 512:(tb + 1) * 512],
                start=True, stop=True), r=[b_alTb[tb], b_wg, b_wg1], w=[bb[bpre]])
            ACT(lambda e, s=s, bpre=bpre: e.activation(etmp[s], pb[bpre][:, :], AF.Exp, scale=-1.0),
                r=[bb[bpre]], w=[b_et[s]])
            ACT(lambda e, s=s: e.activation(etmp[s], etmp[s], AF.Ln, bias=1.0), r=[b_et[s]], w=[b_et[s]])

        def a1_hilo(u):
            s = u % 2
            for c in range(4):
                DVE(lambda e, s=s, c=c: e.tensor_tensor_scan(
                    cs[s][:, c * 128:(c + 1) * 128], ones_t[:, :], etmp[s][:, c * 128:(c + 1) * 128], 0.0,
                    ALU.mult, ALU.add), r=[b_et[s], b_ones], w=[b_la[s]])

        def a1_cum(u):
            s = u % 2
            tb, h = u // 4, u % 4
            sl = slice(tb * 512, (tb + 1) * 512)
            ACT(lambda e, s=s, h=h, sl=sl: e.activation(EpT[:, h, sl], cs[s], AF.Exp, scale=-1.0 / 16.0),
                r=[b_la[s]], w=[b_EpT[u]])
            ACT(lambda e, s=s, h=h, sl=sl: e.activation(EmT[:, h, sl], cs[s], AF.Exp, scale=1.0 / 16.0),
                r=[b_la[s]], w=[b_EmT[u]])
            ACT(lambda e, s=s, h=h, tb=tb: e.activation(
                eblast[:, tb * 4:(tb + 1) * 4, h], cs[s].rearrange("p (c t) -> p c t", c=4)[:, :, 127],
                AF.Exp, scale=-1.0 / 16.0), r=[b_la[s]], w=[b_eblast[u]])

        b_blast = [Buf() for _ in range(NT)]
        b_eblast = [Buf() for _ in range(NT)]
        b_qbb = [[Buf() for _ in range(4)] for _ in range(4)]
        b_kbb = [[Buf() for _ in range(4)] for _ in range(4)]
        b_kdb = [[Buf() for _ in range(4)] for _ in range(4)]

        def proj_qk(kind, h, tb):
            W, B = wq if kind == 0 else wk
            dstT = qbT if kind == 0 else kbT
            dbuf = b_qbb if kind == 0 else b_kbb
            bank = next_bank()
            for k in range(KC):
                PE(lambda e, k=k, h=h, tb=tb, bank=bank, W=W: e.matmul(
                    pb[bank][:, :], W[:, k, h * 128:(h + 1) * 128], xT[:, k, tb * 512:(tb + 1) * 512],
                    start=(k == 0), stop=(k == KC - 1)), r=[B, b_xTb[tb]], w=[bb[bank]])
            if kind == 0:
                DVE(lambda e, h=h, tb=tb, bank=bank: e.tensor_scalar(
                    qbT[:, h, tb * 512:(tb + 1) * 512], pb[bank][:, :], DK ** -0.5, None, ALU.mult),
                    r=[bb[bank]], w=[dbuf[h][tb]])
            else:
                ACT(lambda e, h=h, tb=tb, bank=bank: e.copy(
                    kbT[:, h, tb * 512:(tb + 1) * 512], pb[bank][:, :]), r=[bb[bank]], w=[dbuf[h][tb]])

        fin_pending = []

        def finalize_one(tb, h):
            sl = slice(tb * 512, (tb + 1) * 512)
            DVE(lambda e, h=h, sl=sl: e.tensor_tensor(qbT[:, h, sl], qbT[:, h, sl], EpT[:, h, sl], ALU.mult),
                r=b_EpT[tb * 4:(tb + 1) * 4], w=[b_qbb[h][tb]])
            DVE(lambda e, h=h, sl=sl: e.tensor_tensor(kbT[:, h, sl], kbT[:, h, sl], EmT[:, h, sl], ALU.mult),
                r=b_EmT[tb * 4:(tb + 1) * 4], w=[b_kbb[h][tb]])
            POOL(lambda e, h=h, sl=sl, tb=tb: e.tensor_tensor(
                kdT[:, h, sl].rearrange("p (c t) -> p c t", c=4),
                kbT[:, h, sl].rearrange("p (c t) -> p c t", c=4),
                eblast[:, tb * 4:(tb + 1) * 4, h].unsqueeze(2).broadcast_to([128, 4, 128]), ALU.mult),
                r=[b_kbb[h][tb]] + b_eblast[tb * 4:(tb + 1) * 4], w=[b_kdb[h][tb]])

        def finalize_qk(tb):
            for h in range(4):
                fin_pending.append((tb, h))

        def drain_fin(k=1):
            for _ in range(k):
                if fin_pending:
                    finalize_one(*fin_pending.pop(0))

        fill = [(lambda kind=kind, h=h, tb=tb: proj_qk(kind, h, tb))
                for tb in range(4) for kind in range(2) for h in range(4)]
        for t in range(NT):
            if t + 4 < NT:
                p0_tile(t + 4)
            if t + 8 < NT:
                x_load(t + 8)
            if t == 7:
                vW[0] = ld(R[2], w_in, V0)
                gW[0] = ld(XA, w_in, G0)
                gW[1] = ld(XB, w_in, G0 + 512)
            if t == 11:
                vW[1] = ld(XC, w_in, V0 + 512, extra=[p0_last[15]])
            a1_pre(t)
            fill.pop(0)()
            a1_hilo(t)
            if t > 0:
                a1_cum(t - 1)
                if t % 4 == 0:
                    finalize_qk(t // 4 - 1)
            drain_fin(1)
            fill.pop(0)()
            if t + 4 < NT and (t + 4) % 4 == 3:
                a0_blk((t + 4) // 4)
        a1_cum(NT - 1)
        finalize_qk(3)
        assert not fill


        NVS = 2
        v_s = [A2[:, s * 1024:(s + 1) * 1024] for s in range(NVS)]
        sg_s = [A2[:, 2048 + s * 1024:2048 + (s + 1) * 1024] for s in range(NVS)]
        b_vs = [Buf() for _ in range(NVS)]
        b_sgs = [Buf() for _ in range(NVS)]

        TB = 8192
        kd_s = [A4[:, TB + s * 256:TB + (s + 1) * 256].rearrange("p (j t) -> p j t", j=2) for s in range(2)]
        sT_s = [A4[:, TB + 512 + s * 256:TB + 512 + (s + 1) * 256].rearrange("p (j t) -> p j t", j=2) for s in range(2)]
        yb_s = [A4[:, TB + 1024 + s * 512:TB + 1024 + (s + 1) * 512].rearrange("p (j f) -> p j f", j=2) for s in range(2)]
        yb_s.append(A4[:, TB + 6656:TB + 7168].rearrange("p (j f) -> p j f", j=2))
        Sbf_s = [[A4[:, TB + 2048 + (hp * 2 + s) * 512:TB + 2048 + (hp * 2 + s + 1) * 512].rearrange(
            "p (j f) -> p j f", j=2) for s in range(2)] for hp in range(2)]
        S_f = [A4[:, TB + 4096 + hp * 1024:TB + 4096 + (hp + 1) * 1024].bitcast(F32).rearrange(
            "p (j f) -> p j f", j=2) for hp in range(2)]
        junk = A4[:, TB + 6144:TB + 6656].bitcast(F32)
        ss_s = [small[:, s * 2:(s + 1) * 2] for s in range(2)]
        rs_s = [small[:, 4 + s * 2:4 + (s + 1) * 2] for s in range(2)]
        b_kds = [Buf() for _ in range(2)]
        b_sTs = [Buf() for _ in range(2)]
        b_ybs = [Buf() for _ in range(3)]
        b_Sbf = [[Buf() for _ in range(2)] for _ in range(2)]
        b_S = [Buf() for _ in range(2)]
        b_ss = [Buf() for _ in range(2)]
        b_rs = [Buf() for _ in range(2)]
        b_yaT = [Buf() for _ in range(NT)]

        ga0 = ld(R[0], w_in, GA0)
        wa0 = ld(R[1], w_a, 0)

        def proj_vg(c, kind, blk):
            slot = c % NVS
            W, B = (vW if kind == 0 else gW)[blk]
            bank = next_bank(2, 2)
            for k in range(KC):
                PE(lambda e, k=k, c=c, bank=bank, W=W: e.matmul(
                    pb[bank][:, :], xT[:, k, c * 128:(c + 1) * 128], W[:, k, :],
                    start=(k == 0), stop=(k == KC - 1)), r=[B, b_xT], w=[bb[bank]])
            if kind == 0:
                DVE(lambda e, slot=slot, blk=blk, bank=bank: e.tensor_copy(
                    v_s[slot][:, blk * 512:(blk + 1) * 512], pb[bank][:, :]), r=[bb[bank]], w=[b_vs[slot]])
            else:
                ACT(lambda e, slot=slot, blk=blk, bank=bank: e.activation(
                    sg_s[slot][:, blk * 512:(blk + 1) * 512], pb[bank][:, :], AF.Silu), r=[bb[bank]], w=[b_sgs[slot]])

        def st_T1S(n):
            c, hp = n // 2, n % 2
            s = n % 2
            tb = c // 4
            for j in range(2):
                h = hp * 2 + j
                PE(lambda e, c=c, j=j, h=h: e.transpose(
                    pbT[4][:, j * 128:(j + 1) * 128], kdT[:, h, c * 128:(c + 1) * 128], ident[:]),
                   r=[b_kdb[h][tb], b_ident], w=[bb[4]])
            for j in range(2):
                h = hp * 2 + j
                PE(lambda e, c=c, j=j, h=h: e.matmul(
                    pb[5][:, j * 128:(j + 1) * 128],
                    kbT[:, h, c * 128:(c + 1) * 128], qbT[:, h, c * 128:(c + 1) * 128], start=True, stop=True),
                   r=[b_kbb[h][tb], b_qbb[h][tb]], w=[bb[5]])
            ACT(lambda e, s=s: e.copy(kd_s[s], pbT[4][:, 0:256].rearrange("p (j t) -> p j t", j=2)),
                r=[bb[4]], w=[b_kds[s]])
            DVE(lambda e, s=s: e.tensor_tensor(
                sT_s[s], pb[5][:, 0:256].rearrange("p (j t) -> p j t", j=2),
                lt[:].unsqueeze(1).broadcast_to([128, 2, 128]), ALU.mult),
                r=[bb[5], b_lt], w=[b_sTs[s]])

        def st_UO1(n):
            c, hp = n // 2, n % 2
            s = n % 2
            slot = c % NVS
            for j in range(2):
                h = hp * 2 + j
                PE(lambda e, s=s, j=j, h=h, slot=slot: e.matmul(
                    pb[6][:, j * 256:(j + 1) * 256], kd_s[s][:, j, :], v_s[slot][:, h * 256:(h + 1) * 256],
                    start=True, stop=True), r=[b_kds[s], b_vs[slot]], w=[bb[6]])
            for j in range(2):
                h = hp * 2 + j
                PE(lambda e, c=c, s=s, j=j, h=h, slot=slot: e.matmul(
                    pb[s][:, j * 256:(j + 1) * 256], sT_s[s][:, j, :], v_s[slot][:, h * 256:(h + 1) * 256],
                    start=(j == 0), stop=(c == 0), skip_group_check=True), r=[b_sTs[s], b_vs[slot]], w=[bb[s]])

        def st_O2(n):
            c, hp = n // 2, n % 2
            s = n % 2
            if c == 0:
                return
            tb = c // 4
            for j in range(2):
                h = hp * 2 + j
                PE(lambda e, c=c, s=s, j=j, h=h, hp=hp: e.matmul(
                    pb[s][:, j * 256:(j + 1) * 256], qbT[:, h, c * 128:(c + 1) * 128], Sbf_s[hp][c % 2][:, j, :],
                    start=False, stop=True, skip_group_check=True),
                   r=[b_qbb[h][tb], b_Sbf[hp][c % 2]], w=[bb[s]])

        def st_state(n):
            c, hp = n // 2, n % 2
            s = n % 2
            if c == NT - 1:
                return
            for j in range(2):
                h = hp * 2 + j
                if c == 0:
                    DVE(lambda e, s=s, j=j, hp=hp: e.tensor_copy(S_f[hp][:, j, :], pb[6][:, j * 256:(j + 1) * 256]),
                        r=[bb[6]], w=[b_S[hp]])
                else:
                    DVE(lambda e, c=c, s=s, j=j, h=h, hp=hp: e.scalar_tensor_tensor(
                        S_f[hp][:, j, :], S_f[hp][:, j, :], eblast[:, c, h:h + 1], pb[6][:, j * 256:(j + 1) * 256],
                        ALU.mult, ALU.add), r=[bb[6], b_S[hp], b_eblast[(c // 4) * 4 + h]], w=[b_S[hp]])
            ACT(lambda e, c=c, hp=hp: e.copy(Sbf_s[hp][(c + 1) % 2], S_f[hp]), r=[b_S[hp]], w=[b_Sbf[hp][(c + 1) % 2]])

        def st_sq(n):
            c, hp = n // 2, n % 2
            s = n % 2
            for j in range(2):
                ACT(lambda e, s=s, j=j: e.activation(junk, pb[s][:, j * 256:(j + 1) * 256], AF.Square,
                                                    accum_out=ss_s[s][:, j:j + 1]),
                    r=[bb[s]], w=[b_ss[s]])
            POOL(lambda e, s=s: e.tensor_tensor(rs_s[s], ss_s[s], cpow[:, 4:6], ALU.add),
                 r=[b_ss[s], b_cpow], w=[b_rs[s]])
            POOL(lambda e, s=s: e.tensor_tensor(rs_s[s], rs_s[s], cpow[:, 0:2], ALU.pow),
                 r=[b_cpow], w=[b_rs[s]])

        def st_y(n):
            c, hp = n // 2, n % 2
            s = n % 2
            slot = c % NVS
            for j in range(2):
                h = hp * 2 + j
                DVE(lambda e, s=s, j=j, h=h, slot=slot, y3=n % 3: e.scalar_tensor_tensor(
                    yb_s[y3][:, j, :], pb[s][:, j * 256:(j + 1) * 256], rs_s[s][:, j:j + 1],
                    sg_s[slot][:, h * 256:(h + 1) * 256], ALU.mult, ALU.mult),
                    r=[bb[s], b_rs[s], b_sgs[slot]], w=[b_ybs[n % 3]])

        def st_Y(n):
            c, hp = n // 2, n % 2
            s = n % 2
            for j in range(2):
                for i in range(2):
                    q = j * 2 + i
                    PE(lambda e, y3=n % 3, j=j, i=i, q=q: e.transpose(
                        pbT[7][:, q * 128:(q + 1) * 128],
                        yb_s[y3][:, j, i * 128:(i + 1) * 128], ident[:]),
                       r=[b_ybs[n % 3], b_ident], w=[bb[7]])
            ACT(lambda e, c=c, hp=hp: e.copy(
                y_aT[:, hp * 4:(hp + 1) * 4, c * 128:(c + 1) * 128],
                pbT[7][:, 0:512].rearrange("p (q t) -> p q t", q=4)),
                r=[bb[7]], w=[b_yaT[c]])

        maT = A4[:].rearrange("p (k t) -> p k t", k=KC)
        b_ma = [[Buf() for _ in range(4)] for _ in range(KC)]
        sgm = [sgm_t[:, s, :] for s in range(2)]
        tmpf = [A2[:, 2048 + s * 1024:2048 + (s + 1) * 1024].bitcast(F32) for s in range(2)]
        b_sgm = [Buf() for _ in range(2)]
        b_tmpf = [Buf() for _ in range(2)]
        sctr = [0]

        def gate_piece(tb, dc, gWt, wWt, actT, act_bufs, accumulate, banks=None):
            m = dc % 4
            s = sctr[0] % 2
            sctr[0] += 1
            gWa, gB = gWt
            wWa, wB = wWt
            bank = banks[0] if banks else next_bank()
            for k in range(KC):
                PE(lambda e, k=k, m=m, tb=tb, bank=bank, gWa=gWa: e.matmul(
                    pb[bank][:, :], gWa[:, k, m * 128:(m + 1) * 128],
                    xT[:, k, tb * 512:(tb + 1) * 512], start=(k == 0), stop=(k == KC - 1)),
                   r=[gB, b_xT], w=[bb[bank]])
            ACT(lambda e, s=s, bank=bank: e.activation(sgm[s], pb[bank][:, :], AF.Sigmoid),
                r=[bb[bank]], w=[b_sgm[s]])
            bank2 = banks[1] if banks else next_bank()
            for k in range(KC):
                PE(lambda e, k=k, m=m, tb=tb, bank2=bank2, wWa=wWa: e.matmul(
                    pb[bank2][:, :], wWa[:, k, m * 128:(m + 1) * 128],
                    actT[:, k, tb * 512:(tb + 1) * 512], start=(k == 0), stop=(k == KC - 1)),
                   r=[wB] + act_bufs, w=[bb[bank2]])
            if not accumulate:
                DVE(lambda e, s=s, dc=dc, tb=tb, bank2=bank2: e.tensor_tensor(
                    maT[:, dc, tb * 512:(tb + 1) * 512], pb[bank2][:, :], sgm[s], ALU.mult),
                    r=[bb[bank2], b_sgm[s]], w=[b_ma[dc][tb]])
            else:
                DVE(lambda e, s=s, bank2=bank2: e.tensor_tensor(tmpf[s], pb[bank2][:, :], sgm[s], ALU.mult),
                    r=[bb[bank2], b_sgm[s]], w=[b_tmpf[s]])
                DVE(lambda e, s=s, dc=dc, tb=tb: e.tensor_tensor(
                    maT[:, dc, tb * 512:(tb + 1) * 512], tmpf[s], maT[:, dc, tb * 512:(tb + 1) * 512], ALU.add),
                    r=[b_tmpf[s], b_ma[dc][tb]], w=[b_ma[dc][tb]])


        for kind in range(2):
            for blk in range(2):
                proj_vg(0, kind, blk)
        NS = 2 * NT
        st_T1S(0)
        for n in range(NS):
            c, hp = n // 2, n % 2
            if c + 1 < NT:
                proj_vg(c + 1, hp, 0)
            else:
                gate_piece(hp, 0, ga0, wa0, y_aT, b_yaT[hp * 4:(hp + 1) * 4], False, banks=(2, 3))
            drain_fin(1)
            st_UO1(n)
            if n + 1 < NS:
                st_T1S(n + 1)
            st_state(n)
            if n > 0:
                st_y(n - 1)
            if n > 1:
                st_Y(n - 2)
            if c + 1 < NT:
                proj_vg(c + 1, hp, 1)
            else:
                tbx, dcx = ((2, 0), (0, 1))[hp]
                gate_piece(tbx, dcx, ga0, wa0, y_aT, b_yaT[tbx * 4:(tbx + 1) * 4], False, banks=(2, 3))
            st_O2(n)
            st_sq(n)
            if n == 8:
                scale_w(wa0, cols16[:, 0:8])
        ga1 = ld(R[2], w_in, GA0 + 512)
        wa1 = ld(XC, w_a, 512)
        u0w = ld(XA, w_in, U0)
        u1w = ld(XB, w_in, U0 + 512)

        NB[0] = 8
        if _en('B'):

            gaw = [ga0, ga1]
            waw = [wa0, wa1]
            done = {(0, 0), (0, 1), (0, 2), (1, 0)}
            for dc, tb in ((1, 1), (1, 2)):
                gate_piece(tb, dc, gaw[dc // 4], waw[dc // 4], y_aT, b_yaT[tb * 4:(tb + 1) * 4], False, banks=(2, 3))
                done.add((dc, tb))
            st_y(NS - 1)
            st_Y(NS - 2)
            st_Y(NS - 1)
            cnt = 0
            for dc in range(KC):
                for tb in range(4):
                    if (dc, tb) in done:
                        continue
                    gate_piece(tb, dc, gaw[dc // 4], waw[dc // 4], y_aT, b_yaT[tb * 4:(tb + 1) * 4], False)
                    cnt += 1
                    if cnt == 4:
                        scale_w(wa1, cols16[:, 0:8])

        if _en('C'):

            u_t = A1[:].rearrange("p (t f) -> p t f", t=NT)
            pT = A2[:].rearrange("p (k t) -> p k t", k=KC)
            szT = A3[:].rearrange("p (k t) -> p k t", k=KC)
            b_u = [[Buf() for _ in range(NT)] for _ in range(2)]
            b_pT = [[Buf() for _ in range(4)] for _ in range(KC)]
            b_sz = [[Buf() for _ in range(4)] for _ in range(KC)]
            ectr = [0]
            z0w = ld(R[0], w_in, Z0)
            z1w = ld(R[1], w_in, Z0 + 512)
            gb0 = ld(R[2], w_in, GB0)
            uT = A1[:].rearrange("p (k t) -> p k t", k=KC)
            S_t = xt3[:].bitcast(BF16)
            b_uT = [Buf() for _ in range(KC)]
            b_pTc = [Buf() for _ in range(KC)]
            b_S = Buf()
            b_pfix = Buf()
            last_u_mm = [None]
            for blk in range(2):
                uW, uB = (u0w, u1w)[blk]
                for m in range(4):
                    cc = blk * 4 + m
                    for tb in range(4):
                        bank = next_bank()
                        for k in range(KC):
                            last_u_mm[0] = PE(lambda e, k=k, m=m, tb=tb, bank=bank, uW=uW: e.matmul(
                                pb[bank][:, :], uW[:, k, m * 128:(m + 1) * 128], xT[:, k, tb * 512:(tb + 1) * 512],
                                start=(k == 0), stop=(k == KC - 1)), r=[uB, b_xT], w=[bb[bank]])
                        ACT(lambda e, cc=cc, tb=tb, bank=bank: e.copy(
                            uT[:, cc, tb * 512:(tb + 1) * 512], pb[bank][:, :]), r=[bb[bank]], w=[b_uT[cc]])

            def pool_chunk(cc):
                w = 2 ** (cc // 2 + 1)
                uc = uT[:, cc, :]
                ex = [last_u_mm[0]] if cc >= 4 else []
                DVE(lambda e, uc=uc, w=w: e.tensor_tensor_scan(
                    S_t[:, 0:w], ones_t[:, 0:w], uc[:, 0:w], 0.0, ALU.mult, ALU.add),
                    r=[b_uT[cc], b_ones], w=[b_S])
                DVE(lambda e, uc=uc, w=w: e.tensor_tensor_scan(
                    S_t[:, w:T], uc[:, w:T], uc[:, 0:T - w], S_t[:, w - 1:w], ALU.add, ALU.subtract),
                    r=[b_uT[cc]], w=[b_S])
                DVE(lambda e, cc=cc, uc=uc, w=w: e.scalar_tensor_tensor(
                    pT[:, cc, :], S_t[:, :], 1.0 / w, uc, ALU.mult, ALU.subtract),
                    r=[b_S, b_uT[cc]], w=[b_pTc[cc]], extra=ex)
                DVE(lambda e, w=w: e.tensor_tensor(pfix[:, 0:w - 1], S_t[:, 0:w - 1], invc[:, 0:w - 1], ALU.mult),
                    r=[b_S, b_invc], w=[b_pfix])
                DVE(lambda e, cc=cc, uc=uc, w=w: e.tensor_tensor(
                    pT[:, cc, 0:w - 1], pfix[:, 0:w - 1], uc[:, 0:w - 1], ALU.subtract),
                    r=[b_pfix, b_uT[cc]], w=[b_pTc[cc]])

            for cc in range(KC):
                pool_chunk(cc)
            pool_done = [P.last["dve"]]
            wb0 = ld(X123[0], w_b, 0, extra=pool_done)
            wb1 = ld(X123[1], w_b, 512, extra=pool_done)
            wo0 = ld(X123[2], w_o, 0, extra=pool_done)
            for blk in range(2):
                zW, zB = (z0w, z1w)[blk]
                for m in range(4):
                    zc = blk * 4 + m
                    for tb in range(4):
                        bank = next_bank()
                        for k in range(KC):
                            PE(lambda e, k=k, m=m, tb=tb, bank=bank, zW=zW: e.matmul(
                                pb[bank][:, :], zW[:, k, m * 128:(m + 1) * 128], xT[:, k, tb * 512:(tb + 1) * 512],
                                start=(k == 0), stop=(k == KC - 1)), r=[zB, b_xT], w=[bb[bank]])
                        ACT(lambda e, zc=zc, tb=tb, bank=bank: e.activation(
                            szT[:, zc, tb * 512:(tb + 1) * 512], pb[bank][:, :], AF.Silu),
                            r=[bb[bank]], w=[b_sz[zc][tb]])
            gb1 = ld(R[0], w_in, GB0 + 512)
            wo1 = ld(R[1], w_o, 512)
            scale_w(wb0, colsT[:, 16:24])
            scale_w(wb1, colsT[:, 16:24])
            for tb in range(4):
                for dc in range(KC):
                    g = dc // 2
                    j = dc % 2
                    bank = next_bank()
                    for i in range(2):
                        PE(lambda e, g=g, j=j, i=i, tb=tb, bank=bank: e.matmul(
                            pb[bank][:, :], poolw[:, g * 2 + i, j * 128:(j + 1) * 128],
                            pT[:, g * 2 + i, tb * 512:(tb + 1) * 512], start=(i == 0), stop=(i == 1)),
                           r=[b_poolw, b_pTc[g * 2 + i]], w=[bb[bank]])
                    DVE(lambda e, dc=dc, tb=tb, bank=bank: e.scalar_tensor_tensor(
                        szT[:, dc, tb * 512:(tb + 1) * 512], pb[bank][:, :], colsT[:, 8 + dc:9 + dc],
                        szT[:, dc, tb * 512:(tb + 1) * 512], ALU.add, ALU.mult),
                        r=[bb[bank], b_cols, b_sz[dc][tb]], w=[b_sz[dc][tb]])

        if _en('E'):
            lasts = [P.last[e] for e in ("pe", "act", "dve")]

            gbw = [gb0, gb1]
            wbw = [wb0, wb1]
            wow = [wo0, wo1]
            xt = [A2[:, 4096 + s * 2048:4096 + (s + 1) * 2048].bitcast(F32) for s in range(2)]
            rr = [A2[:, 8192 + s * 2048:8192 + (s + 1) * 2048].bitcast(F32) for s in range(2)]
            yo = [A2[:, 12288 + s * 2048:12288 + (s + 1) * 2048].bitcast(F32) for s in range(2)]
            lnw_t = A1[:, 12288:14336].bitcast(F32)
            lnb_t = A1[:, 14336:16384].bitcast(F32)
            xt.append(xt3[:])
            b_xt = [Buf() for _ in range(3)]
            b_rr = [Buf() for _ in range(2)]
            b_yo = [Buf() for _ in range(2)]
            b_ln = Buf()
            stats = [small[:, 8 + s * 12:8 + (s + 1) * 12] for s in range(2)]
            mv = [small[:, 32 + s * 2:32 + (s + 1) * 2] for s in range(2)]
            rstd = [small[:, 36 + s:37 + s] for s in range(2)]
            nmr = [small[:, 38 + s:39 + s] for s in range(2)]
            b_st = [Buf() for _ in range(2)]
            b_mv = [Buf() for _ in range(2)]
            b_rn = [Buf() for _ in range(2)]
            sums = [small[:, 40 + s * 2:40 + (s + 1) * 2] for s in range(2)]
            b_sums = [Buf() for _ in range(2)]
            for i in range(4):
                rr.append(A3[:, i * 2048:(i + 1) * 2048].bitcast(F32))
                yo.append(A3[:, 8192 + i * 2048:8192 + (i + 1) * 2048].bitcast(F32))
                mv.append(small2[:, i * 2:(i + 1) * 2])
                rstd.append(small2[:, 8 + i:9 + i])
                nmr.append(small2[:, 12 + i:13 + i])
                sums.append(small2[:, 16 + i * 2:16 + (i + 1) * 2])
                for lst in (b_rr, b_yo, b_mv, b_rn, b_sums):
                    lst.append(Buf())

            def eslot(t):
                return t % 2 if t < NT - 4 else 2 + (t - (NT - 4))
            P.add("sp", lambda e: e.dma_start(out=lnw_t, in_=ln_w[0:1, :].broadcast_to([128, D])),
                  writes=[b_ln], lane="c_ln", extra=lasts)
            P.add("sp", lambda e: e.dma_start(out=lnb_t, in_=ln_b[0:1, :].broadcast_to([128, D])),
                  writes=[b_ln], lane="c_ln", extra=lasts)
            stores = []

            def d_piece(tb, dc):
                blk = dc // 4
                m = dc % 4
                s = sctr[0] % 2
                sctr[0] += 1
                gW, gB = gbw[blk]
                wW, wB = wbw[blk]
                bank = next_bank()
                for k in range(KC):
                    PE(lambda e, k=k, m=m, tb=tb, bank=bank, gW=gW: e.matmul(
                        pb[bank][:, :], gW[:, k, m * 128:(m + 1) * 128],
                        xT[:, k, tb * 512:(tb + 1) * 512], start=(k == 0), stop=(k == KC - 1)),
                       r=[gB, b_xT], w=[bb[bank]])
                ACT(lambda e, s=s, bank=bank: e.activation(sgm[s], pb[bank][:, :], AF.Sigmoid),
                    r=[bb[bank]], w=[b_sgm[s]])
                bank2 = next_bank()
                for k in range(KC):
                    PE(lambda e, k=k, m=m, tb=tb, bank2=bank2, wW=wW: e.matmul(
                        pb[bank2][:, :], wW[:, k, m * 128:(m + 1) * 128],
                        szT[:, k, tb * 512:(tb + 1) * 512], start=(k == 0), stop=(k == KC - 1)),
                       r=[wB] + [b_sz[k][tb] for k in range(KC)], w=[bb[bank2]])
                DVE(lambda e, s=s, bank2=bank2: e.tensor_tensor(tmpf[s], pb[bank2][:, :], sgm[s], ALU.mult),
                    r=[bb[bank2], b_sgm[s]], w=[b_tmpf[s]])
                DVE(lambda e, s=s, dc=dc, tb=tb: e.tensor_tensor(
                    maT[:, dc, tb * 512:(tb + 1) * 512], tmpf[s], maT[:, dc, tb * 512:(tb + 1) * 512], ALU.add),
                    r=[b_tmpf[s], b_ma[dc][tb]], w=[b_ma[dc][tb]])

            def ld_x(t):
                s = t % 3
                P.add("sp", lambda e, t=t, s=s: e.dma_start(out=xt[s], in_=x[t * 128:(t + 1) * 128, :]),
                      writes=[b_xt[s]], lane=f"xt{s}", extra=lasts)

            def e_tile(t):
                s = eslot(t)
                x3 = t % 3
                if t + 2 < NT:
                    ld_x(t + 2)
                for half in range(2):
                    bank = next_bank()
                    oW, oB = wow[half]
                    for k in range(KC):
                        PE(lambda e, k=k, t=t, bank=bank, oW=oW: e.matmul(
                            pb[bank][:, :], maT[:, k, t * 128:(t + 1) * 128], oW[:, k, :],
                            start=(k == 0), stop=(k == KC - 1)),
                           r=[oB] + [b_ma[k][t // 4] for k in range(KC)], w=[bb[bank]])
                    DVE(lambda e, s=s, x3=x3, half=half, bank=bank: e.scalar_tensor_tensor(
                        rr[s][:, half * 512:(half + 1) * 512], xt[x3][:, half * 512:(half + 1) * 512], ALPHA,
                        pb[bank][:, :], ALU.mult, ALU.add), r=[bb[bank], b_xt[x3]], w=[b_rr[s]])
                ACT(lambda e, s=s: e.activation(junk2[:], rr[s], AF.Identity, accum_out=sums[s][:, 0:1]),
                    r=[b_rr[s]], w=[b_sums[s]])
                ACT(lambda e, s=s: e.activation(junk2[:], rr[s], AF.Square, accum_out=sums[s][:, 1:2]),
                    r=[b_rr[s]], w=[b_sums[s]])
                if t == NT - 1:
                    e_norm(t - 1)
                    e_fin(t - 2)
                    DVE(lambda e, s=s: e.tensor_tensor(mv[s], sums[s], cpow[:, 10:12], ALU.mult),
                        r=[b_sums[s], b_cpow], w=[b_mv[s]])
                    DVE(lambda e, s=s: e.scalar_tensor_tensor(rstd[s], mv[s][:, 0:1], mv[s][:, 0:1], mv[s][:, 1:2],
                                                             ALU.mult, ALU.subtract),
                        r=[b_mv[s]], w=[b_rn[s]])
                    ACT(lambda e, s=s: e.activation(rstd[s], rstd[s], AF.Sqrt, bias=EPS, scale=-1.0),
                        r=[b_rn[s]], w=[b_rn[s]])
                    DVE(lambda e, s=s: e.reciprocal(rstd[s], rstd[s]), r=[b_rn[s]], w=[b_rn[s]])
                    DVE(lambda e, s=s: e.tensor_tensor(nmr[s], mv[s][:, 0:1], rstd[s], ALU.mult),
                        r=[b_mv[s], b_rn[s]], w=[b_rn[s]])
                    return
                POOL(lambda e, s=s: e.tensor_tensor(mv[s], sums[s], cpow[:, 8:10], ALU.mult),
                     r=[b_sums[s], b_cpow], w=[b_mv[s]])
                POOL(lambda e, s=s: e.tensor_tensor(rstd[s], mv[s][:, 0:1], mv[s][:, 0:1], ALU.mult),
                     r=[b_mv[s]], w=[b_rn[s]])
                POOL(lambda e, s=s: e.tensor_tensor(rstd[s], mv[s][:, 1:2], rstd[s], ALU.subtract),
                     r=[b_mv[s]], w=[b_rn[s]])
                POOL(lambda e, s=s: e.tensor_tensor(rstd[s], rstd[s], cpow[:, 6:7], ALU.add),
                     r=[b_cpow], w=[b_rn[s]])
                POOL(lambda e, s=s: e.tensor_tensor(rstd[s], rstd[s], cpow[:, 0:1], ALU.pow),
                     r=[b_cpow], w=[b_rn[s]])
                POOL(lambda e, s=s: e.tensor_tensor(nmr[s], mv[s][:, 0:1], rstd[s], ALU.mult),
                     r=[b_mv[s]], w=[b_rn[s]])

            def e_norm(t):
                s = eslot(t)
                ACT(lambda e, s=s: e.activation(yo[s], rr[s], AF.Identity, bias=nmr[s], scale=rstd[s]),
                    r=[b_rr[s], b_rn[s]], w=[b_yo[s]])

            def e_fin(t):
                s = eslot(t)
                DVE(lambda e, s=s: e.tensor_tensor(yo[s], yo[s], lnw_t, ALU.mult), r=[b_yo[s], b_ln], w=[b_yo[s]])
                if True:
                    DVE(lambda e, s=s: e.tensor_tensor(yo[s], yo[s], lnb_t, ALU.add), r=[b_yo[s], b_ln], w=[b_yo[s]])
                else:
                    POOL(lambda e, s=s: e.tensor_tensor(yo[s], yo[s], lnb_t, ALU.add), r=[b_yo[s], b_ln], w=[b_yo[s]])
                stores.append(P.add("sp", lambda e, t=t, s=s: e.dma_start(out=out[t * 128:(t + 1) * 128, :], in_=yo[s]),
                                    reads=[b_yo[s]], lane=f"out{s}"))

            ld_x(0)
            ld_x(1)
            for dc in range(KC):
                d_piece(0, dc)
            for tb in range(4):
                for i in range(4):
                    t = 4 * tb + i
                    if tb + 1 < 4:
                        d_piece(tb + 1, 2 * i)
                        if i == 3:
                            d_piece(tb + 1, 2 * i + 1)
                    if t == NT - 4:
                        ACT(lambda e: e.activation(small[:, 56:57], cpow[:, 6:7], AF.Sqrt), r=[b_cpow])
                    e_tile(t)
                    if t > 0 and t != NT - 1:
                        e_norm(t - 1)
                    if t > 1 and t != NT - 1:
                        e_fin(t - 2)
                    if tb + 1 < 4 and i != 3:
                        d_piece(tb + 1, 2 * i + 1)
            e_norm(NT - 1)
            e_fin(NT - 2)
            e_fin(NT - 1)

        if dbg:
            lasts = P.barrier()
        for name, (ap, shape, dt, bufs) in dbg_specs(locals(), dbg).items():
            dten = nc.dram_tensor("dbg_" + name, shape, dt, kind="ExternalOutput").ap()
            stores.append(P.add("sp", lambda e, dten=dten, ap=ap: e.dma_start(out=dten, in_=ap),
                                reads=bufs, lane="dbg_" + name, extra=lasts))
            dbg_out[name] = "dbg_" + name

        P.add("sp", lambda e: e.nop(), extra=stores)

        lanes = P.finalize()
        sems = {ln: es.enter_context(nc.semaphore("s_" + ln)) for ln in lanes}
        with nc.Block() as block:
            @block.tensor
            def _(e):
                P.replay("pe", e, sems)

            @block.scalar
            def _(e):
                P.replay("act", e, sems)

            @block.vector
            def _(e):
                P.replay("dve", e, sems)

            @block.gpsimd
            def _(e):
                P.replay("pool", e, sems)

            @block.sync
            def _(e):
                P.replay("sp", e, sems)
    return nc, dbg_out


def dbg_specs(loc, dbg):
    specs = {}
    for name in dbg:
        if name == "xT":
            specs[name] = (loc["xT"][:], [128, KC, T], BF16, [loc["b_xT"]])
        elif name == "y_aT":
            specs[name] = (loc["A3"][:], [128, 16384], BF16, [])
        elif name == "maT":
            specs[name] = (loc["A4"][:], [128, 16384], BF16, [])
        elif name == "A1":
            specs[name] = (loc["A1"][:], [128, 16384], BF16, [])
        elif name == "A2":
            specs[name] = (loc["A2"][:], [128, 16384], BF16, [])
        elif name == "colsT":
            specs[name] = (loc["colsT"][:], [128, 24], F32, [])
        elif name == "eblast":
            specs[name] = (loc["eblast"][:], [128, NT, 4], F32, [])
    return specs


_CACHE = {}


def _get_program():
    if "nc" not in _CACHE:
        _CACHE["nc"] = build_program()[0]
    return _CACHE["nc"]


def make_in_maps(inputs):
    c = _consts()
    f = lambda a: np.ascontiguousarray(np.asarray(a, dtype=np.float32))
    shared = {
        "w_in": f(inputs["w_in"][0]),
        "w_gate_up": f(inputs["w_gate_up"][0]),
        "b_gate": f(inputs["b_gate"]).reshape(1, 512),
        "gn_w": f(inputs["gn_w"]).reshape(8, 128),
        "pool_w": f(inputs["pool_w"][0]),
        "pool_b": f(inputs["pool_b"]).reshape(8, 128),
        "pool_scale": f(inputs["pool_scale"]).reshape(8, 128),
        "w_a": f(inputs["w_a"][0]),
        "w_b": f(inputs["w_b"][0]),
        "w_o": f(inputs["w_o"][0]),
        "ln_w": f(inputs["ln_w"]).reshape(1, D),
        "ln_b": f(inputs["ln_b"]).reshape(1, D),
    }
    shared.update(c)
    xs = f(inputs["x"])
    return [dict(shared, x=xs[b]) for b in range(N_CORES)]


def kernel(**inputs):
    nc = _get_program()
    in_maps = make_in_maps(inputs)
    res = run_bass_kernel_spmd(nc, in_maps, core_ids=list(range(N_CORES)))
    return np.stack([res.results[b]["out"] for b in range(N_CORES)], axis=0).astype(np.float32)
```

```python
import numpy as np
import ml_dtypes
from contextlib import ExitStack
import concourse.bass as bass
import concourse.mybir as mybir
from concourse.bass_utils import run_bass_kernel_spmd

F32 = mybir.dt.float32
BF16 = mybir.dt.bfloat16
AF = mybir.ActivationFunctionType
ALU = mybir.AluOpType

T = 2048
NT = 16
D = 1024
KC = 8
DIN = 7184
Q0, K0, V0, G0, AL0, U0, Z0, GA0, GB0 = 0, 512, 1024, 2048, 3072, 3088, 4112, 5136, 6160
DK = 128
EPS = 1e-5
ALPHA = 2.0 ** 0.25
NW = 3
N_CORES = 8

ENGS = ("pe", "act", "dve", "pool", "sp")


class Op:
    __slots__ = ("eng", "fn", "deps", "sig", "lane", "ticket", "is_dma")

    def __init__(self, eng, fn, deps, lane, is_dma):
        self.eng = eng
        self.fn = fn
        self.deps = deps
        self.sig = False
        self.lane = lane
        self.ticket = None
        self.is_dma = is_dma


class Buf:
    __slots__ = ("w", "r", "excl")

    def __init__(self, excl=False):
        self.w = None
        self.r = []
        self.excl = excl


class Prog:
    def __init__(self):
        self.ops = {e: [] for e in ENGS}
        self.last = {e: None for e in ENGS}

    def add(self, eng, fn, reads=(), writes=(), lane=None, extra=()):
        is_dma = lane is not None
        deps = []
        writes = list(writes) + [b for b in reads if b.excl]
        reads = [b for b in reads if not b.excl]
        for b in reads:
            if b.w is not None:
                deps.append(b.w)
        for b in writes:
            if b.w is not None:
                deps.append(b.w)
            deps.extend(b.r)
        deps.extend([d for d in extra if d is not None])
        op = Op(eng, fn, deps, lane if is_dma else eng, is_dma)
        for b in reads:
            b.r.append(op)
        for b in writes:
            b.w = op
            b.r = []
        self.ops[eng].append(op)
        if not is_dma:
            self.last[eng] = op
        return op

    def barrier(self, engs=("pe", "act", "dve")):
        lasts = [self.last[e] for e in engs if self.last[e] is not None]
        saved = dict(self.last)
        for e in engs:
            self.add(e, lambda eng: None, extra=[l for l in lasts if l.eng != e])
        self.last = saved
        return lasts

    def finalize(self):
        for e in ENGS:
            for op in self.ops[e]:
                keep = []
                for d in op.deps:
                    if d is op:
                        continue
                    if (not d.is_dma) and d.eng == op.eng and op.eng in ("pe", "sp"):
                        continue
                    keep.append(d)
                op.deps = keep
                for d in keep:
                    d.sig = True
        cnt = {}
        for e in ENGS:
            for op in self.ops[e]:
                if op.is_dma:
                    op.sig = True
                if op.sig:
                    inc = 16 if op.is_dma else 1
                    cnt[op.lane] = cnt.get(op.lane, 0) + inc
                    op.ticket = cnt[op.lane]
        return sorted(cnt.keys())

    def replay(self, eng_name, eng, sems):
        waited = {}
        for op in self.ops[eng_name]:
            need = {}
            for d in op.deps:
                if need.get(d.lane, 0) < d.ticket:
                    need[d.lane] = d.ticket
            for lane, val in need.items():
                if waited.get(lane, 0) < val:
                    eng.wait_ge(sems[lane], val)
                    waited[lane] = val
            ins = op.fn(eng)
            if op.sig:
                assert ins is not None
                ins.then_inc(sems[op.lane], 16 if op.is_dma else 1)


def _pool_mats():
    pm = np.zeros((128, 12, 128), np.float32)
    for g in range(4):
        w = 2 ** (g + 1)
        for t in range(128):
            for s in range(t - w + 1, t + 1):
                if s >= 0:
                    pm[s, g * 3 + 0, t] += 1.0 / w
                else:
                    pm[s + 128, g * 3 + 1, t] += 1.0 / w
            pm[t, g * 3 + 0, t] -= 1.0
            cnt = min(t + 1, w)
            for s in range(max(0, t - w + 1), t + 1):
                pm[s, g * 3 + 2, t] += 1.0 / cnt
            pm[t, g * 3 + 2, t] -= 1.0
    return pm.astype(ml_dtypes.bfloat16)


def _consts():
    ident = np.eye(128, dtype=np.float32)
    lt = np.triu(np.ones((128, 128), np.float32))
    return {
        "ident_bf": ident.astype(ml_dtypes.bfloat16),
        "ident_f": ident,
        "lt_bf": lt.astype(ml_dtypes.bfloat16),
        "pmat": _pool_mats(),
        "invc": np.tile((1.0 / np.arange(1, 17, dtype=np.float32))[None, :], (128, 1)).astype(np.float32),
    }


def build_program(dbg=(), stop_after='E'):
    nc = bass.Bass("TRN2", target_bir_lowering=False)
    _order = ['A', 'B', 'C', 'D', 'E']

    def _en(n):
        return _order.index(n) <= _order.index(stop_after)

    lasts = []
    stores = []

    def din(name, shape, dt=F32):
        return nc.dram_tensor(name, shape, dt, kind="ExternalInput").ap()

    x = din("x", [T, D])
    w_in = din("w_in", [D, DIN])
    w_gate_up = din("w_gate_up", [16, 512])
    b_gate = din("b_gate", [1, 512])
    gn_w = din("gn_w", [8, 128])
    pool_w = din("pool_w", [4, 256, 256])
    pool_b = din("pool_b", [8, 128])
    pool_scale = din("pool_scale", [8, 128])
    w_a = din("w_a", [D, D])
    w_b = din("w_b", [D, D])
    w_o = din("w_o", [D, D])
    ln_w = din("ln_w", [1, D])
    ln_b = din("ln_b", [1, D])
    ident_bf_d = din("ident_bf", [128, 128], BF16)
    ident_f_d = din("ident_f", [128, 128], F32)
    lt_bf_d = din("lt_bf", [128, 128], BF16)
    pmat_d = din("pmat", [128, 12, 128], BF16)
    invc_d = din("invc", [128, 16], F32)
    out = nc.dram_tensor("out", [T, D], F32, kind="ExternalOutput").ap()
    dbg_out = {}

    P = Prog()

    def PE(fn, r=(), w=(), extra=()):
        return P.add("pe", fn, r, w, extra=extra)

    def ACT(fn, r=(), w=(), extra=()):
        return P.add("act", fn, r, w, extra=extra)

    def DVE(fn, r=(), w=(), extra=()):
        return P.add("dve", fn, r, w, extra=extra)

    def POOL(fn, r=(), w=(), extra=()):
        return P.add("pool", fn, r, w, extra=extra)

    with ExitStack() as es:
        def sb(name, shape, dt):
            return es.enter_context(nc.sbuf_tensor(name, shape, dt))

        def ps(name, shape, dt):
            return es.enter_context(nc.psum_tensor(name, shape, dt))

        xT = sb("xT", [128, KC, T], BF16)
        b_xT = Buf()
        wslot = [sb(f"ws{i}", [128, KC, 512], BF16) for i in range(NW)]
        wbuf = [Buf() for _ in range(NW)]
        A1 = sb("A1", [128, 16384], BF16)
        A2 = sb("A2", [128, 16384], BF16)
        A3 = sb("A3", [128, 16384], BF16)
        A4 = sb("A4", [128, 16384], BF16)
        ident = sb("ident", [128, 128], BF16)
        lt = sb("lt", [128, 128], BF16)
        pmat = sb("pmat_s", [128, 12, 128], BF16)
        wg_aug = sb("wg_aug", [32, 512], BF16)
        wal = sb("wal", [128, KC, 16], BF16)
        poolw = sb("poolw", [128, 8, 256], BF16)
        colsT = sb("colsT", [128, 24], F32)
        cols16 = sb("cols16", [128, 8], F32)
        blast = sb("blast", [128, NT, 4], F32)
        eblast = sb("eblast", [128, NT, 4], F32)
        small = sb("small", [128, 64], F32)
        cpow = sb("cpow", [128, 16], F32)
        ones_t = sb("ones_t", [128, 128], F32)
        invc = sb("invc_s", [128, 16], F32)
        pfix = sb("pfix", [128, 16], F32)
        b_invc = Buf()
        b_ones = Buf()
        junk2 = sb("junk2", [128, D], BF16)
        small2 = sb("small2", [128, 32], F32)
        xt3 = sb("xt3", [128, D], F32)
        sgm_t = sb("sgm_t", [128, 2, 512], F32)
        b_cpow = Buf()
        b_ident, b_lt, b_pmat, b_wg, b_wal, b_poolw, b_cols, b_cols16 = [Buf() for _ in range(8)]

        pb = [ps(f"pb{i}", [128, 512], F32) for i in range(8)]
        pbT = [p[:].bitcast(BF16) for p in pb]
        bb = [Buf(excl=True) for _ in range(8)]
        bank_ctr = [0]

        NB = [4]

        def next_bank(n=None, base=0):
            n = NB[0] if n is None else n
            i = base + bank_ctr[0] % n
            bank_ctr[0] += 1
            return i

        wctr = [0]

        def load_w(src, c0):
            i = wctr[0] % NW
            wctr[0] += 1
            srcv = src[:, c0:c0 + 512].rearrange("(kc p) n -> p kc n", p=128)
            P.add("pool", lambda e, i=i, srcv=srcv: e.dma_start(out=wslot[i][:], in_=srcv),
                  writes=[wbuf[i]], lane=f"w{i}")
            return i

        def mkslot(ap, lane, buf=None):
            return (ap, buf if buf is not None else Buf(), lane)

        R = [mkslot(wslot[i][:], f"w{i}", wbuf[i]) for i in range(NW)]
        XC = mkslot(A2[:, 4096:8192].rearrange("p (k n) -> p k n", k=KC), "wxC")
        XA = mkslot(A2[:, 8192:12288].rearrange("p (k n) -> p k n", k=KC), "wxA")
        XB = mkslot(A2[:, 12288:16384].rearrange("p (k n) -> p k n", k=KC), "wxB")
        X123 = [mkslot(A1[:, i * 4096:(i + 1) * 4096].rearrange("p (k n) -> p k n", k=KC), f"wx{i}")
                for i in range(3)]

        def ld(slot, src, c0, extra=()):
            ap, buf, lane = slot
            srcv = src[:, c0:c0 + 512].rearrange("(kc p) n -> p kc n", p=128)
            P.add("pool", lambda e, ap=ap, srcv=srcv: e.dma_start(out=ap, in_=srcv),
                  writes=[buf], lane=lane, extra=extra)
            return (ap, buf)

        def scale_w(wt, cols):
            W, B = wt
            DVE(lambda e, W=W, cols=cols: e.tensor_tensor(
                W, W, cols.unsqueeze(2).broadcast_to([128, KC, 512]), ALU.mult),
                r=[B, b_cols, b_cols16], w=[B])

        P.add("sp", lambda e: e.dma_start(out=ident[:], in_=ident_bf_d[:, :]), writes=[b_ident], lane="c_id")
        P.add("sp", lambda e: e.dma_start(out=lt[:], in_=lt_bf_d[:, :]), writes=[b_lt], lane="c_lt")
        P.add("sp", lambda e: e.dma_start(out=invc[:], in_=invc_d[:, :]), writes=[b_invc], lane="c_pm")


        NXB = 8
        xb = [A2[:, s * 1024:(s + 1) * 1024] for s in range(NXB)]
        b_xb = [Buf() for _ in range(NXB)]
        b_xTb = [Buf() for _ in range(4)]
        p0_last = [None] * NT
        def x_load(t):
            P.add("pool", lambda e, t=t: e.dma_start(out=xb[t % NXB], in_=x[t * 128:(t + 1) * 128, :]),
                  writes=[b_xb[t % NXB]], lane=f"xb{t % NXB}")

        b_wg1 = Buf()
        for t in range(4):
            x_load(t)
        P.add("pool", lambda e: e.dma_start(
            out=wal[:], in_=w_in[:, AL0:AL0 + 16].rearrange("(kc p) n -> p kc n", p=128)),
            writes=[b_wal], lane="c_wal")
        P.add("pool", lambda e: e.dma_start(out=wg_aug[0:16, :], in_=w_gate_up[:, :]), writes=[b_wg], lane="c_wg")
        P.add("pool", lambda e: e.dma_start(out=wg_aug[16:17, :], in_=b_gate[:, :]), writes=[b_wg1], lane="c_wg1")
        wq = ld(R[0], w_in, Q0)
        for t in range(4, 8):
            x_load(t)
        wk = ld(R[1], w_in, K0)
        vW = [None, None]
        gW = [None, None]
        P.add("pool", lambda e: e.dma_start(
            out=poolw[:], in_=pool_w.rearrange("g (cc p) d -> p (g cc) d", p=128)),
            writes=[b_poolw], lane="c_pw")
        POOL(lambda e: e.memset(cpow[:, 0:4], -0.5), w=[b_cpow])
        POOL(lambda e: e.memset(cpow[:, 4:6], 256.0 * EPS), w=[b_cpow])
        POOL(lambda e: e.memset(cpow[:, 6:7], EPS), w=[b_cpow])
        POOL(lambda e: e.memset(cpow[:, 7:8], -1.0), w=[b_cpow])
        POOL(lambda e: e.memset(cpow[:, 8:10], 1.0 / D), w=[b_cpow])
        POOL(lambda e: e.memset(cpow[:, 10:11], -1.0 / D), w=[b_cpow])
        POOL(lambda e: e.memset(cpow[:, 11:12], 1.0 / D), w=[b_cpow])
        POOL(lambda e: e.memset(cpow[:, 12:16], 0.0), w=[b_cpow])

        def p0_tile(t):
            s = t % NXB
            bank = next_bank()
            for k in range(KC):
                p0_last[t] = PE(lambda e, k=k, s=s, bank=bank: e.transpose(
                    pbT[bank][:, k * 128:(k + 1) * 128], xb[s][:, k * 128:(k + 1) * 128], ident[:]),
                   r=[b_xb[s], b_ident], w=[bb[bank]])
            src = pbT[bank].rearrange("p (k t) -> p k t", k=KC)
            dst = xT[:, :, t * 128:(t + 1) * 128]
            if t % 2 == 0:
                DVE(lambda e, dst=dst, src=src: e.tensor_copy(dst, src), r=[bb[bank]], w=[b_xT, b_xTb[t // 4]])
            else:
                ACT(lambda e, dst=dst, src=src: e.copy(dst, src), r=[bb[bank]], w=[b_xT, b_xTb[t // 4]])

        for t in range(4):
            p0_tile(t)

        identf = A3[:, 4096:4096 + 256].bitcast(F32)
        rows = A3[:, 4352:4352 + 256].bitcast(F32)
        b_identf, b_rows = Buf(), Buf()
        P.add("sp", lambda e: e.dma_start(out=identf, in_=ident_f_d[:, :]), writes=[b_identf], lane="c_if")
        b_rows1, b_rows2 = Buf(), Buf()
        P.add("sp", lambda e: e.dma_start(out=rows[0:8, :], in_=gn_w[:, :]), writes=[b_rows], lane="c_rows")
        P.add("sp", lambda e: e.dma_start(out=rows[8:16, :], in_=pool_b[:, :]), writes=[b_rows1], lane="c_rows1")
        P.add("sp", lambda e: e.dma_start(out=rows[16:24, :], in_=pool_scale[:, :]), writes=[b_rows2], lane="c_rows2")
        PE(lambda e: e.transpose(pb[2][:, 0:24], rows[0:24, :], identf[0:24, 0:24]),
           r=[b_rows, b_rows1, b_rows2, b_identf], w=[bb[2]])
        DVE(lambda e: e.tensor_copy(colsT[:], pb[2][:, 0:24]), r=[bb[2]], w=[b_cols])
        DVE(lambda e: e.tensor_scalar(cols16[:], colsT[:, 0:8], 16.0, None, ALU.mult), r=[b_cols], w=[b_cols16])

        qbT = A1[:, 0:8192].rearrange("p (h t) -> p h t", h=4)
        kbT = A1[:, 8192:16384].rearrange("p (h t) -> p h t", h=4)
        EpT = A3[:, 0:8192].rearrange("p (h t) -> p h t", h=4)
        EmT = A3[:, 8192:16384].rearrange("p (h t) -> p h t", h=4)
        y_aT = A3[:].rearrange("p (k t) -> p k t", k=KC)
        kdT = A4[:, 0:8192].rearrange("p (h t) -> p h t", h=4)
        alT = A4[0:32, 8192:10240]
        la_hi = [A4[:, 10240 + s * 512:10240 + (s + 1) * 512] for s in range(2)]
        la_lo = [A4[:, 11264 + s * 512:11264 + (s + 1) * 512] for s in range(2)]
        etmp = [A4[:, 12288 + s * 1024:12288 + (s + 1) * 1024].bitcast(F32) for s in range(2)]
        b_alT, b_Ep, b_Em, b_qb, b_kb, b_kd = [Buf() for _ in range(6)]
        b_la = [Buf() for _ in range(2)]
        b_et = [Buf() for _ in range(2)]
        b_EpT = [Buf() for _ in range(NT)]
        b_EmT = [Buf() for _ in range(NT)]

        b_alTb = [Buf() for _ in range(4)]
        DVE(lambda e: e.memset(alT[0:32, :], 1.0), w=b_alTb)
        DVE(lambda e: e.memset(ones_t[:, :], 1.0), w=[b_ones])

        def a0_blk(tb):
            bank = next_bank()
            for k in range(KC):
                PE(lambda e, k=k, tb=tb, bank=bank: e.matmul(
                    pb[bank][0:16, :], wal[:, k, :], xT[:, k, tb * 512:(tb + 1) * 512],
                    start=(k == 0), stop=(k == KC - 1)), r=[b_wal, b_xTb[tb]], w=[bb[bank]])
            DVE(lambda e, tb=tb, bank=bank: e.tensor_copy(alT[0:16, tb * 512:(tb + 1) * 512], pb[bank][0:16, :]),
                r=[bb[bank]], w=[b_alTb[tb]])

        a0_blk(0)

        cs = [A4[:, 10240 + s * 1024:10240 + (s + 1) * 1024].bitcast(F32) for s in range(2)]

        def a1_pre(u):
            s = u % 2
            tb, h = u // 4, u % 4
            bpre = 4 + s
            PE(lambda e, tb=tb, h=h, bpre=bpre: e.matmul(
                pb[bpre][:, :], wg_aug[0:17, h * 128:(h + 1) * 128], alT[0:17, tb * 512:(tb + 1) * 512],
                start=True, stop=True), r=[b_alTb[tb], b_wg, b_wg1], w=[bb[bpre]])
            ACT(lambda e, s=s, bpre=bpre: e.activation(etmp[s], pb[bpre][:, :], AF.Exp, scale=-1.0),
                r=[bb[bpre]], w=[b_et[s]])
            ACT(lambda e, s=s: e.activation(etmp[s], etmp[s], AF.Ln, bias=1.0), r=[b_et[s]], w=[b_et[s]])

        def a1_hilo(u):
            s = u % 2
            for c in range(4):
                DVE(lambda e, s=s, c=c: e.tensor_tensor_scan(
                    cs[s][:, c * 128:(c + 1) * 128], ones_t[:, :], etmp[s][:, c * 128:(c + 1) * 128], 0.0,
                    ALU.mult, ALU.add), r=[b_et[s], b_ones], w=[b_la[s]])

        def a1_cum(u):
            s = u % 2
            tb, h = u // 4, u % 4
            sl = slice(tb * 512, (tb + 1) * 512)
            ACT(lambda e, s=s, h=h, sl=sl: e.activation(EpT[:, h, sl], cs[s], AF.Exp, scale=-1.0 / 16.0),
                r=[b_la[s]], w=[b_EpT[u]])
            ACT(lambda e, s=s, h=h, sl=sl: e.activation(EmT[:, h, sl], cs[s], AF.Exp, scale=1.0 / 16.0),
                r=[b_la[s]], w=[b_EmT[u]])
            ACT(lambda e, s=s, h=h, tb=tb: e.activation(
                eblast[:, tb * 4:(tb + 1) * 4, h], cs[s].rearrange("p (c t) -> p c t", c=4)[:, :, 127],
                AF.Exp, scale=-1.0 / 16.0), r=[b_la[s]], w=[b_eblast[u]])

        b_blast = [Buf() for _ in range(NT)]
        b_eblast = [Buf() for _ in range(NT)]
        b_qbb = [[Buf() for _ in range(4)] for _ in range(4)]
        b_kbb = [[Buf() for _ in range(4)] for _ in range(4)]
        b_kdb = [[Buf() for _ in range(4)] for _ in range(4)]

        def proj_qk(kind, h, tb):
            W, B = wq if kind == 0 else wk
            dstT = qbT if kind == 0 else kbT
            dbuf = b_qbb if kind == 0 else b_kbb
            bank = next_bank()
            for k in range(KC):
                PE(lambda e, k=k, h=h, tb=tb, bank=bank, W=W: e.matmul(
                    pb[bank][:, :], W[:, k, h * 128:(h + 1) * 128], xT[:, k, tb * 512:(tb + 1) * 512],
                    start=(k == 0), stop=(k == KC - 1)), r=[B, b_xTb[tb]], w=[bb[bank]])
            if kind == 0:
                DVE(lambda e, h=h, tb=tb, bank=bank: e.tensor_scalar(
                    qbT[:, h, tb * 512:(tb + 1) * 512], pb[bank][:, :], DK ** -0.5, None, ALU.mult),
                    r=[bb[bank]], w=[dbuf[h][tb]])
            else:
                ACT(lambda e, h=h, tb=tb, bank=bank: e.copy(
                    kbT[:, h, tb * 512:(tb + 1) * 512], pb[bank][:, :]), r=[bb[bank]], w=[dbuf[h][tb]])

        fin_pending = []

        def finalize_one(tb, h):
            sl = slice(tb * 512, (tb + 1) * 512)
            DVE(lambda e, h=h, sl=sl: e.tensor_tensor(qbT[:, h, sl], qbT[:, h, sl], EpT[:, h, sl], ALU.mult),
                r=b_EpT[tb * 4:(tb + 1) * 4], w=[b_qbb[h][tb]])
            DVE(lambda e, h=h, sl=sl: e.tensor_tensor(kbT[:, h, sl], kbT[:, h, sl], EmT[:, h, sl], ALU.mult),
                r=b_EmT[tb * 4:(tb + 1) * 4], w=[b_kbb[h][tb]])
            POOL(lambda e, h=h, sl=sl, tb=tb: e.tensor_tensor(
                kdT[:, h, sl].rearrange("p (c t) -> p c t", c=4),
                kbT[:, h, sl].rearrange("p (c t) -> p c t", c=4),
                eblast[:, tb * 4:(tb + 1) * 4, h].unsqueeze(2).broadcast_to([128, 4, 128]), ALU.mult),
                r=[b_kbb[h][tb]] + b_eblast[tb * 4:(tb + 1) * 4], w=[b_kdb[h][tb]])

        def finalize_qk(tb):
            for h in range(4):
                fin_pending.append((tb, h))

        def drain_fin(k=1):
            for _ in range(k):
                if fin_pending:
                    finalize_one(*fin_pending.pop(0))

        fill = [(lambda kind=kind, h=h, tb=tb: proj_qk(kind, h, tb))
                for tb in range(4) for kind in range(2) for h in range(4)]
        for t in range(NT):
            if t + 4 < NT:
                p0_tile(t + 4)
            if t + 8 < NT:
                x_load(t + 8)
            if t == 7:
                vW[0] = ld(R[2], w_in, V0)
                gW[0] = ld(XA, w_in, G0)
                gW[1] = ld(XB, w_in, G0 + 512)
            if t == 11:
                vW[1] = ld(XC, w_in, V0 + 512, extra=[p0_last[15]])
            a1_pre(t)
            fill.pop(0)()
            a1_hilo(t)
            if t > 0:
                a1_cum(t - 1)
                if t % 4 == 0:
                    finalize_qk(t // 4 - 1)
            drain_fin(1)
            fill.pop(0)()
            if t + 4 < NT and (t + 4) % 4 == 3:
                a0_blk((t + 4) // 4)
        a1_cum(NT - 1)
        finalize_qk(3)
        assert not fill


        NVS = 2
        v_s = [A2[:, s * 1024:(s + 1) * 1024] for s in range(NVS)]
        sg_s = [A2[:, 2048 + s * 1024:2048 + (s + 1) * 1024] for s in range(NVS)]
        b_vs = [Buf() for _ in range(NVS)]
        b_sgs = [Buf() for _ in range(NVS)]

        TB = 8192
        kd_s = [A4[:, TB + s * 256:TB + (s + 1) * 256].rearrange("p (j t) -> p j t", j=2) for s in range(2)]
        sT_s = [A4[:, TB + 512 + s * 256:TB + 512 + (s + 1) * 256].rearrange("p (j t) -> p j t", j=2) for s in range(2)]
        yb_s = [A4[:, TB + 1024 + s * 512:TB + 1024 + (s + 1) * 512].rearrange("p (j f) -> p j f", j=2) for s in range(2)]
        yb_s.append(A4[:, TB + 6656:TB + 7168].rearrange("p (j f) -> p j f", j=2))
        Sbf_s = [[A4[:, TB + 2048 + (hp * 2 + s) * 512:TB + 2048 + (hp * 2 + s + 1) * 512].rearrange(
            "p (j f) -> p j f", j=2) for s in range(2)] for hp in range(2)]
        S_f = [A4[:, TB + 4096 + hp * 1024:TB + 4096 + (hp + 1) * 1024].bitcast(F32).rearrange(
            "p (j f) -> p j f", j=2) for hp in range(2)]
        junk = A4[:, TB + 6144:TB + 6656].bitcast(F32)
        ss_s = [small[:, s * 2:(s + 1) * 2] for s in range(2)]
        rs_s = [small[:, 4 + s * 2:4 + (s + 1) * 2] for s in range(2)]
        b_kds = [Buf() for _ in range(2)]
        b_sTs = [Buf() for _ in range(2)]
        b_ybs = [Buf() for _ in range(3)]
        b_Sbf = [[Buf() for _ in range(2)] for _ in range(2)]
        b_S = [Buf() for _ in range(2)]
        b_ss = [Buf() for _ in range(2)]
        b_rs = [Buf() for _ in range(2)]
        b_yaT = [Buf() for _ in range(NT)]

        ga0 = ld(R[0], w_in, GA0)
        wa0 = ld(R[1], w_a, 0)

        def proj_vg(c, kind, blk):
            slot = c % NVS
            W, B = (vW if kind == 0 else gW)[blk]
            bank = next_bank(2, 2)
            for k in range(KC):
                PE(lambda e, k=k, c=c, bank=bank, W=W: e.matmul(
                    pb[bank][:, :], xT[:, k, c * 128:(c + 1) * 128], W[:, k, :],
                    start=(k == 0), stop=(k == KC - 1)), r=[B, b_xT], w=[bb[bank]])
            if kind == 0:
                DVE(lambda e, slot=slot, blk=blk, bank=bank: e.tensor_copy(
                    v_s[slot][:, blk * 512:(blk + 1) * 512], pb[bank][:, :]), r=[bb[bank]], w=[b_vs[slot]])
            else:
                ACT(lambda e, slot=slot, blk=blk, bank=bank: e.activation(
                    sg_s[slot][:, blk * 512:(blk + 1) * 512], pb[bank][:, :], AF.Silu), r=[bb[bank]], w=[b_sgs[slot]])

        def st_T1S(n):
            c, hp = n // 2, n % 2
            s = n % 2
            tb = c // 4
            for j in range(2):
                h = hp * 2 + j
                PE(lambda e, c=c, j=j, h=h: e.transpose(
                    pbT[4][:, j * 128:(j + 1) * 128], kdT[:, h, c * 128:(c + 1) * 128], ident[:]),
                   r=[b_kdb[h][tb], b_ident], w=[bb[4]])
            for j in range(2):
                h = hp * 2 + j
                PE(lambda e, c=c, j=j, h=h: e.matmul(
                    pb[5][:, j * 128:(j + 1) * 128],
                    kbT[:, h, c * 128:(c + 1) * 128], qbT[:, h, c * 128:(c + 1) * 128], start=True, stop=True),
                   r=[b_kbb[h][tb], b_qbb[h][tb]], w=[bb[5]])
            ACT(lambda e, s=s: e.copy(kd_s[s], pbT[4][:, 0:256].rearrange("p (j t) -> p j t", j=2)),
                r=[bb[4]], w=[b_kds[s]])
            DVE(lambda e, s=s: e.tensor_tensor(
                sT_s[s], pb[5][:, 0:256].rearrange("p (j t) -> p j t", j=2),
                lt[:].unsqueeze(1).broadcast_to([128, 2, 128]), ALU.mult),
                r=[bb[5], b_lt], w=[b_sTs[s]])

        def st_UO1(n):
            c, hp = n // 2, n % 2
            s = n % 2
            slot = c % NVS
            for j in range(2):
                h = hp * 2 + j
                PE(lambda e, s=s, j=j, h=h, slot=slot: e.matmul(
                    pb[6][:, j * 256:(j + 1) * 256], kd_s[s][:, j, :], v_s[slot][:, h * 256:(h + 1) * 256],
                    start=True, stop=True), r=[b_kds[s], b_vs[slot]], w=[bb[6]])
            for j in range(2):
                h = hp * 2 + j
                PE(lambda e, c=c, s=s, j=j, h=h, slot=slot: e.matmul(
                    pb[s][:, j * 256:(j + 1) * 256], sT_s[s][:, j, :], v_s[slot][:, h * 256:(h + 1) * 256],
                    start=(j == 0), stop=(c == 0), skip_group_check=True), r=[b_sTs[s], b_vs[slot]], w=[bb[s]])

        def st_O2(n):
            c, hp = n // 2, n % 2
            s = n % 2
            if c == 0:
                return
            tb = c // 4
            for j in range(2):
                h = hp * 2 + j
                PE(lambda e, c=c, s=s, j=j, h=h, hp=hp: e.matmul(
                    pb[s][:, j * 256:(j + 1) * 256], qbT[:, h, c * 128:(c + 1) * 128], Sbf_s[hp][c % 2][:, j, :],
                    start=False, stop=True, skip_group_check=True),
                   r=[b_qbb[h][tb], b_Sbf[hp][c % 2]], w=[bb[s]])

        def st_state(n):
            c, hp = n // 2, n % 2
            s = n % 2
            if c == NT - 1:
                return
            for j in range(2):
                h = hp * 2 + j
                if c == 0:
                    DVE(lambda e, s=s, j=j, hp=hp: e.tensor_copy(S_f[hp][:, j, :], pb[6][:, j * 256:(j + 1) * 256]),
                        r=[bb[6]], w=[b_S[hp]])
                else:
                    DVE(lambda e, c=c, s=s, j=j, h=h, hp=hp: e.scalar_tensor_tensor(
                        S_f[hp][:, j, :], S_f[hp][:, j, :], eblast[:, c, h:h + 1], pb[6][:, j * 256:(j + 1) * 256],
                        ALU.mult, ALU.add), r=[bb[6], b_S[hp], b_eblast[(c // 4) * 4 + h]], w=[b_S[hp]])
            ACT(lambda e, c=c, hp=hp: e.copy(Sbf_s[hp][(c + 1) % 2], S_f[hp]), r=[b_S[hp]], w=[b_Sbf[hp][(c + 1) % 2]])

        def st_sq(n):
            c, hp = n // 2, n % 2
            s = n % 2
            for j in range(2):
                ACT(lambda e, s=s, j=j: e.activation(junk, pb[s][:, j * 256:(j + 1) * 256], AF.Square,
                                                    accum_out=ss_s[s][:, j:j + 1]),
                    r=[bb[s]], w=[b_ss[s]])
            POOL(lambda e, s=s: e.tensor_tensor(rs_s[s], ss_s[s], cpow[:, 4:6], ALU.add),
                 r=[b_ss[s], b_cpow], w=[b_rs[s]])
            POOL(lambda e, s=s: e.tensor_tensor(rs_s[s], rs_s[s], cpow[:, 0:2], ALU.pow),
                 r=[b_cpow], w=[b_rs[s]])

        def st_y(n):
            c, hp = n // 2, n % 2
            s = n % 2
            slot = c % NVS
            for j in range(2):
                h = hp * 2 + j
                DVE(lambda e, s=s, j=j, h=h, slot=slot, y3=n % 3: e.scalar_tensor_tensor(
                    yb_s[y3][:, j, :], pb[s][:, j * 256:(j + 1) * 256], rs_s[s][:, j:j + 1],
                    sg_s[slot][:, h * 256:(h + 1) * 256], ALU.mult, ALU.mult),
                    r=[bb[s], b_rs[s], b_sgs[slot]], w=[b_ybs[n % 3]])

        def st_Y(n):
            c, hp = n // 2, n % 2
            s = n % 2
            for j in range(2):
                for i in range(2):
                    q = j * 2 + i
                    PE(lambda e, y3=n % 3, j=j, i=i, q=q: e.transpose(
                        pbT[7][:, q * 128:(q + 1) * 128],
                        yb_s[y3][:, j, i * 128:(i + 1) * 128], ident[:]),
                       r=[b_ybs[n % 3], b_ident], w=[bb[7]])
            ACT(lambda e, c=c, hp=hp: e.copy(
                y_aT[:, hp * 4:(hp + 1) * 4, c * 128:(c + 1) * 128],
                pbT[7][:, 0:512].rearrange("p (q t) -> p q t", q=4)),
                r=[bb[7]], w=[b_yaT[c]])

        maT = A4[:].rearrange("p (k t) -> p k t", k=KC)
        b_ma = [[Buf() for _ in range(4)] for _ in range(KC)]
        sgm = [sgm_t[:, s, :] for s in range(2)]
        tmpf = [A2[:, 2048 + s * 1024:2048 + (s + 1) * 1024].bitcast(F32) for s in range(2)]
        b_sgm = [Buf() for _ in range(2)]
        b_tmpf = [Buf() for _ in range(2)]
        sctr = [0]

        def gate_piece(tb, dc, gWt, wWt, actT, act_bufs, accumulate, banks=None):
            m = dc % 4
            s = sctr[0] % 2
            sctr[0] += 1
            gWa, gB = gWt
            wWa, wB = wWt
            bank = banks[0] if banks else next_bank()
            for k in range(KC):
                PE(lambda e, k=k, m=m, tb=tb, bank=bank, gWa=gWa: e.matmul(
                    pb[bank][:, :], gWa[:, k, m * 128:(m + 1) * 128],
                    xT[:, k, tb * 512:(tb + 1) * 512], start=(k == 0), stop=(k == KC - 1)),
                   r=[gB, b_xT], w=[bb[bank]])
            ACT(lambda e, s=s, bank=bank: e.activation(sgm[s], pb[bank][:, :], AF.Sigmoid),
                r=[bb[bank]], w=[b_sgm[s]])
            bank2 = banks[1] if banks else next_bank()
            for k in range(KC):
                PE(lambda e, k=k, m=m, tb=tb, bank2=bank2, wWa=wWa: e.matmul(
                    pb[bank2][:, :], wWa[:, k, m * 128:(m + 1) * 128],
                    actT[:, k, tb * 512:(tb + 1) * 512], start=(k == 0), stop=(k == KC - 1)),
                   r=[wB] + act_bufs, w=[bb[bank2]])
            if not accumulate:
                DVE(lambda e, s=s, dc=dc, tb=tb, bank2=bank2: e.tensor_tensor(
                    maT[:, dc, tb * 512:(tb + 1) * 512], pb[bank2][:, :], sgm[s], ALU.mult),
                    r=[bb[bank2], b_sgm[s]], w=[b_ma[dc][tb]])
            else:
                DVE(lambda e, s=s, bank2=bank2: e.tensor_tensor(tmpf[s], pb[bank2][:, :], sgm[s], ALU.mult),
                    r=[bb[bank2], b_sgm[s]], w=[b_tmpf[s]])
                DVE(lambda e, s=s, dc=dc, tb=tb: e.tensor_tensor(
                    maT[:, dc, tb * 512:(tb + 1) * 512], tmpf[s], maT[:, dc, tb * 512:(tb + 1) * 512], ALU.add),
                    r=[b_tmpf[s], b_ma[dc][tb]], w=[b_ma[dc][tb]])


        for kind in range(2):
            for blk in range(2):
                proj_vg(0, kind, blk)
        NS = 2 * NT
        st_T1S(0)
        for n in range(NS):
            c, hp = n // 2, n % 2
            if c + 1 < NT:
                proj_vg(c + 1, hp, 0)
            else:
                gate_piece(hp, 0, ga0, wa0, y_aT, b_yaT[hp * 4:(hp + 1) * 4], False, banks=(2, 3))
            drain_fin(1)
            st_UO1(n)
            if n + 1 < NS:
                st_T1S(n + 1)
            st_state(n)
            if n > 0:
                st_y(n - 1)
            if n > 1:
                st_Y(n - 2)
            if c + 1 < NT:
                proj_vg(c + 1, hp, 1)
            else:
                tbx, dcx = ((2, 0), (0, 1))[hp]
                gate_piece(tbx, dcx, ga0, wa0, y_aT, b_yaT[tbx * 4:(tbx + 1) * 4], False, banks=(2, 3))
            st_O2(n)
            st_sq(n)
            if n == 8:
                scale_w(wa0, cols16[:, 0:8])
        ga1 = ld(R[2], w_in, GA0 + 512)
        wa1 = ld(XC, w_a, 512)
        u0w = ld(XA, w_in, U0)
        u1w = ld(XB, w_in, U0 + 512)

        NB[0] = 8
        if _en('B'):

            gaw = [ga0, ga1]
            waw = [wa0, wa1]
            done = {(0, 0), (0, 1), (0, 2), (1, 0)}
            for dc, tb in ((1, 1), (1, 2)):
                gate_piece(tb, dc, gaw[dc // 4], waw[dc // 4], y_aT, b_yaT[tb * 4:(tb + 1) * 4], False, banks=(2, 3))
                done.add((dc, tb))
            st_y(NS - 1)
            st_Y(NS - 2)
            st_Y(NS - 1)
            cnt = 0
            for dc in range(KC):
                for tb in range(4):
                    if (dc, tb) in done:
                        continue
                    gate_piece(tb, dc, gaw[dc // 4], waw[dc // 4], y_aT, b_yaT[tb * 4:(tb + 1) * 4], False)
                    cnt += 1
                    if cnt == 4:
                        scale_w(wa1, cols16[:, 0:8])

        if _en('C'):

            u_t = A1[:].rearrange("p (t f) -> p t f", t=NT)
            pT = A2[:].rearrange("p (k t) -> p k t", k=KC)
            szT = A3[:].rearrange("p (k t) -> p k t", k=KC)
            b_u = [[Buf() for _ in range(NT)] for _ in range(2)]
            b_pT = [[Buf() for _ in range(4)] for _ in range(KC)]
            b_sz = [[Buf() for _ in range(4)] for _ in range(KC)]
            ectr = [0]
            z0w = ld(R[0], w_in, Z0)
            z1w = ld(R[1], w_in, Z0 + 512)
            gb0 = ld(R[2], w_in, GB0)
            uT = A1[:].rearrange("p (k t) -> p k t", k=KC)
            S_t = xt3[:].bitcast(BF16)
            b_uT = [Buf() for _ in range(KC)]
            b_pTc = [Buf() for _ in range(KC)]
            b_S = Buf()
            b_pfix = Buf()
            last_u_mm = [None]
            for blk in range(2):
                uW, uB = (u0w, u1w)[blk]
                for m in range(4):
                    cc = blk * 4 + m
                    for tb in range(4):
                        bank = next_bank()
                        for k in range(KC):
                            last_u_mm[0] = PE(lambda e, k=k, m=m, tb=tb, bank=bank, uW=uW: e.matmul(
                                pb[bank][:, :], uW[:, k, m * 128:(m + 1) * 128], xT[:, k, tb * 512:(tb + 1) * 512],
                                start=(k == 0), stop=(k == KC - 1)), r=[uB, b_xT], w=[bb[bank]])
                        ACT(lambda e, cc=cc, tb=tb, bank=bank: e.copy(
                            uT[:, cc, tb * 512:(tb + 1) * 512], pb[bank][:, :]), r=[bb[bank]], w=[b_uT[cc]])

            def pool_chunk(cc):
                w = 2 ** (cc // 2 + 1)
                uc = uT[:, cc, :]
                ex = [last_u_mm[0]] if cc >= 4 else []
                DVE(lambda e, uc=uc, w=w: e.tensor_tensor_scan(
                    S_t[:, 0:w], ones_t[:, 0:w], uc[:, 0:w], 0.0, ALU.mult, ALU.add),
                    r=[b_uT[cc], b_ones], w=[b_S])
                DVE(lambda e, uc=uc, w=w: e.tensor_tensor_scan(
                    S_t[:, w:T], uc[:, w:T], uc[:, 0:T - w], S_t[:, w - 1:w], ALU.add, ALU.subtract),
                    r=[b_uT[cc]], w=[b_S])
                DVE(lambda e, cc=cc, uc=uc, w=w: e.scalar_tensor_tensor(
                    pT[:, cc, :], S_t[:, :], 1.0 / w, uc, ALU.mult, ALU.subtract),
                    r=[b_S, b_uT[cc]], w=[b_pTc[cc]], extra=ex)
                DVE(lambda e, w=w: e.tensor_tensor(pfix[:, 0:w - 1], S_t[:, 0:w - 1], invc[:, 0:w - 1], ALU.mult),
                    r=[b_S, b_invc], w=[b_pfix])
                DVE(lambda e, cc=cc, uc=uc, w=w: e.tensor_tensor(
                    pT[:, cc, 0:w - 1], pfix[:, 0:w - 1], uc[:, 0:w - 1], ALU.subtract),
                    r=[b_pfix, b_uT[cc]], w=[b_pTc[cc]])

            for cc in range(KC):
                pool_chunk(cc)
            pool_done = [P.last["dve"]]
            wb0 = ld(X123[0], w_b, 0, extra=pool_done)
            wb1 = ld(X123[1], w_b, 512, extra=pool_done)
            wo0 = ld(X123[2], w_o, 0, extra=pool_done)
            for blk in range(2):
                zW, zB = (z0w, z1w)[blk]
                for m in range(4):
                    zc = blk * 4 + m
                    for tb in range(4):
                        bank = next_bank()
                        for k in range(KC):
                            PE(lambda e, k=k, m=m, tb=tb, bank=bank, zW=zW: e.matmul(
                                pb[bank][:, :], zW[:, k, m * 128:(m + 1) * 128], xT[:, k, tb * 512:(tb + 1) * 512],
                                start=(k == 0), stop=(k == KC - 1)), r=[zB, b_xT], w=[bb[bank]])
                        ACT(lambda e, zc=zc, tb=tb, bank=bank: e.activation(
                            szT[:, zc, tb * 512:(tb + 1) * 512], pb[bank][:, :], AF.Silu),
                            r=[bb[bank]], w=[b_sz[zc][tb]])
            gb1 = ld(R[0], w_in, GB0 + 512)
            wo1 = ld(R[1], w_o, 512)
            scale_w(wb0, colsT[:, 16:24])
            scale_w(wb1, colsT[:, 16:24])
            for tb in range(4):
                for dc in range(KC):
                    g = dc // 2
                    j = dc % 2
                    bank = next_bank()
                    for i in range(2):
                        PE(lambda e, g=g, j=j, i=i, tb=tb, bank=bank: e.matmul(
                            pb[bank][:, :], poolw[:, g * 2 + i, j * 128:(j + 1) * 128],
                            pT[:, g * 2 + i, tb * 512:(tb + 1) * 512], start=(i == 0), stop=(i == 1)),
                           r=[b_poolw, b_pTc[g * 2 + i]], w=[bb[bank]])
                    DVE(lambda e, dc=dc, tb=tb, bank=bank: e.scalar_tensor_tensor(
                        szT[:, dc, tb * 512:(tb + 1) * 512], pb[bank][:, :], colsT[:, 8 + dc:9 + dc],
                        szT[:, dc, tb * 512:(tb + 1) * 512], ALU.add, ALU.mult),
                        r=[bb[bank], b_cols, b_sz[dc][tb]], w=[b_sz[dc][tb]])

        if _en('E'):
            lasts = [P.last[e] for e in ("pe", "act", "dve")]

            gbw = [gb0, gb1]
            wbw = [wb0, wb1]
            wow = [wo0, wo1]
            xt = [A2[:, 4096 + s * 2048:4096 + (s + 1) * 2048].bitcast(F32) for s in range(2)]
            rr = [A2[:, 8192 + s * 2048:8192 + (s + 1) * 2048].bitcast(F32) for s in range(2)]
            yo = [A2[:, 12288 + s * 2048:12288 + (s + 1) * 2048].bitcast(F32) for s in range(2)]
            lnw_t = A1[:, 12288:14336].bitcast(F32)
            lnb_t = A1[:, 14336:16384].bitcast(F32)
            xt.append(xt3[:])
            b_xt = [Buf() for _ in range(3)]
            b_rr = [Buf() for _ in range(2)]
            b_yo = [Buf() for _ in range(2)]
            b_ln = Buf()
            stats = [small[:, 8 + s * 12:8 + (s + 1) * 12] for s in range(2)]
            mv = [small[:, 32 + s * 2:32 + (s + 1) * 2] for s in range(2)]
            rstd = [small[:, 36 + s:37 + s] for s in range(2)]
            nmr = [small[:, 38 + s:39 + s] for s in range(2)]
            b_st = [Buf() for _ in range(2)]
            b_mv = [Buf() for _ in range(2)]
            b_rn = [Buf() for _ in range(2)]
            sums = [small[:, 40 + s * 2:40 + (s + 1) * 2] for s in range(2)]
            b_sums = [Buf() for _ in range(2)]
            for i in range(4):
                rr.append(A3[:, i * 2048:(i + 1) * 2048].bitcast(F32))
                yo.append(A3[:, 8192 + i * 2048:8192 + (i + 1) * 2048].bitcast(F32))
                mv.append(small2[:, i * 2:(i + 1) * 2])
                rstd.append(small2[:, 8 + i:9 + i])
                nmr.append(small2[:, 12 + i:13 + i])
                sums.append(small2[:, 16 + i * 2:16 + (i + 1) * 2])
                for lst in (b_rr, b_yo, b_mv, b_rn, b_sums):
                    lst.append(Buf())

            def eslot(t):
                return t % 2 if t < NT - 4 else 2 + (t - (NT - 4))
            P.add("sp", lambda e: e.dma_start(out=lnw_t, in_=ln_w[0:1, :].broadcast_to([128, D])),
                  writes=[b_ln], lane="c_ln", extra=lasts)
            P.add("sp", lambda e: e.dma_start(out=lnb_t, in_=ln_b[0:1, :].broadcast_to([128, D])),
                  writes=[b_ln], lane="c_ln", extra=lasts)
            stores = []

            def d_piece(tb, dc):
                blk = dc // 4
                m = dc % 4
                s = sctr[0] % 2
                sctr[0] += 1
                gW, gB = gbw[blk]
                wW, wB = wbw[blk]
                bank = next_bank()
                for k in range(KC):
                    PE(lambda e, k=k, m=m, tb=tb, bank=bank, gW=gW: e.matmul(
                        pb[bank][:, :], gW[:, k, m * 128:(m + 1) * 128],
                        xT[:, k, tb * 512:(tb + 1) * 512], start=(k == 0), stop=(k == KC - 1)),
                       r=[gB, b_xT], w=[bb[bank]])
                ACT(lambda e, s=s, bank=bank: e.activation(sgm[s], pb[bank][:, :], AF.Sigmoid),
                    r=[bb[bank]], w=[b_sgm[s]])
                bank2 = next_bank()
                for k in range(KC):
                    PE(lambda e, k=k, m=m, tb=tb, bank2=bank2, wW=wW: e.matmul(
                        pb[bank2][:, :], wW[:, k, m * 128:(m + 1) * 128],
                        szT[:, k, tb * 512:(tb + 1) * 512], start=(k == 0), stop=(k == KC - 1)),
                       r=[wB] + [b_sz[k][tb] for k in range(KC)], w=[bb[bank2]])
                DVE(lambda e, s=s, bank2=bank2: e.tensor_tensor(tmpf[s], pb[bank2][:, :], sgm[s], ALU.mult),
                    r=[bb[bank2], b_sgm[s]], w=[b_tmpf[s]])
                DVE(lambda e, s=s, dc=dc, tb=tb: e.tensor_tensor(
                    maT[:, dc, tb * 512:(tb + 1) * 512], tmpf[s], maT[:, dc, tb * 512:(tb + 1) * 512], ALU.add),
                    r=[b_tmpf[s], b_ma[dc][tb]], w=[b_ma[dc][tb]])

            def ld_x(t):
                s = t % 3
                P.add("sp", lambda e, t=t, s=s: e.dma_start(out=xt[s], in_=x[t * 128:(t + 1) * 128, :]),
                      writes=[b_xt[s]], lane=f"xt{s}", extra=lasts)

            def e_tile(t):
                s = eslot(t)
                x3 = t % 3
                if t + 2 < NT:
                    ld_x(t + 2)
                for half in range(2):
                    bank = next_bank()
                    oW, oB = wow[half]
                    for k in range(KC):
                        PE(lambda e, k=k, t=t, bank=bank, oW=oW: e.matmul(
                            pb[bank][:, :], maT[:, k, t * 128:(t + 1) * 128], oW[:, k, :],
                            start=(k == 0), stop=(k == KC - 1)),
                           r=[oB] + [b_ma[k][t // 4] for k in range(KC)], w=[bb[bank]])
                    DVE(lambda e, s=s, x3=x3, half=half, bank=bank: e.scalar_tensor_tensor(
                        rr[s][:, half * 512:(half + 1) * 512], xt[x3][:, half * 512:(half + 1) * 512], ALPHA,
                        pb[bank][:, :], ALU.mult, ALU.add), r=[bb[bank], b_xt[x3]], w=[b_rr[s]])
                ACT(lambda e, s=s: e.activation(junk2[:], rr[s], AF.Identity, accum_out=sums[s][:, 0:1]),
                    r=[b_rr[s]], w=[b_sums[s]])
                ACT(lambda e, s=s: e.activation(junk2[:], rr[s], AF.Square, accum_out=sums[s][:, 1:2]),
                    r=[b_rr[s]], w=[b_sums[s]])
                if t == NT - 1:
                    e_norm(t - 1)
                    e_fin(t - 2)
                    DVE(lambda e, s=s: e.tensor_tensor(mv[s], sums[s], cpow[:, 10:12], ALU.mult),
                        r=[b_sums[s], b_cpow], w=[b_mv[s]])
                    DVE(lambda e, s=s: e.scalar_tensor_tensor(rstd[s], mv[s][:, 0:1], mv[s][:, 0:1], mv[s][:, 1:2],
                                                             ALU.mult, ALU.subtract),
                        r=[b_mv[s]], w=[b_rn[s]])
                    ACT(lambda e, s=s: e.activation(rstd[s], rstd[s], AF.Sqrt, bias=EPS, scale=-1.0),
                        r=[b_rn[s]], w=[b_rn[s]])
                    DVE(lambda e, s=s: e.reciprocal(rstd[s], rstd[s]), r=[b_rn[s]], w=[b_rn[s]])
                    DVE(lambda e, s=s: e.tensor_tensor(nmr[s], mv[s][:, 0:1], rstd[s], ALU.mult),
                        r=[b_mv[s], b_rn[s]], w=[b_rn[s]])
                    return
                POOL(lambda e, s=s: e.tensor_tensor(mv[s], sums[s], cpow[:, 8:10], ALU.mult),
                     r=[b_sums[s], b_cpow], w=[b_mv[s]])
                POOL(lambda e, s=s: e.tensor_tensor(rstd[s], mv[s][:, 0:1], mv[s][:, 0:1], ALU.mult),
                     r=[b_mv[s]], w=[b_rn[s]])
                POOL(lambda e, s=s: e.tensor_tensor(rstd[s], mv[s][:, 1:2], rstd[s], ALU.subtract),
                     r=[b_mv[s]], w=[b_rn[s]])
                POOL(lambda e, s=s: e.tensor_tensor(rstd[s], rstd[s], cpow[:, 6:7], ALU.add),
                     r=[b_cpow], w=[b_rn[s]])
                POOL(lambda e, s=s: e.tensor_tensor(rstd[s], rstd[s], cpow[:, 0:1], ALU.pow),
                     r=[b_cpow], w=[b_rn[s]])
                POOL(lambda e, s=s: e.tensor_tensor(nmr[s], mv[s][:, 0:1], rstd[s], ALU.mult),
                     r=[b_mv[s]], w=[b_rn[s]])
                POOL(lambda e, s=s: e.tensor_tensor(nmr[s], nmr[s], cpow[:, 7:8], ALU.mult),
                     r=[b_cpow], w=[b_rn[s]])

            def e_norm(t):
                s = eslot(t)
                ACT(lambda e, s=s: e.activation(yo[s], rr[s], AF.Identity, bias=nmr[s], scale=rstd[s]),
                    r=[b_rr[s], b_rn[s]], w=[b_yo[s]])

            def e_fin(t):
                s = eslot(t)
                if t == NT - 1:
                    for half in range(2):
                        cs_ = slice(half * 512, (half + 1) * 512)
                        bh = Buf()
                        DVE(lambda e, s=s, cs_=cs_: e.tensor_tensor(yo[s][:, cs_], yo[s][:, cs_], lnw_t[:, cs_], ALU.mult),
                            r=[b_yo[s], b_ln], w=[bh])
                        DVE(lambda e, s=s, cs_=cs_: e.tensor_tensor(yo[s][:, cs_], yo[s][:, cs_], lnb_t[:, cs_], ALU.add),
                            r=[bh, b_ln], w=[bh])
                        stores.append(P.add("sp", lambda e, t=t, s=s, cs_=cs_: e.dma_start(
                            out=out[t * 128:(t + 1) * 128, cs_], in_=yo[s][:, cs_]), reads=[bh], lane=f"outh{half}"))
                    return
                DVE(lambda e, s=s: e.tensor_tensor(yo[s], yo[s], lnw_t, ALU.mult), r=[b_yo[s], b_ln], w=[b_yo[s]])
                if True:
                    DVE(lambda e, s=s: e.tensor_tensor(yo[s], yo[s], lnb_t, ALU.add), r=[b_yo[s], b_ln], w=[b_yo[s]])
                else:
                    POOL(lambda e, s=s: e.tensor_tensor(yo[s], yo[s], lnb_t, ALU.add), r=[b_yo[s], b_ln], w=[b_yo[s]])
                stores.append(P.add("sp", lambda e, t=t, s=s: e.dma_start(out=out[t * 128:(t + 1) * 128, :], in_=yo[s]),
                                    reads=[b_yo[s]], lane=f"out{s}"))

            ld_x(0)
            ld_x(1)
            for dc in range(KC):
                d_piece(0, dc)
            for tb in range(4):
                for i in range(4):
                    t = 4 * tb + i
                    if tb + 1 < 4:
                        d_piece(tb + 1, 2 * i)
                        if i == 3:
                            d_piece(tb + 1, 2 * i + 1)
                    if t == NT - 4:
                        ACT(lambda e: e.activation(small[:, 56:57], cpow[:, 6:7], AF.Sqrt), r=[b_cpow])
                    e_tile(t)
                    if t > 0 and t != NT - 1:
                        e_norm(t - 1)
                    if t > 1 and t != NT - 1:
                        e_fin(t - 2)
                    if tb + 1 < 4 and i != 3:
                        d_piece(tb + 1, 2 * i + 1)
            e_norm(NT - 1)
            e_fin(NT - 2)
            e_fin(NT - 1)

        if dbg:
            lasts = P.barrier()
        for name, (ap, shape, dt, bufs) in dbg_specs(locals(), dbg).items():
            dten = nc.dram_tensor("dbg_" + name, shape, dt, kind="ExternalOutput").ap()
            stores.append(P.add("sp", lambda e, dten=dten, ap=ap: e.dma_start(out=dten, in_=ap),
                                reads=bufs, lane="dbg_" + name, extra=lasts))
            dbg_out[name] = "dbg_" + name

        P.add("sp", lambda e: e.nop(), extra=stores)

        lanes = P.finalize()
        sems = {ln: es.enter_context(nc.semaphore("s_" + ln)) for ln in lanes}
        with nc.Block() as block:
            @block.tensor
            def _(e):
                P.replay("pe", e, sems)

            @block.scalar
            def _(e):
                P.replay("act", e, sems)

            @block.vector
            def _(e):
                P.replay("dve", e, sems)

            @block.gpsimd
            def _(e):
                P.replay("pool", e, sems)

            @block.sync
            def _(e):
                P.replay("sp", e, sems)
    return nc, dbg_out


def dbg_specs(loc, dbg):
    specs = {}
    for name in dbg:
        if name == "xT":
            specs[name] = (loc["xT"][:], [128, KC, T], BF16, [loc["b_xT"]])
        elif name == "y_aT":
            specs[name] = (loc["A3"][:], [128, 16384], BF16, [])
        elif name == "maT":
            specs[name] = (loc["A4"][:], [128, 16384], BF16, [])
        elif name == "A1":
            specs[name] = (loc["A1"][:], [128, 16384], BF16, [])
        elif name == "A2":
            specs[name] = (loc["A2"][:], [128, 16384], BF16, [])
        elif name == "colsT":
            specs[name] = (loc["colsT"][:], [128, 24], F32, [])
        elif name == "eblast":
            specs[name] = (loc["eblast"][:], [128, NT, 4], F32, [])
    return specs


_CACHE = {}


def _get_program():
    if "nc" not in _CACHE:
        _CACHE["nc"] = build_program()[0]
    return _CACHE["nc"]


def make_in_maps(inputs):
    c = _consts()
    f = lambda a: np.ascontiguousarray(np.asarray(a, dtype=np.float32))
    shared = {
        "w_in": f(inputs["w_in"][0]),
        "w_gate_up": f(inputs["w_gate_up"][0]),
        "b_gate": f(inputs["b_gate"]).reshape(1, 512),
        "gn_w": f(inputs["gn_w"]).reshape(8, 128),
        "pool_w": f(inputs["pool_w"][0]),
        "pool_b": f(inputs["pool_b"]).reshape(8, 128),
        "pool_scale": f(inputs["pool_scale"]).reshape(8, 128),
        "w_a": f(inputs["w_a"][0]),
        "w_b": f(inputs["w_b"][0]),
        "w_o": f(inputs["w_o"][0]),
        "ln_w": f(inputs["ln_w"]).reshape(1, D),
        "ln_b": f(inputs["ln_b"]).reshape(1, D),
    }
    shared.update(c)
    xs = f(inputs["x"])
    return [dict(shared, x=xs[b]) for b in range(N_CORES)]


def kernel(**inputs):
    nc = _get_program()
    in_maps = make_in_maps(inputs)
    res = run_bass_kernel_spmd(nc, in_maps, core_ids=list(range(N_CORES)))
    return np.stack([res.results[b]["out"] for b in range(N_CORES)], axis=0).astype(np.float32)
```

```python
import numpy as np
import ml_dtypes
from contextlib import ExitStack
import concourse.bass as bass
import concourse.mybir as mybir
from concourse.bass_utils import run_bass_kernel_spmd

F32 = mybir.dt.float32
BF16 = mybir.dt.bfloat16
AF = mybir.ActivationFunctionType
ALU = mybir.AluOpType

T = 2048
NT = 16
D = 1024
KC = 8
DIN = 7184
Q0, K0, V0, G0, AL0, U0, Z0, GA0, GB0 = 0, 512, 1024, 2048, 3072, 3088, 4112, 5136, 6160
DK = 128
EPS = 1e-5
ALPHA = 2.0 ** 0.25
NW = 3
N_CORES = 8

ENGS = ("pe", "act", "dve", "pool", "sp")


class Op:
    __slots__ = ("eng", "fn", "deps", "sig", "lane", "ticket", "is_dma")

    def __init__(self, eng, fn, deps, lane, is_dma):
        self.eng = eng
        self.fn = fn
        self.deps = deps
        self.sig = False
        self.lane = lane
        self.ticket = None
        self.is_dma = is_dma


class Buf:
    __slots__ = ("w", "r", "excl")

    def __init__(self, excl=False):
        self.w = None
        self.r = []
        self.excl = excl


class Prog:
    def __init__(self):
        self.ops = {e: [] for e in ENGS}
        self.last = {e: None for e in ENGS}

    def add(self, eng, fn, reads=(), writes=(), lane=None, extra=()):
        is_dma = lane is not None
        deps = []
        writes = list(writes) + [b for b in reads if b.excl]
        reads = [b for b in reads if not b.excl]
        for b in reads:
            if b.w is not None:
                deps.append(b.w)
        for b in writes:
            if b.w is not None:
                deps.append(b.w)
            deps.extend(b.r)
        deps.extend([d for d in extra if d is not None])
        op = Op(eng, fn, deps, lane if is_dma else eng, is_dma)
        for b in reads:
            b.r.append(op)
        for b in writes:
            b.w = op
            b.r = []
        self.ops[eng].append(op)
        if not is_dma:
            self.last[eng] = op
        return op

    def barrier(self, engs=("pe", "act", "dve")):
        lasts = [self.last[e] for e in engs if self.last[e] is not None]
        saved = dict(self.last)
        for e in engs:
            self.add(e, lambda eng: None, extra=[l for l in lasts if l.eng != e])
        self.last = saved
        return lasts

    def finalize(self):
        for e in ENGS:
            for op in self.ops[e]:
                keep = []
                for d in op.deps:
                    if d is op:
                        continue
                    if (not d.is_dma) and d.eng == op.eng and op.eng in ("pe", "sp"):
                        continue
                    keep.append(d)
                op.deps = keep
                for d in keep:
                    d.sig = True
        cnt = {}
        for e in ENGS:
            for op in self.ops[e]:
                if op.is_dma:
                    op.sig = True
                if op.sig:
                    inc = 16 if op.is_dma else 1
                    cnt[op.lane] = cnt.get(op.lane, 0) + inc
                    op.ticket = cnt[op.lane]
        return sorted(cnt.keys())

    def replay(self, eng_name, eng, sems):
        waited = {}
        for op in self.ops[eng_name]:
            need = {}
            for d in op.deps:
                if need.get(d.lane, 0) < d.ticket:
                    need[d.lane] = d.ticket
            for lane, val in need.items():
                if waited.get(lane, 0) < val:
                    eng.wait_ge(sems[lane], val)
                    waited[lane] = val
            ins = op.fn(eng)
            if op.sig:
                assert ins is not None
                ins.then_inc(sems[op.lane], 16 if op.is_dma else 1)


def _pool_mats():
    pm = np.zeros((128, 12, 128), np.float32)
    for g in range(4):
        w = 2 ** (g + 1)
        for t in range(128):
            for s in range(t - w + 1, t + 1):
                if s >= 0:
                    pm[s, g * 3 + 0, t] += 1.0 / w
                else:
                    pm[s + 128, g * 3 + 1, t] += 1.0 / w
            pm[t, g * 3 + 0, t] -= 1.0
            cnt = min(t + 1, w)
            for s in range(max(0, t - w + 1), t + 1):
                pm[s, g * 3 + 2, t] += 1.0 / cnt
            pm[t, g * 3 + 2, t] -= 1.0
    return pm.astype(ml_dtypes.bfloat16)


def _consts():
    ident = np.eye(128, dtype=np.float32)
    lt = np.triu(np.ones((128, 128), np.float32))
    return {
        "ident_bf": ident.astype(ml_dtypes.bfloat16),
        "ident_f": ident,
        "lt_bf": lt.astype(ml_dtypes.bfloat16),
        "pmat": _pool_mats(),
        "invc": np.tile((1.0 / np.arange(1, 17, dtype=np.float32))[None, :], (128, 1)).astype(np.float32),
    }


def build_program(dbg=(), stop_after='E'):
    nc = bass.Bass("TRN2", target_bir_lowering=False)
    _order = ['A', 'B', 'C', 'D', 'E']

    def _en(n):
        return _order.index(n) <= _order.index(stop_after)

    lasts = []
    stores = []

    def din(name, shape, dt=F32):
        return nc.dram_tensor(name, shape, dt, kind="ExternalInput").ap()

    x = din("x", [T, D])
    w_in = din("w_in", [D, DIN])
    w_gate_up = din("w_gate_up", [16, 512])
    b_gate = din("b_gate", [1, 512])
    gn_w = din("gn_w", [8, 128])
    pool_w = din("pool_w", [4, 256, 256])
    pool_b = din("pool_b", [8, 128])
    pool_scale = din("pool_scale", [8, 128])
    w_a = din("w_a", [D, D])
    w_b = din("w_b", [D, D])
    w_o = din("w_o", [D, D])
    ln_w = din("ln_w", [1, D])
    ln_b = din("ln_b", [1, D])
    ident_bf_d = din("ident_bf", [128, 128], BF16)
    ident_f_d = din("ident_f", [128, 128], F32)
    lt_bf_d = din("lt_bf", [128, 128], BF16)
    pmat_d = din("pmat", [128, 12, 128], BF16)
    invc_d = din("invc", [128, 16], F32)
    out = nc.dram_tensor("out", [T, D], F32, kind="ExternalOutput").ap()
    dbg_out = {}

    P = Prog()

    def PE(fn, r=(), w=(), extra=()):
        return P.add("pe", fn, r, w, extra=extra)

    def ACT(fn, r=(), w=(), extra=()):
        return P.add("act", fn, r, w, extra=extra)

    def DVE(fn, r=(), w=(), extra=()):
        return P.add("dve", fn, r, w, extra=extra)

    def POOL(fn, r=(), w=(), extra=()):
        return P.add("pool", fn, r, w, extra=extra)

    with ExitStack() as es:
        def sb(name, shape, dt):
            return es.enter_context(nc.sbuf_tensor(name, shape, dt))

        def ps(name, shape, dt):
            return es.enter_context(nc.psum_tensor(name, shape, dt))

        xT = sb("xT", [128, KC, T], BF16)
        b_xT = Buf()
        wslot = [sb(f"ws{i}", [128, KC, 512], BF16) for i in range(NW)]
        wbuf = [Buf() for _ in range(NW)]
        A1 = sb("A1", [128, 16384], BF16)
        A2 = sb("A2", [128, 16384], BF16)
        A3 = sb("A3", [128, 16384], BF16)
        A4 = sb("A4", [128, 16384], BF16)
        ident = sb("ident", [128, 128], BF16)
        lt = sb("lt", [128, 128], BF16)
        pmat = sb("pmat_s", [128, 12, 128], BF16)
        wg_aug = sb("wg_aug", [32, 512], BF16)
        wal = sb("wal", [128, KC, 16], BF16)
        poolw = sb("poolw", [128, 8, 256], BF16)
        colsT = sb("colsT", [128, 24], F32)
        cols16 = sb("cols16", [128, 8], F32)
        blast = sb("blast", [128, NT, 4], F32)
        eblast = sb("eblast", [128, NT, 4], F32)
        small = sb("small", [128, 64], F32)
        cpow = sb("cpow", [128, 16], F32)
        ones_t = sb("ones_t", [128, 128], F32)
        invc = sb("invc_s", [128, 16], F32)
        pfix = sb("pfix", [128, 16], F32)
        b_invc = Buf()
        b_ones = Buf()
        junk2 = sb("junk2", [128, D], BF16)
        small2 = sb("small2", [128, 32], F32)
        xt3 = sb("xt3", [128, D], F32)
        sgm_t = sb("sgm_t", [128, 2, 512], F32)
        b_cpow = Buf()
        b_ident, b_lt, b_pmat, b_wg, b_wal, b_poolw, b_cols, b_cols16 = [Buf() for _ in range(8)]

        pb = [ps(f"pb{i}", [128, 512], F32) for i in range(8)]
        pbT = [p[:].bitcast(BF16) for p in pb]
        bb = [Buf(excl=True) for _ in range(8)]
        bank_ctr = [0]

        NB = [4]

        def next_bank(n=None, base=0):
            n = NB[0] if n is None else n
            i = base + bank_ctr[0] % n
            bank_ctr[0] += 1
            return i

        wctr = [0]

        def load_w(src, c0):
            i = wctr[0] % NW
            wctr[0] += 1
            srcv = src[:, c0:c0 + 512].rearrange("(kc p) n -> p kc n", p=128)
            P.add("pool", lambda e, i=i, srcv=srcv: e.dma_start(out=wslot[i][:], in_=srcv),
                  writes=[wbuf[i]], lane=f"w{i}")
            return i

        def mkslot(ap, lane, buf=None):
            return (ap, buf if buf is not None else Buf(), lane)

        R = [mkslot(wslot[i][:], f"w{i}", wbuf[i]) for i in range(NW)]
        XC = mkslot(A2[:, 4096:8192].rearrange("p (k n) -> p k n", k=KC), "wxC")
        XA = mkslot(A2[:, 8192:12288].rearrange("p (k n) -> p k n", k=KC), "wxA")
        XB = mkslot(A2[:, 12288:16384].rearrange("p (k n) -> p k n", k=KC), "wxB")
        X123 = [mkslot(A1[:, i * 4096:(i + 1) * 4096].rearrange("p (k n) -> p k n", k=KC), f"wx{i}")
                for i in range(3)]

        def ld(slot, src, c0, extra=()):
            ap, buf, lane = slot
            srcv = src[:, c0:c0 + 512].rearrange("(kc p) n -> p kc n", p=128)
            P.add("pool", lambda e, ap=ap, srcv=srcv: e.dma_start(out=ap, in_=srcv),
                  writes=[buf], lane=lane, extra=extra)
            return (ap, buf)

        def scale_w(wt, cols):
            W, B = wt
            DVE(lambda e, W=W, cols=cols: e.tensor_tensor(
                W, W, cols.unsqueeze(2).broadcast_to([128, KC, 512]), ALU.mult),
                r=[B, b_cols, b_cols16], w=[B])

        P.add("sp", lambda e: e.dma_start(out=ident[:], in_=ident_bf_d[:, :]), writes=[b_ident], lane="c_id")
        P.add("sp", lambda e: e.dma_start(out=lt[:], in_=lt_bf_d[:, :]), writes=[b_lt], lane="c_lt")
        P.add("sp", lambda e: e.dma_start(out=invc[:], in_=invc_d[:, :]), writes=[b_invc], lane="c_pm")


        NXB = 8
        xb = [A2[:, s * 1024:(s + 1) * 1024] for s in range(NXB)]
        b_xb = [Buf() for _ in range(NXB)]
        b_xTb = [Buf() for _ in range(4)]
        p0_last = [None] * NT
        def x_load(t):
            P.add("pool", lambda e, t=t: e.dma_start(out=xb[t % NXB], in_=x[t * 128:(t + 1) * 128, :]),
                  writes=[b_xb[t % NXB]], lane=f"xb{t % NXB}")

        b_wg1 = Buf()
        for t in range(4):
            x_load(t)
        P.add("pool", lambda e: e.dma_start(
            out=wal[:], in_=w_in[:, AL0:AL0 + 16].rearrange("(kc p) n -> p kc n", p=128)),
            writes=[b_wal], lane="c_wal")
        P.add("pool", lambda e: e.dma_start(out=wg_aug[0:16, :], in_=w_gate_up[:, :]), writes=[b_wg], lane="c_wg")
        P.add("pool", lambda e: e.dma_start(out=wg_aug[16:17, :], in_=b_gate[:, :]), writes=[b_wg1], lane="c_wg1")
        wq = ld(R[0], w_in, Q0)
        for t in range(4, 8):
            x_load(t)
        wk = ld(R[1], w_in, K0)
        vW = [None, None]
        gW = [None, None]
        P.add("pool", lambda e: e.dma_start(
            out=poolw[:], in_=pool_w.rearrange("g (cc p) d -> p (g cc) d", p=128)),
            writes=[b_poolw], lane="c_pw")
        POOL(lambda e: e.memset(cpow[:, 0:4], -0.5), w=[b_cpow])
        POOL(lambda e: e.memset(cpow[:, 4:6], 256.0 * EPS), w=[b_cpow])
        POOL(lambda e: e.memset(cpow[:, 6:7], EPS), w=[b_cpow])
        POOL(lambda e: e.memset(cpow[:, 7:8], -1.0), w=[b_cpow])
        POOL(lambda e: e.memset(cpow[:, 8:10], 1.0 / D), w=[b_cpow])
        POOL(lambda e: e.memset(cpow[:, 10:11], -1.0 / D), w=[b_cpow])
        POOL(lambda e: e.memset(cpow[:, 11:12], 1.0 / D), w=[b_cpow])
        POOL(lambda e: e.memset(cpow[:, 12:16], 0.0), w=[b_cpow])

        def p0_tile(t):
            s = t % NXB
            bank = next_bank()
            for k in range(KC):
                p0_last[t] = PE(lambda e, k=k, s=s, bank=bank: e.transpose(
                    pbT[bank][:, k * 128:(k + 1) * 128], xb[s][:, k * 128:(k + 1) * 128], ident[:]),
                   r=[b_xb[s], b_ident], w=[bb[bank]])
            src = pbT[bank].rearrange("p (k t) -> p k t", k=KC)
            dst = xT[:, :, t * 128:(t + 1) * 128]
            if t % 2 == 0:
                DVE(lambda e, dst=dst, src=src: e.tensor_copy(dst, src), r=[bb[bank]], w=[b_xT, b_xTb[t // 4]])
            else:
                ACT(lambda e, dst=dst, src=src: e.copy(dst, src), r=[bb[bank]], w=[b_xT, b_xTb[t // 4]])

        for t in range(4):
            p0_tile(t)

        identf = A3[:, 4096:4096 + 256].bitcast(F32)
        rows = A3[:, 4352:4352 + 256].bitcast(F32)
        b_identf, b_rows = Buf(), Buf()
        P.add("sp", lambda e: e.dma_start(out=identf, in_=ident_f_d[:, :]), writes=[b_identf], lane="c_if")
        b_rows1, b_rows2 = Buf(), Buf()
        P.add("sp", lambda e: e.dma_start(out=rows[0:8, :], in_=gn_w[:, :]), writes=[b_rows], lane="c_rows")
        P.add("sp", lambda e: e.dma_start(out=rows[8:16, :], in_=pool_b[:, :]), writes=[b_rows1], lane="c_rows1")
        P.add("sp", lambda e: e.dma_start(out=rows[16:24, :], in_=pool_scale[:, :]), writes=[b_rows2], lane="c_rows2")
        PE(lambda e: e.transpose(pb[2][:, 0:24], rows[0:24, :], identf[0:24, 0:24]),
           r=[b_rows, b_rows1, b_rows2, b_identf], w=[bb[2]])
        DVE(lambda e: e.tensor_copy(colsT[:], pb[2][:, 0:24]), r=[bb[2]], w=[b_cols])
        DVE(lambda e: e.tensor_scalar(cols16[:], colsT[:, 0:8], 16.0, None, ALU.mult), r=[b_cols], w=[b_cols16])

        qbT = A1[:, 0:8192].rearrange("p (h t) -> p h t", h=4)
        kbT = A1[:, 8192:16384].rearrange("p (h t) -> p h t", h=4)
        EpT = A3[:, 0:8192].rearrange("p (h t) -> p h t", h=4)
        EmT = A3[:, 8192:16384].rearrange("p (h t) -> p h t", h=4)
        y_aT = A3[:].rearrange("p (k t) -> p k t", k=KC)
        kdT = A4[:, 0:8192].rearrange("p (h t) -> p h t", h=4)
        alT = A4[0:32, 8192:10240]
        la_hi = [A4[:, 10240 + s * 512:10240 + (s + 1) * 512] for s in range(2)]
        la_lo = [A4[:, 11264 + s * 512:11264 + (s + 1) * 512] for s in range(2)]
        etmp = [A4[:, 12288 + s * 1024:12288 + (s + 1) * 1024].bitcast(F32) for s in range(2)]
        b_alT, b_Ep, b_Em, b_qb, b_kb, b_kd = [Buf() for _ in range(6)]
        b_la = [Buf() for _ in range(2)]
        b_et = [Buf() for _ in range(2)]
        b_EpT = [Buf() for _ in range(NT)]
        b_EmT = [Buf() for _ in range(NT)]

        b_alTb = [Buf() for _ in range(4)]
        DVE(lambda e: e.memset(alT[0:32, :], 1.0), w=b_alTb)
        DVE(lambda e: e.memset(ones_t[:, :], 1.0), w=[b_ones])

        def a0_blk(tb):
            bank = next_bank()
            for k in range(KC):
                PE(lambda e, k=k, tb=tb, bank=bank: e.matmul(
                    pb[bank][0:16, :], wal[:, k, :], xT[:, k, tb * 512:(tb + 1) * 512],
                    start=(k == 0), stop=(k == KC - 1)), r=[b_wal, b_xTb[tb]], w=[bb[bank]])
            DVE(lambda e, tb=tb, bank=bank: e.tensor_copy(alT[0:16, tb * 512:(tb + 1) * 512], pb[bank][0:16, :]),
                r=[bb[bank]], w=[b_alTb[tb]])

        a0_blk(0)

        cs = [A4[:, 10240 + s * 1024:10240 + (s + 1) * 1024].bitcast(F32) for s in range(2)]

        def a1_pre(u):
            s = u % 2
            tb, h = u // 4, u % 4
            bpre = 4 + s
            PE(lambda e, tb=tb, h=h, bpre=bpre: e.matmul(
                pb[bpre][:, :], wg_aug[0:17, h * 128:(h + 1) * 128], alT[0:17, tb * 512:(tb + 1) * 512],
                start=True, stop=True), r=[b_alTb[tb], b_wg, b_wg1], w=[bb[bpre]])
            ACT(lambda e, s=s, bpre=bpre: e.activation(etmp[s], pb[bpre][:, :], AF.Exp, scale=-1.0),
                r=[bb[bpre]], w=[b_et[s]])
            ACT(lambda e, s=s: e.activation(etmp[s], etmp[s], AF.Ln, bias=1.0), r=[b_et[s]], w=[b_et[s]])

        def a1_hilo(u):
            s = u % 2
            for c in range(4):
                DVE(lambda e, s=s, c=c: e.tensor_tensor_scan(
                    cs[s][:, c * 128:(c + 1) * 128], ones_t[:, :], etmp[s][:, c * 128:(c + 1) * 128], 0.0,
                    ALU.mult, ALU.add), r=[b_et[s], b_ones], w=[b_la[s]])

        def a1_cum(u):
            s = u % 2
            tb, h = u // 4, u % 4
            sl = slice(tb * 512, (tb + 1) * 512)
            ACT(lambda e, s=s, h=h, sl=sl: e.activation(EpT[:, h, sl], cs[s], AF.Exp, scale=-1.0 / 16.0),
                r=[b_la[s]], w=[b_EpT[u]])
            ACT(lambda e, s=s, h=h, sl=sl: e.activation(EmT[:, h, sl], cs[s], AF.Exp, scale=1.0 / 16.0),
                r=[b_la[s]], w=[b_EmT[u]])
            ACT(lambda e, s=s, h=h, tb=tb: e.activation(
                eblast[:, tb * 4:(tb + 1) * 4, h], cs[s].rearrange("p (c t) -> p c t", c=4)[:, :, 127],
                AF.Exp, scale=-1.0 / 16.0), r=[b_la[s]], w=[b_eblast[u]])

        b_blast = [Buf() for _ in range(NT)]
        b_eblast = [Buf() for _ in range(NT)]
        b_qbb = [[Buf() for _ in range(4)] for _ in range(4)]
        b_kbb = [[Buf() for _ in range(4)] for _ in range(4)]
        b_kdb = [[Buf() for _ in range(4)] for _ in range(4)]

        def proj_qk(kind, h, tb):
            W, B = wq if kind == 0 else wk
            dstT = qbT if kind == 0 else kbT
            dbuf = b_qbb if kind == 0 else b_kbb
            bank = next_bank()
            for k in range(KC):
                PE(lambda e, k=k, h=h, tb=tb, bank=bank, W=W: e.matmul(
                    pb[bank][:, :], W[:, k, h * 128:(h + 1) * 128], xT[:, k, tb * 512:(tb + 1) * 512],
                    start=(k == 0), stop=(k == KC - 1)), r=[B, b_xTb[tb]], w=[bb[bank]])
            if kind == 0:
                DVE(lambda e, h=h, tb=tb, bank=bank: e.tensor_scalar(
                    qbT[:, h, tb * 512:(tb + 1) * 512], pb[bank][:, :], DK ** -0.5, None, ALU.mult),
                    r=[bb[bank]], w=[dbuf[h][tb]])
            else:
                ACT(lambda e, h=h, tb=tb, bank=bank: e.copy(
                    kbT[:, h, tb * 512:(tb + 1) * 512], pb[bank][:, :]), r=[bb[bank]], w=[dbuf[h][tb]])

        fin_pending = []

        def finalize_one(tb, h):
            sl = slice(tb * 512, (tb + 1) * 512)
            DVE(lambda e, h=h, sl=sl: e.tensor_tensor(qbT[:, h, sl], qbT[:, h, sl], EpT[:, h, sl], ALU.mult),
                r=b_EpT[tb * 4:(tb + 1) * 4], w=[b_qbb[h][tb]])
            DVE(lambda e, h=h, sl=sl: e.tensor_tensor(kbT[:, h, sl], kbT[:, h, sl], EmT[:, h, sl], ALU.mult),
                r=b_EmT[tb * 4:(tb + 1) * 4], w=[b_kbb[h][tb]])
            POOL(lambda e, h=h, sl=sl, tb=tb: e.tensor_tensor(
                kdT[:, h, sl].rearrange("p (c t) -> p c t", c=4),
                kbT[:, h, sl].rearrange("p (c t) -> p c t", c=4),
                eblast[:, tb * 4:(tb + 1) * 4, h].unsqueeze(2).broadcast_to([128, 4, 128]), ALU.mult),
                r=[b_kbb[h][tb]] + b_eblast[tb * 4:(tb + 1) * 4], w=[b_kdb[h][tb]])

        def finalize_qk(tb):
            for h in range(4):
                fin_pending.append((tb, h))

        def drain_fin(k=1):
            for _ in range(k):
                if fin_pending:
                    finalize_one(*fin_pending.pop(0))

        fill = [(lambda kind=kind, h=h, tb=tb: proj_qk(kind, h, tb))
                for tb in range(4) for kind in range(2) for h in range(4)]
        for t in range(NT):
            if t + 4 < NT:
                p0_tile(t + 4)
            if t + 8 < NT:
                x_load(t + 8)
            if t == 7:
                vW[0] = ld(R[2], w_in, V0)
                gW[0] = ld(XA, w_in, G0)
                gW[1] = ld(XB, w_in, G0 + 512)
            if t == 11:
                vW[1] = ld(XC, w_in, V0 + 512, extra=[p0_last[15]])
            a1_pre(t)
            fill.pop(0)()
            a1_hilo(t)
            if t > 0:
                a1_cum(t - 1)
                if t % 4 == 0:
                    finalize_qk(t // 4 - 1)
            drain_fin(1)
            fill.pop(0)()
            if t + 4 < NT and (t + 4) % 4 == 3:
                a0_blk((t + 4) // 4)
        a1_cum(NT - 1)
        finalize_qk(3)
        assert not fill


        NVS = 2
        v_s = [A2[:, s * 1024:(s + 1) * 1024] for s in range(NVS)]
        sg_s = [A2[:, 2048 + s * 1024:2048 + (s + 1) * 1024] for s in range(NVS)]
        b_vs = [Buf() for _ in range(NVS)]
        b_sgs = [Buf() for _ in range(NVS)]

        TB = 8192
        kd_s = [A4[:, TB + s * 256:TB + (s + 1) * 256].rearrange("p (j t) -> p j t", j=2) for s in range(2)]
        sT_s = [A4[:, TB + 512 + s * 256:TB + 512 + (s + 1) * 256].rearrange("p (j t) -> p j t", j=2) for s in range(2)]
        yb_s = [A4[:, TB + 1024 + s * 512:TB + 1024 + (s + 1) * 512].rearrange("p (j f) -> p j f", j=2) for s in range(2)]
        yb_s.append(A4[:, TB + 6656:TB + 7168].rearrange("p (j f) -> p j f", j=2))
        Sbf_s = [[A4[:, TB + 2048 + (hp * 2 + s) * 512:TB + 2048 + (hp * 2 + s + 1) * 512].rearrange(
            "p (j f) -> p j f", j=2) for s in range(2)] for hp in range(2)]
        S_f = [A4[:, TB + 4096 + hp * 1024:TB + 4096 + (hp + 1) * 1024].bitcast(F32).rearrange(
            "p (j f) -> p j f", j=2) for hp in range(2)]
        junk = A4[:, TB + 6144:TB + 6656].bitcast(F32)
        ss_s = [small[:, s * 2:(s + 1) * 2] for s in range(2)]
        rs_s = [small[:, 4 + s * 2:4 + (s + 1) * 2] for s in range(2)]
        b_kds = [Buf() for _ in range(2)]
        b_sTs = [Buf() for _ in range(2)]
        b_ybs = [Buf() for _ in range(3)]
        b_Sbf = [[Buf() for _ in range(2)] for _ in range(2)]
        b_S = [Buf() for _ in range(2)]
        b_ss = [Buf() for _ in range(2)]
        b_rs = [Buf() for _ in range(2)]
        b_yaT = [Buf() for _ in range(NT)]

        ga0 = ld(R[0], w_in, GA0)
        wa0 = ld(R[1], w_a, 0)

        def proj_vg(c, kind, blk):
            slot = c % NVS
            W, B = (vW if kind == 0 else gW)[blk]
            bank = next_bank(2, 2)
            for k in range(KC):
                PE(lambda e, k=k, c=c, bank=bank, W=W: e.matmul(
                    pb[bank][:, :], xT[:, k, c * 128:(c + 1) * 128], W[:, k, :],
                    start=(k == 0), stop=(k == KC - 1)), r=[B, b_xT], w=[bb[bank]])
            if kind == 0:
                DVE(lambda e, slot=slot, blk=blk, bank=bank: e.tensor_copy(
                    v_s[slot][:, blk * 512:(blk + 1) * 512], pb[bank][:, :]), r=[bb[bank]], w=[b_vs[slot]])
            else:
                ACT(lambda e, slot=slot, blk=blk, bank=bank: e.activation(
                    sg_s[slot][:, blk * 512:(blk + 1) * 512], pb[bank][:, :], AF.Silu), r=[bb[bank]], w=[b_sgs[slot]])

        def st_T1S(n):
            c, hp = n // 2, n % 2
            s = n % 2
            tb = c // 4
            for j in range(2):
                h = hp * 2 + j
                PE(lambda e, c=c, j=j, h=h: e.transpose(
                    pbT[4][:, j * 128:(j + 1) * 128], kdT[:, h, c * 128:(c + 1) * 128], ident[:]),
                   r=[b_kdb[h][tb], b_ident], w=[bb[4]])
            for j in range(2):
                h = hp * 2 + j
                PE(lambda e, c=c, j=j, h=h: e.matmul(
                    pb[5][:, j * 128:(j + 1) * 128],
                    kbT[:, h, c * 128:(c + 1) * 128], qbT[:, h, c * 128:(c + 1) * 128], start=True, stop=True),
                   r=[b_kbb[h][tb], b_qbb[h][tb]], w=[bb[5]])
            ACT(lambda e, s=s: e.copy(kd_s[s], pbT[4][:, 0:256].rearrange("p (j t) -> p j t", j=2)),
                r=[bb[4]], w=[b_kds[s]])
            DVE(lambda e, s=s: e.tensor_tensor(
                sT_s[s], pb[5][:, 0:256].rearrange("p (j t) -> p j t", j=2),
                lt[:].unsqueeze(1).broadcast_to([128, 2, 128]), ALU.mult),
                r=[bb[5], b_lt], w=[b_sTs[s]])

        def st_UO1(n):
            c, hp = n // 2, n % 2
            s = n % 2
            slot = c % NVS
            for j in range(2):
                h = hp * 2 + j
                PE(lambda e, s=s, j=j, h=h, slot=slot: e.matmul(
                    pb[6][:, j * 256:(j + 1) * 256], kd_s[s][:, j, :], v_s[slot][:, h * 256:(h + 1) * 256],
                    start=True, stop=True), r=[b_kds[s], b_vs[slot]], w=[bb[6]])
            for j in range(2):
                h = hp * 2 + j
                PE(lambda e, c=c, s=s, j=j, h=h, slot=slot: e.matmul(
                    pb[s][:, j * 256:(j + 1) * 256], sT_s[s][:, j, :], v_s[slot][:, h * 256:(h + 1) * 256],
                    start=(j == 0), stop=(c == 0), skip_group_check=True), r=[b_sTs[s], b_vs[slot]], w=[bb[s]])

        def st_O2(n):
            c, hp = n // 2, n % 2
            s = n % 2
            if c == 0:
                return
            tb = c // 4
            for j in range(2):
                h = hp * 2 + j
                PE(lambda e, c=c, s=s, j=j, h=h, hp=hp: e.matmul(
                    pb[s][:, j * 256:(j + 1) * 256], qbT[:, h, c * 128:(c + 1) * 128], Sbf_s[hp][c % 2][:, j, :],
                    start=False, stop=True, skip_group_check=True),
                   r=[b_qbb[h][tb], b_Sbf[hp][c % 2]], w=[bb[s]])

        def st_state(n):
            c, hp = n // 2, n % 2
            s = n % 2
            if c == NT - 1:
                return
            for j in range(2):
                h = hp * 2 + j
                if c == 0:
                    DVE(lambda e, s=s, j=j, hp=hp: e.tensor_copy(S_f[hp][:, j, :], pb[6][:, j * 256:(j + 1) * 256]),
                        r=[bb[6]], w=[b_S[hp]])
                else:
                    DVE(lambda e, c=c, s=s, j=j, h=h, hp=hp: e.scalar_tensor_tensor(
                        S_f[hp][:, j, :], S_f[hp][:, j, :], eblast[:, c, h:h + 1], pb[6][:, j * 256:(j + 1) * 256],
                        ALU.mult, ALU.add), r=[bb[6], b_S[hp], b_eblast[(c // 4) * 4 + h]], w=[b_S[hp]])
            ACT(lambda e, c=c, hp=hp: e.copy(Sbf_s[hp][(c + 1) % 2], S_f[hp]), r=[b_S[hp]], w=[b_Sbf[hp][(c + 1) % 2]])

        def st_sq(n):
            c, hp = n // 2, n % 2
            s = n % 2
            for j in range(2):
                ACT(lambda e, s=s, j=j: e.activation(junk, pb[s][:, j * 256:(j + 1) * 256], AF.Square,
                                                    accum_out=ss_s[s][:, j:j + 1]),
                    r=[bb[s]], w=[b_ss[s]])
            POOL(lambda e, s=s: e.tensor_tensor(rs_s[s], ss_s[s], cpow[:, 4:6], ALU.add),
                 r=[b_ss[s], b_cpow], w=[b_rs[s]])
            POOL(lambda e, s=s: e.tensor_tensor(rs_s[s], rs_s[s], cpow[:, 0:2], ALU.pow),
                 r=[b_cpow], w=[b_rs[s]])

        def st_y(n):
            c, hp = n // 2, n % 2
            s = n % 2
            slot = c % NVS
            for j in range(2):
                h = hp * 2 + j
                DVE(lambda e, s=s, j=j, h=h, slot=slot, y3=n % 3: e.scalar_tensor_tensor(
                    yb_s[y3][:, j, :], pb[s][:, j * 256:(j + 1) * 256], rs_s[s][:, j:j + 1],
                    sg_s[slot][:, h * 256:(h + 1) * 256], ALU.mult, ALU.mult),
                    r=[bb[s], b_rs[s], b_sgs[slot]], w=[b_ybs[n % 3]])

        def st_Y(n):
            c, hp = n // 2, n % 2
            s = n % 2
            for j in range(2):
                for i in range(2):
                    q = j * 2 + i
                    PE(lambda e, y3=n % 3, j=j, i=i, q=q: e.transpose(
                        pbT[7][:, q * 128:(q + 1) * 128],
                        yb_s[y3][:, j, i * 128:(i + 1) * 128], ident[:]),
                       r=[b_ybs[n % 3], b_ident], w=[bb[7]])
            ACT(lambda e, c=c, hp=hp: e.copy(
                y_aT[:, hp * 4:(hp + 1) * 4, c * 128:(c + 1) * 128],
                pbT[7][:, 0:512].rearrange("p (q t) -> p q t", q=4)),
                r=[bb[7]], w=[b_yaT[c]])

        maT = A4[:].rearrange("p (k t) -> p k t", k=KC)
        b_ma = [[Buf() for _ in range(4)] for _ in range(KC)]
        sgm = [sgm_t[:, s, :] for s in range(2)]
        tmpf = [A2[:, 2048 + s * 1024:2048 + (s + 1) * 1024].bitcast(F32) for s in range(2)]
        b_sgm = [Buf() for _ in range(2)]
        b_tmpf = [Buf() for _ in range(2)]
        sctr = [0]

        def gate_piece(tb, dc, gWt, wWt, actT, act_bufs, accumulate, banks=None):
            m = dc % 4
            s = sctr[0] % 2
            sctr[0] += 1
            gWa, gB = gWt
            wWa, wB = wWt
            bank = banks[0] if banks else next_bank()
            for k in range(KC):
                PE(lambda e, k=k, m=m, tb=tb, bank=bank, gWa=gWa: e.matmul(
                    pb[bank][:, :], gWa[:, k, m * 128:(m + 1) * 128],
                    xT[:, k, tb * 512:(tb + 1) * 512], start=(k == 0), stop=(k == KC - 1)),
                   r=[gB, b_xT], w=[bb[bank]])
            ACT(lambda e, s=s, bank=bank: e.activation(sgm[s], pb[bank][:, :], AF.Sigmoid),
                r=[bb[bank]], w=[b_sgm[s]])
            bank2 = banks[1] if banks else next_bank()
            for k in range(KC):
                PE(lambda e, k=k, m=m, tb=tb, bank2=bank2, wWa=wWa: e.matmul(
                    pb[bank2][:, :], wWa[:, k, m * 128:(m + 1) * 128],
                    actT[:, k, tb * 512:(tb + 1) * 512], start=(k == 0), stop=(k == KC - 1)),
                   r=[wB] + act_bufs, w=[bb[bank2]])
            if not accumulate:
                DVE(lambda e, s=s, dc=dc, tb=tb, bank2=bank2: e.tensor_tensor(
                    maT[:, dc, tb * 512:(tb + 1) * 512], pb[bank2][:, :], sgm[s], ALU.mult),
                    r=[bb[bank2], b_sgm[s]], w=[b_ma[dc][tb]])
            else:
                DVE(lambda e, s=s, bank2=bank2: e.tensor_tensor(tmpf[s], pb[bank2][:, :], sgm[s], ALU.mult),
                    r=[bb[bank2], b_sgm[s]], w=[b_tmpf[s]])
                DVE(lambda e, s=s, dc=dc, tb=tb: e.tensor_tensor(
                    maT[:, dc, tb * 512:(tb + 1) * 512], tmpf[s], maT[:, dc, tb * 512:(tb + 1) * 512], ALU.add),
                    r=[b_tmpf[s], b_ma[dc][tb]], w=[b_ma[dc][tb]])


        for kind in range(2):
            for blk in range(2):
                proj_vg(0, kind, blk)
        NS = 2 * NT
        st_T1S(0)
        for n in range(NS):
            c, hp = n // 2, n % 2
            if c + 1 < NT:
                proj_vg(c + 1, hp, 0)
            else:
                gate_piece(hp, 0, ga0, wa0, y_aT, b_yaT[hp * 4:(hp + 1) * 4], False, banks=(2, 3))
            drain_fin(1)
            st_UO1(n)
            if n + 1 < NS:
                st_T1S(n + 1)
            st_state(n)
            if n > 0:
                st_y(n - 1)
            if n > 1:
                st_Y(n - 2)
            if c + 1 < NT:
                proj_vg(c + 1, hp, 1)
            else:
                tbx, dcx = ((2, 0), (0, 1))[hp]
                gate_piece(tbx, dcx, ga0, wa0, y_aT, b_yaT[tbx * 4:(tbx + 1) * 4], False, banks=(2, 3))
            st_O2(n)
            st_sq(n)
            if n == 8:
                scale_w(wa0, cols16[:, 0:8])
        ga1 = ld(R[2], w_in, GA0 + 512)
        wa1 = ld(XC, w_a, 512)
        u0w = ld(XA, w_in, U0)
        u1w = ld(XB, w_in, U0 + 512)

        NB[0] = 8
        if _en('B'):

            gaw = [ga0, ga1]
            waw = [wa0, wa1]
            done = {(0, 0), (0, 1), (0, 2), (1, 0)}
            for dc, tb in ((1, 1), (1, 2)):
                gate_piece(tb, dc, gaw[dc // 4], waw[dc // 4], y_aT, b_yaT[tb * 4:(tb + 1) * 4], False, banks=(2, 3))
                done.add((dc, tb))
            st_y(NS - 1)
            st_Y(NS - 2)
            st_Y(NS - 1)
            cnt = 0
            for dc in range(KC):
                for tb in range(4):
                    if (dc, tb) in done:
                        continue
                    gate_piece(tb, dc, gaw[dc // 4], waw[dc // 4], y_aT, b_yaT[tb * 4:(tb + 1) * 4], False)
                    cnt += 1
                    if cnt == 4:
                        scale_w(wa1, cols16[:, 0:8])

        if _en('C'):

            u_t = A1[:].rearrange("p (t f) -> p t f", t=NT)
            pT = A2[:].rearrange("p (k t) -> p k t", k=KC)
            szT = A3[:].rearrange("p (k t) -> p k t", k=KC)
            b_u = [[Buf() for _ in range(NT)] for _ in range(2)]
            b_pT = [[Buf() for _ in range(4)] for _ in range(KC)]
            b_sz = [[Buf() for _ in range(4)] for _ in range(KC)]
            ectr = [0]
            z0w = ld(R[0], w_in, Z0)
            z1w = ld(R[1], w_in, Z0 + 512)
            gb0 = ld(R[2], w_in, GB0)
            uT = A1[:].rearrange("p (k t) -> p k t", k=KC)
            S_t = xt3[:].bitcast(BF16)
            b_uT = [Buf() for _ in range(KC)]
            b_pTc = [Buf() for _ in range(KC)]
            b_S = Buf()
            b_pfix = Buf()
            last_u_mm = [None]
            for blk in range(2):
                uW, uB = (u0w, u1w)[blk]
                for m in range(4):
                    cc = blk * 4 + m
                    for tb in range(4):
                        bank = next_bank()
                        for k in range(KC):
                            last_u_mm[0] = PE(lambda e, k=k, m=m, tb=tb, bank=bank, uW=uW: e.matmul(
                                pb[bank][:, :], uW[:, k, m * 128:(m + 1) * 128], xT[:, k, tb * 512:(tb + 1) * 512],
                                start=(k == 0), stop=(k == KC - 1)), r=[uB, b_xT], w=[bb[bank]])
                        ACT(lambda e, cc=cc, tb=tb, bank=bank: e.copy(
                            uT[:, cc, tb * 512:(tb + 1) * 512], pb[bank][:, :]), r=[bb[bank]], w=[b_uT[cc]])

            def pool_chunk(cc):
                w = 2 ** (cc // 2 + 1)
                uc = uT[:, cc, :]
                ex = [last_u_mm[0]] if cc >= 4 else []
                DVE(lambda e, uc=uc, w=w: e.tensor_tensor_scan(
                    S_t[:, 0:w], ones_t[:, 0:w], uc[:, 0:w], 0.0, ALU.mult, ALU.add),
                    r=[b_uT[cc], b_ones], w=[b_S])
                DVE(lambda e, uc=uc, w=w: e.tensor_tensor_scan(
                    S_t[:, w:T], uc[:, w:T], uc[:, 0:T - w], S_t[:, w - 1:w], ALU.add, ALU.subtract),
                    r=[b_uT[cc]], w=[b_S])
                DVE(lambda e, cc=cc, uc=uc, w=w: e.scalar_tensor_tensor(
                    pT[:, cc, :], S_t[:, :], 1.0 / w, uc, ALU.mult, ALU.subtract),
                    r=[b_S, b_uT[cc]], w=[b_pTc[cc]], extra=ex)
                DVE(lambda e, w=w: e.tensor_tensor(pfix[:, 0:w - 1], S_t[:, 0:w - 1], invc[:, 0:w - 1], ALU.mult),
                    r=[b_S, b_invc], w=[b_pfix])
                DVE(lambda e, cc=cc, uc=uc, w=w: e.tensor_tensor(
                    pT[:, cc, 0:w - 1], pfix[:, 0:w - 1], uc[:, 0:w - 1], ALU.subtract),
                    r=[b_pfix, b_uT[cc]], w=[b_pTc[cc]])

            for cc in range(KC):
                pool_chunk(cc)
            pool_done = [P.last["dve"]]
            wb0 = ld(X123[0], w_b, 0, extra=pool_done)
            wb1 = ld(X123[1], w_b, 512, extra=pool_done)
            wo0 = ld(X123[2], w_o, 0, extra=pool_done)
            for blk in range(2):
                zW, zB = (z0w, z1w)[blk]
                for m in range(4):
                    zc = blk * 4 + m
                    for tb in range(4):
                        bank = next_bank()
                        for k in range(KC):
                            PE(lambda e, k=k, m=m, tb=tb, bank=bank, zW=zW: e.matmul(
                                pb[bank][:, :], zW[:, k, m * 128:(m + 1) * 128], xT[:, k, tb * 512:(tb + 1) * 512],
                                start=(k == 0), stop=(k == KC - 1)), r=[zB, b_xT], w=[bb[bank]])
                        ACT(lambda e, zc=zc, tb=tb, bank=bank: e.activation(
                            szT[:, zc, tb * 512:(tb + 1) * 512], pb[bank][:, :], AF.Silu),
                            r=[bb[bank]], w=[b_sz[zc][tb]])
            gb1 = ld(R[0], w_in, GB0 + 512)
            wo1 = ld(R[1], w_o, 512)
            scale_w(wb0, colsT[:, 16:24])
            scale_w(wb1, colsT[:, 16:24])
            for tb in range(4):
                for dc in range(KC):
                    g = dc // 2
                    j = dc % 2
                    bank = next_bank()
                    for i in range(2):
                        PE(lambda e, g=g, j=j, i=i, tb=tb, bank=bank: e.matmul(
                            pb[bank][:, :], poolw[:, g * 2 + i, j * 128:(j + 1) * 128],
                            pT[:, g * 2 + i, tb * 512:(tb + 1) * 512], start=(i == 0), stop=(i == 1)),
                           r=[b_poolw, b_pTc[g * 2 + i]], w=[bb[bank]])
                    DVE(lambda e, dc=dc, tb=tb, bank=bank: e.scalar_tensor_tensor(
                        szT[:, dc, tb * 512:(tb + 1) * 512], pb[bank][:, :], colsT[:, 8 + dc:9 + dc],
                        szT[:, dc, tb * 512:(tb + 1) * 512], ALU.add, ALU.mult),
                        r=[bb[bank], b_cols, b_sz[dc][tb]], w=[b_sz[dc][tb]])

        if _en('E'):
            lasts = [P.last[e] for e in ("pe", "act", "dve")]

            gbw = [gb0, gb1]
            wbw = [wb0, wb1]
            wow = [wo0, wo1]
            xt = [A2[:, 4096 + s * 2048:4096 + (s + 1) * 2048].bitcast(F32) for s in range(2)]
            rr = [A2[:, 8192 + s * 2048:8192 + (s + 1) * 2048].bitcast(F32) for s in range(2)]
            yo = [A2[:, 12288 + s * 2048:12288 + (s + 1) * 2048].bitcast(F32) for s in range(2)]
            lnw_t = A1[:, 12288:14336].bitcast(F32)
            lnb_t = A1[:, 14336:16384].bitcast(F32)
            xt.append(xt3[:])
            b_xt = [Buf() for _ in range(3)]
            b_rr = [Buf() for _ in range(2)]
            b_yo = [Buf() for _ in range(2)]
            b_ln = Buf()
            stats = [small[:, 8 + s * 12:8 + (s + 1) * 12] for s in range(2)]
            mv = [small[:, 32 + s * 2:32 + (s + 1) * 2] for s in range(2)]
            rstd = [small[:, 36 + s:37 + s] for s in range(2)]
            nmr = [small[:, 38 + s:39 + s] for s in range(2)]
            b_st = [Buf() for _ in range(2)]
            b_mv = [Buf() for _ in range(2)]
            b_rn = [Buf() for _ in range(2)]
            sums = [small[:, 40 + s * 2:40 + (s + 1) * 2] for s in range(2)]
            b_sums = [Buf() for _ in range(2)]
            for i in range(4):
                rr.append(A3[:, i * 2048:(i + 1) * 2048].bitcast(F32))
                yo.append(A3[:, 8192 + i * 2048:8192 + (i + 1) * 2048].bitcast(F32))
                mv.append(small2[:, i * 2:(i + 1) * 2])
                rstd.append(small2[:, 8 + i:9 + i])
                nmr.append(small2[:, 12 + i:13 + i])
                sums.append(small2[:, 16 + i * 2:16 + (i + 1) * 2])
                for lst in (b_rr, b_yo, b_mv, b_rn, b_sums):
                    lst.append(Buf())

            def eslot(t):
                return t % 2 if t < NT - 4 else 2 + (t - (NT - 4))
            P.add("sp", lambda e: e.dma_start(out=lnw_t, in_=ln_w[0:1, :].broadcast_to([128, D])),
                  writes=[b_ln], lane="c_ln", extra=lasts)
            P.add("sp", lambda e: e.dma_start(out=lnb_t, in_=ln_b[0:1, :].broadcast_to([128, D])),
                  writes=[b_ln], lane="c_ln", extra=lasts)
            stores = []

            def d_piece(tb, dc):
                blk = dc // 4
                m = dc % 4
                s = sctr[0] % 2
                sctr[0] += 1
                gW, gB = gbw[blk]
                wW, wB = wbw[blk]
                bank = next_bank()
                for k in range(KC):
                    PE(lambda e, k=k, m=m, tb=tb, bank=bank, gW=gW: e.matmul(
                        pb[bank][:, :], gW[:, k, m * 128:(m + 1) * 128],
                        xT[:, k, tb * 512:(tb + 1) * 512], start=(k == 0), stop=(k == KC - 1)),
                       r=[gB, b_xT], w=[bb[bank]])
                ACT(lambda e, s=s, bank=bank: e.activation(sgm[s], pb[bank][:, :], AF.Sigmoid),
                    r=[bb[bank]], w=[b_sgm[s]])
                bank2 = next_bank()
                for k in range(KC):
                    PE(lambda e, k=k, m=m, tb=tb, bank2=bank2, wW=wW: e.matmul(
                        pb[bank2][:, :], wW[:, k, m * 128:(m + 1) * 128],
                        szT[:, k, tb * 512:(tb + 1) * 512], start=(k == 0), stop=(k == KC - 1)),
                       r=[wB] + [b_sz[k][tb] for k in range(KC)], w=[bb[bank2]])
                DVE(lambda e, s=s, bank2=bank2: e.tensor_tensor(tmpf[s], pb[bank2][:, :], sgm[s], ALU.mult),
                    r=[bb[bank2], b_sgm[s]], w=[b_tmpf[s]])
                DVE(lambda e, s=s, dc=dc, tb=tb: e.tensor_tensor(
                    maT[:, dc, tb * 512:(tb + 1) * 512], tmpf[s], maT[:, dc, tb * 512:(tb + 1) * 512], ALU.add),
                    r=[b_tmpf[s], b_ma[dc][tb]], w=[b_ma[dc][tb]])

            def ld_x(t):
                s = t % 3
                P.add("sp", lambda e, t=t, s=s: e.dma_start(out=xt[s], in_=x[t * 128:(t + 1) * 128, :]),
                      writes=[b_xt[s]], lane=f"xt{s}", extra=lasts)

            def e_tile(t):
                s = eslot(t)
                x3 = t % 3
                if t + 2 < NT:
                    ld_x(t + 2)
                for half in range(2):
                    bank = next_bank()
                    oW, oB = wow[half]
                    for k in range(KC):
                        PE(lambda e, k=k, t=t, bank=bank, oW=oW: e.matmul(
                            pb[bank][:, :], maT[:, k, t * 128:(t + 1) * 128], oW[:, k, :],
                            start=(k == 0), stop=(k == KC - 1)),
                           r=[oB] + [b_ma[k][t // 4] for k in range(KC)], w=[bb[bank]])
                    DVE(lambda e, s=s, x3=x3, half=half, bank=bank: e.scalar_tensor_tensor(
                        rr[s][:, half * 512:(half + 1) * 512], xt[x3][:, half * 512:(half + 1) * 512], ALPHA,
                        pb[bank][:, :], ALU.mult, ALU.add), r=[bb[bank], b_xt[x3]], w=[b_rr[s]])
                ACT(lambda e, s=s: e.activation(junk2[:], rr[s], AF.Identity, accum_out=sums[s][:, 0:1]),
                    r=[b_rr[s]], w=[b_sums[s]])
                ACT(lambda e, s=s: e.activation(junk2[:], rr[s], AF.Square, accum_out=sums[s][:, 1:2]),
                    r=[b_rr[s]], w=[b_sums[s]])
                if t == NT - 1:
                    e_norm(t - 1)
                    e_fin(t - 2)
                    DVE(lambda e, s=s: e.tensor_tensor(mv[s], sums[s], cpow[:, 10:12], ALU.mult),
                        r=[b_sums[s], b_cpow], w=[b_mv[s]])
                    DVE(lambda e, s=s: e.scalar_tensor_tensor(rstd[s], mv[s][:, 0:1], mv[s][:, 0:1], mv[s][:, 1:2],
                                                             ALU.mult, ALU.subtract),
                        r=[b_mv[s]], w=[b_rn[s]])
                    ACT(lambda e, s=s: e.activation(rstd[s], rstd[s], AF.Sqrt, bias=EPS, scale=-1.0),
                        r=[b_rn[s]], w=[b_rn[s]])
                    DVE(lambda e, s=s: e.reciprocal(rstd[s], rstd[s]), r=[b_rn[s]], w=[b_rn[s]])
                    DVE(lambda e, s=s: e.tensor_tensor(nmr[s], mv[s][:, 0:1], rstd[s], ALU.mult),
                        r=[b_mv[s], b_rn[s]], w=[b_rn[s]])
                    return
                POOL(lambda e, s=s: e.tensor_tensor(mv[s], sums[s], cpow[:, 8:10], ALU.mult),
                     r=[b_sums[s], b_cpow], w=[b_mv[s]])
                POOL(lambda e, s=s: e.tensor_tensor(rstd[s], mv[s][:, 0:1], mv[s][:, 0:1], ALU.mult),
                     r=[b_mv[s]], w=[b_rn[s]])
                POOL(lambda e, s=s: e.tensor_tensor(rstd[s], mv[s][:, 1:2], rstd[s], ALU.subtract),
                     r=[b_mv[s]], w=[b_rn[s]])
                POOL(lambda e, s=s: e.tensor_tensor(rstd[s], rstd[s], cpow[:, 6:7], ALU.add),
                     r=[b_cpow], w=[b_rn[s]])
                POOL(lambda e, s=s: e.tensor_tensor(rstd[s], rstd[s], cpow[:, 0:1], ALU.pow),
                     r=[b_cpow], w=[b_rn[s]])
                POOL(lambda e, s=s: e.tensor_tensor(nmr[s], mv[s][:, 0:1], rstd[s], ALU.mult),
                     r=[b_mv[s]], w=[b_rn[s]])
                POOL(lambda e, s=s: e.tensor_tensor(nmr[s], nmr[s], cpow[:, 7:8], ALU.mult),
                     r=[b_cpow], w=[b_rn[s]])

            b_yoh = [Buf(), Buf()]

            def e_norm(t):
                s = eslot(t)
                if t == NT - 1:
                    for half in range(2):
                        cs_ = slice(half * 512, (half + 1) * 512)
                        ACT(lambda e, s=s, cs_=cs_: e.activation(yo[s][:, cs_], rr[s][:, cs_], AF.Identity,
                                                                bias=nmr[s], scale=rstd[s]),
                            r=[b_rr[s], b_rn[s]], w=[b_yoh[half]])
                    return
                ACT(lambda e, s=s: e.activation(yo[s], rr[s], AF.Identity, bias=nmr[s], scale=rstd[s]),
                    r=[b_rr[s], b_rn[s]], w=[b_yo[s]])

            def e_fin(t):
                s = eslot(t)
                if t == NT - 1:
                    for half in range(2):
                        cs_ = slice(half * 512, (half + 1) * 512)
                        bh = Buf()
                        DVE(lambda e, s=s, cs_=cs_: e.tensor_tensor(yo[s][:, cs_], yo[s][:, cs_], lnw_t[:, cs_], ALU.mult),
                            r=[b_yoh[half], b_ln], w=[bh])
                        DVE(lambda e, s=s, cs_=cs_: e.tensor_tensor(yo[s][:, cs_], yo[s][:, cs_], lnb_t[:, cs_], ALU.add),
                            r=[bh, b_ln], w=[bh])
                        stores.append(P.add("sp", lambda e, t=t, s=s, cs_=cs_: e.dma_start(
                            out=out[t * 128:(t + 1) * 128, cs_], in_=yo[s][:, cs_]), reads=[bh], lane=f"outh{half}"))
                    return
                DVE(lambda e, s=s: e.tensor_tensor(yo[s], yo[s], lnw_t, ALU.mult), r=[b_yo[s], b_ln], w=[b_yo[s]])
                if True:
                    DVE(lambda e, s=s: e.tensor_tensor(yo[s], yo[s], lnb_t, ALU.add), r=[b_yo[s], b_ln], w=[b_yo[s]])
                else:
                    POOL(lambda e, s=s: e.tensor_tensor(yo[s], yo[s], lnb_t, ALU.add), r=[b_yo[s], b_ln], w=[b_yo[s]])
                stores.append(P.add("sp", lambda e, t=t, s=s: e.dma_start(out=out[t * 128:(t + 1) * 128, :], in_=yo[s]),
                                    reads=[b_yo[s]], lane=f"out{s}"))

            ld_x(0)
            ld_x(1)
            for dc in range(KC):
                d_piece(0, dc)
            for tb in range(4):
                for i in range(4):
                    t = 4 * tb + i
                    if tb + 1 < 4:
                        d_piece(tb + 1, 2 * i)
                        if i == 3:
                            d_piece(tb + 1, 2 * i + 1)
                    if t == NT - 4:
                        ACT(lambda e: e.activation(small[:, 56:57], cpow[:, 6:7], AF.Sqrt), r=[b_cpow])
                    e_tile(t)
                    if t > 0 and t != NT - 1:
                        e_norm(t - 1)
                    if t > 1 and t != NT - 1:
                        e_fin(t - 2)
                    if tb + 1 < 4 and i != 3:
                        d_piece(tb + 1, 2 * i + 1)
            e_norm(NT - 1)
            e_fin(NT - 2)
            e_fin(NT - 1)

        if dbg:
            lasts = P.barrier()
        for name, (ap, shape, dt, bufs) in dbg_specs(locals(), dbg).items():
            dten = nc.dram_tensor("dbg_" + name, shape, dt, kind="ExternalOutput").ap()
            stores.append(P.add("sp", lambda e, dten=dten, ap=ap: e.dma_start(out=dten, in_=ap),
                                reads=bufs, lane="dbg_" + name, extra=lasts))
            dbg_out[name] = "dbg_" + name

        P.add("sp", lambda e: e.nop(), extra=stores)

        lanes = P.finalize()
        sems = {ln: es.enter_context(nc.semaphore("s_" + ln)) for ln in lanes}
        with nc.Block() as block:
            @block.tensor
            def _(e):
                P.replay("pe", e, sems)

            @block.scalar
            def _(e):
                P.replay("act", e, sems)

            @block.vector
            def _(e):
                P.replay("dve", e, sems)

            @block.gpsimd
            def _(e):
                P.replay("pool", e, sems)

            @block.sync
            def _(e):
                P.replay("sp", e, sems)
    return nc, dbg_out


def dbg_specs(loc, dbg):
    specs = {}
    for name in dbg:
        if name == "xT":
            specs[name] = (loc["xT"][:], [128, KC, T], BF16, [loc["b_xT"]])
        elif name == "y_aT":
            specs[name] = (loc["A3"][:], [128, 16384], BF16, [])
        elif name == "maT":
            specs[name] = (loc["A4"][:], [128, 16384], BF16, [])
        elif name == "A1":
            specs[name] = (loc["A1"][:], [128, 16384], BF16, [])
        elif name == "A2":
            specs[name] = (loc["A2"][:], [128, 16384], BF16, [])
        elif name == "colsT":
            specs[name] = (loc["colsT"][:], [128, 24], F32, [])
        elif name == "eblast":
            specs[name] = (loc["eblast"][:], [128, NT, 4], F32, [])
    return specs


_CACHE = {}


def _get_program():
    if "nc" not in _CACHE:
        _CACHE["nc"] = build_program()[0]
    return _CACHE["nc"]


def make_in_maps(inputs):
    c = _consts()
    f = lambda a: np.ascontiguousarray(np.asarray(a, dtype=np.float32))
    shared = {
        "w_in": f(inputs["w_in"][0]),
        "w_gate_up": f(inputs["w_gate_up"][0]),
        "b_gate": f(inputs["b_gate"]).reshape(1, 512),
        "gn_w": f(inputs["gn_w"]).reshape(8, 128),
        "pool_w": f(inputs["pool_w"][0]),
        "pool_b": f(inputs["pool_b"]).reshape(8, 128),
        "pool_scale": f(inputs["pool_scale"]).reshape(8, 128),
        "w_a": f(inputs["w_a"][0]),
        "w_b": f(inputs["w_b"][0]),
        "w_o": f(inputs["w_o"][0]),
        "ln_w": f(inputs["ln_w"]).reshape(1, D),
        "ln_b": f(inputs["ln_b"]).reshape(1, D),
    }
    shared.update(c)
    xs = f(inputs["x"])
    return [dict(shared, x=xs[b]) for b in range(N_CORES)]


def kernel(**inputs):
    nc = _get_program()
    in_maps = make_in_maps(inputs)
    res = run_bass_kernel_spmd(nc, in_maps, core_ids=list(range(N_CORES)))
    return np.stack([res.results[b]["out"] for b in range(N_CORES)], axis=0).astype(np.float32)
```

```python
import numpy as np
import ml_dtypes
from contextlib import ExitStack
import concourse.bass as bass
import concourse.mybir as mybir
from concourse.bass_utils import run_bass_kernel_spmd

F32 = mybir.dt.float32
BF16 = mybir.dt.bfloat16
AF = mybir.ActivationFunctionType
ALU = mybir.AluOpType

T = 2048
NT = 16
D = 1024
KC = 8
DIN = 7184
Q0, K0, V0, G0, AL0, U0, Z0, GA0, GB0 = 0, 512, 1024, 2048, 3072, 3088, 4112, 5136, 6160
DK = 128
EPS = 1e-5
ALPHA = 2.0 ** 0.25
NW = 3
N_CORES = 8

ENGS = ("pe", "act", "dve", "pool", "sp")


class Op:
    __slots__ = ("eng", "fn", "deps", "sig", "lane", "ticket", "is_dma")

    def __init__(self, eng, fn, deps, lane, is_dma):
        self.eng = eng
        self.fn = fn
        self.deps = deps
        self.sig = False
        self.lane = lane
        self.ticket = None
        self.is_dma = is_dma


class Buf:
    __slots__ = ("w", "r", "excl")

    def __init__(self, excl=False):
        self.w = None
        self.r = []
        self.excl = excl


class Prog:
    def __init__(self):
        self.ops = {e: [] for e in ENGS}
        self.last = {e: None for e in ENGS}

    def add(self, eng, fn, reads=(), writes=(), lane=None, extra=()):
        is_dma = lane is not None
        deps = []
        writes = list(writes) + [b for b in reads if b.excl]
        reads = [b for b in reads if not b.excl]
        for b in reads:
            if b.w is not None:
                deps.append(b.w)
        for b in writes:
            if b.w is not None:
                deps.append(b.w)
            deps.extend(b.r)
        deps.extend([d for d in extra if d is not None])
        op = Op(eng, fn, deps, lane if is_dma else eng, is_dma)
        for b in reads:
            b.r.append(op)
        for b in writes:
            b.w = op
            b.r = []
        self.ops[eng].append(op)
        if not is_dma:
            self.last[eng] = op
        return op

    def barrier(self, engs=("pe", "act", "dve")):
        lasts = [self.last[e] for e in engs if self.last[e] is not None]
        saved = dict(self.last)
        for e in engs:
            self.add(e, lambda eng: None, extra=[l for l in lasts if l.eng != e])
        self.last = saved
        return lasts

    def finalize(self):
        for e in ENGS:
            for op in self.ops[e]:
                keep = []
                for d in op.deps:
                    if d is op:
                        continue
                    if (not d.is_dma) and d.eng == op.eng and op.eng in ("pe", "sp"):
                        continue
                    keep.append(d)
                op.deps = keep
                for d in keep:
                    d.sig = True
        cnt = {}
        for e in ENGS:
            for op in self.ops[e]:
                if op.is_dma:
                    op.sig = True
                if op.sig:
                    inc = 16 if op.is_dma else 1
                    cnt[op.lane] = cnt.get(op.lane, 0) + inc
                    op.ticket = cnt[op.lane]
        return sorted(cnt.keys())

    def replay(self, eng_name, eng, sems):
        waited = {}
        for op in self.ops[eng_name]:
            need = {}
            for d in op.deps:
                if need.get(d.lane, 0) < d.ticket:
                    need[d.lane] = d.ticket
            for lane, val in need.items():
                if waited.get(lane, 0) < val:
                    eng.wait_ge(sems[lane], val)
                    waited[lane] = val
            ins = op.fn(eng)
            if op.sig:
                assert ins is not None
                ins.then_inc(sems[op.lane], 16 if op.is_dma else 1)


def _pool_mats():
    pm = np.zeros((128, 12, 128), np.float32)
    for g in range(4):
        w = 2 ** (g + 1)
        for t in range(128):
            for s in range(t - w + 1, t + 1):
                if s >= 0:
                    pm[s, g * 3 + 0, t] += 1.0 / w
                else:
                    pm[s + 128, g * 3 + 1, t] += 1.0 / w
            pm[t, g * 3 + 0, t] -= 1.0
            cnt = min(t + 1, w)
            for s in range(max(0, t - w + 1), t + 1):
                pm[s, g * 3 + 2, t] += 1.0 / cnt
            pm[t, g * 3 + 2, t] -= 1.0
    return pm.astype(ml_dtypes.bfloat16)


def _consts():
    ident = np.eye(128, dtype=np.float32)
    lt = np.triu(np.ones((128, 128), np.float32))
    return {
        "ident_bf": ident.astype(ml_dtypes.bfloat16),
        "ident_f": ident,
        "lt_bf": lt.astype(ml_dtypes.bfloat16),
        "pmat": _pool_mats(),
        "invc": np.tile((1.0 / np.arange(1, 17, dtype=np.float32))[None, :], (128, 1)).astype(np.float32),
    }


def build_program(dbg=(), stop_after='E'):
    nc = bass.Bass("TRN2", target_bir_lowering=False)
    _order = ['A', 'B', 'C', 'D', 'E']

    def _en(n):
        return _order.index(n) <= _order.index(stop_after)

    lasts = []
    stores = []

    def din(name, shape, dt=F32):
        return nc.dram_tensor(name, shape, dt, kind="ExternalInput").ap()

    x = din("x", [T, D])
    w_in = din("w_in", [D, DIN])
    w_gate_up = din("w_gate_up", [16, 512])
    b_gate = din("b_gate", [1, 512])
    gn_w = din("gn_w", [8, 128])
    pool_w = din("pool_w", [4, 256, 256])
    pool_b = din("pool_b", [8, 128])
    pool_scale = din("pool_scale", [8, 128])
    w_a = din("w_a", [D, D])
    w_b = din("w_b", [D, D])
    w_o = din("w_o", [D, D])
    ln_w = din("ln_w", [1, D])
    ln_b = din("ln_b", [1, D])
    ident_bf_d = din("ident_bf", [128, 128], BF16)
    ident_f_d = din("ident_f", [128, 128], F32)
    lt_bf_d = din("lt_bf", [128, 128], BF16)
    pmat_d = din("pmat", [128, 12, 128], BF16)
    invc_d = din("invc", [128, 16], F32)
    out = nc.dram_tensor("out", [T, D], F32, kind="ExternalOutput").ap()
    dbg_out = {}

    P = Prog()

    def PE(fn, r=(), w=(), extra=()):
        return P.add("pe", fn, r, w, extra=extra)

    def ACT(fn, r=(), w=(), extra=()):
        return P.add("act", fn, r, w, extra=extra)

    def DVE(fn, r=(), w=(), extra=()):
        return P.add("dve", fn, r, w, extra=extra)

    def POOL(fn, r=(), w=(), extra=()):
        return P.add("pool", fn, r, w, extra=extra)

    with ExitStack() as es:
        def sb(name, shape, dt):
            return es.enter_context(nc.sbuf_tensor(name, shape, dt))

        def ps(name, shape, dt):
            return es.enter_context(nc.psum_tensor(name, shape, dt))

        xT = sb("xT", [128, KC, T], BF16)
        b_xT = Buf()
        wslot = [sb(f"ws{i}", [128, KC, 512], BF16) for i in range(NW)]
        wbuf = [Buf() for _ in range(NW)]
        A1 = sb("A1", [128, 16384], BF16)
        A2 = sb("A2", [128, 16384], BF16)
        A3 = sb("A3", [128, 16384], BF16)
        A4 = sb("A4", [128, 16384], BF16)
        ident = sb("ident", [128, 128], BF16)
        lt = sb("lt", [128, 128], BF16)
        pmat = sb("pmat_s", [128, 12, 128], BF16)
        wg_aug = sb("wg_aug", [32, 512], BF16)
        wal = sb("wal", [128, KC, 16], BF16)
        poolw = sb("poolw", [128, 8, 256], BF16)
        colsT = sb("colsT", [128, 24], F32)
        cols16 = sb("cols16", [128, 8], F32)
        blast = sb("blast", [128, NT, 4], F32)
        eblast = sb("eblast", [128, NT, 4], F32)
        small = sb("small", [128, 64], F32)
        cpow = sb("cpow", [128, 16], F32)
        ones_t = sb("ones_t", [128, 128], F32)
        invc = sb("invc_s", [128, 16], F32)
        pfix = sb("pfix", [128, 16], F32)
        b_invc = Buf()
        b_ones = Buf()
        junk2 = sb("junk2", [128, D], BF16)
        small2 = sb("small2", [128, 32], F32)
        xt3 = sb("xt3", [128, D], F32)
        sgm_t = sb("sgm_t", [128, 2, 512], F32)
        b_cpow = Buf()
        b_ident, b_lt, b_pmat, b_wg, b_wal, b_poolw, b_cols, b_cols16 = [Buf() for _ in range(8)]

        pb = [ps(f"pb{i}", [128, 512], F32) for i in range(8)]
        pbT = [p[:].bitcast(BF16) for p in pb]
        bb = [Buf(excl=True) for _ in range(8)]
        bank_ctr = [0]

        NB = [4]

        def next_bank(n=None, base=0):
            n = NB[0] if n is None else n
            i = base + bank_ctr[0] % n
            bank_ctr[0] += 1
            return i

        wctr = [0]

        def load_w(src, c0):
            i = wctr[0] % NW
            wctr[0] += 1
            srcv = src[:, c0:c0 + 512].rearrange("(kc p) n -> p kc n", p=128)
            P.add("pool", lambda e, i=i, srcv=srcv: e.dma_start(out=wslot[i][:], in_=srcv),
                  writes=[wbuf[i]], lane=f"w{i}")
            return i

        def mkslot(ap, lane, buf=None):
            return (ap, buf if buf is not None else Buf(), lane)

        R = [mkslot(wslot[i][:], f"w{i}", wbuf[i]) for i in range(NW)]
        XC = mkslot(A2[:, 4096:8192].rearrange("p (k n) -> p k n", k=KC), "wxC")
        XA = mkslot(A2[:, 8192:12288].rearrange("p (k n) -> p k n", k=KC), "wxA")
        XB = mkslot(A2[:, 12288:16384].rearrange("p (k n) -> p k n", k=KC), "wxB")
        X123 = [mkslot(A1[:, i * 4096:(i + 1) * 4096].rearrange("p (k n) -> p k n", k=KC), f"wx{i}")
                for i in range(3)]

        def ld(slot, src, c0, extra=()):
            ap, buf, lane = slot
            srcv = src[:, c0:c0 + 512].rearrange("(kc p) n -> p kc n", p=128)
            P.add("pool", lambda e, ap=ap, srcv=srcv: e.dma_start(out=ap, in_=srcv),
                  writes=[buf], lane=lane, extra=extra)
            return (ap, buf)

        def scale_w(wt, cols):
            W, B = wt
            DVE(lambda e, W=W, cols=cols: e.tensor_tensor(
                W, W, cols.unsqueeze(2).broadcast_to([128, KC, 512]), ALU.mult),
                r=[B, b_cols, b_cols16], w=[B])

        P.add("sp", lambda e: e.dma_start(out=ident[:], in_=ident_bf_d[:, :]), writes=[b_ident], lane="c_id")
        P.add("sp", lambda e: e.dma_start(out=lt[:], in_=lt_bf_d[:, :]), writes=[b_lt], lane="c_lt")
        P.add("sp", lambda e: e.dma_start(out=invc[:], in_=invc_d[:, :]), writes=[b_invc], lane="c_pm")


        NXB = 8
        xb = [A2[:, s * 1024:(s + 1) * 1024] for s in range(NXB)]
        b_xb = [Buf() for _ in range(NXB)]
        b_xTb = [Buf() for _ in range(4)]
        p0_last = [None] * NT
        def x_load(t):
            P.add("pool", lambda e, t=t: e.dma_start(out=xb[t % NXB], in_=x[t * 128:(t + 1) * 128, :]),
                  writes=[b_xb[t % NXB]], lane=f"xb{t % NXB}")

        b_wg1 = Buf()
        for t in range(4):
            x_load(t)
        P.add("pool", lambda e: e.dma_start(
            out=wal[:], in_=w_in[:, AL0:AL0 + 16].rearrange("(kc p) n -> p kc n", p=128)),
            writes=[b_wal], lane="c_wal")
        P.add("pool", lambda e: e.dma_start(out=wg_aug[0:16, :], in_=w_gate_up[:, :]), writes=[b_wg], lane="c_wg")
        P.add("pool", lambda e: e.dma_start(out=wg_aug[16:17, :], in_=b_gate[:, :]), writes=[b_wg1], lane="c_wg1")
        wq = ld(R[0], w_in, Q0)
        for t in range(4, 8):
            x_load(t)
        wk = ld(R[1], w_in, K0)
        vW = [None, None]
        gW = [None, None]
        P.add("pool", lambda e: e.dma_start(
            out=poolw[:], in_=pool_w.rearrange("g (cc p) d -> p (g cc) d", p=128)),
            writes=[b_poolw], lane="c_pw")
        POOL(lambda e: e.memset(cpow[:, 0:4], -0.5), w=[b_cpow])
        POOL(lambda e: e.memset(cpow[:, 4:6], 256.0 * EPS), w=[b_cpow])
        POOL(lambda e: e.memset(cpow[:, 6:7], EPS), w=[b_cpow])
        POOL(lambda e: e.memset(cpow[:, 7:8], -1.0), w=[b_cpow])
        POOL(lambda e: e.memset(cpow[:, 8:10], 1.0 / D), w=[b_cpow])
        POOL(lambda e: e.memset(cpow[:, 10:11], -1.0 / D), w=[b_cpow])
        POOL(lambda e: e.memset(cpow[:, 11:12], 1.0 / D), w=[b_cpow])

        def p0_tile(t):
            s = t % NXB
            bank = next_bank()
            for k in range(KC):
                p0_last[t] = PE(lambda e, k=k, s=s, bank=bank: e.transpose(
                    pbT[bank][:, k * 128:(k + 1) * 128], xb[s][:, k * 128:(k + 1) * 128], ident[:]),
                   r=[b_xb[s], b_ident], w=[bb[bank]])
            src = pbT[bank].rearrange("p (k t) -> p k t", k=KC)
            dst = xT[:, :, t * 128:(t + 1) * 128]
            if t % 2 == 0:
                DVE(lambda e, dst=dst, src=src: e.tensor_copy(dst, src), r=[bb[bank]], w=[b_xT, b_xTb[t // 4]])
            else:
                ACT(lambda e, dst=dst, src=src: e.copy(dst, src), r=[bb[bank]], w=[b_xT, b_xTb[t // 4]])

        for t in range(4):
            p0_tile(t)

        identf = A3[:, 4096:4096 + 256].bitcast(F32)
        rows = A3[:, 4352:4352 + 256].bitcast(F32)
        b_identf, b_rows = Buf(), Buf()
        P.add("sp", lambda e: e.dma_start(out=identf, in_=ident_f_d[:, :]), writes=[b_identf], lane="c_if")
        b_rows1, b_rows2 = Buf(), Buf()
        P.add("sp", lambda e: e.dma_start(out=rows[0:8, :], in_=gn_w[:, :]), writes=[b_rows], lane="c_rows")
        P.add("sp", lambda e: e.dma_start(out=rows[8:16, :], in_=pool_b[:, :]), writes=[b_rows1], lane="c_rows1")
        P.add("sp", lambda e: e.dma_start(out=rows[16:24, :], in_=pool_scale[:, :]), writes=[b_rows2], lane="c_rows2")
        PE(lambda e: e.transpose(pb[2][:, 0:24], rows[0:24, :], identf[0:24, 0:24]),
           r=[b_rows, b_rows1, b_rows2, b_identf], w=[bb[2]])
        DVE(lambda e: e.tensor_copy(colsT[:], pb[2][:, 0:24]), r=[bb[2]], w=[b_cols])
        DVE(lambda e: e.tensor_scalar(cols16[:], colsT[:, 0:8], 16.0, None, ALU.mult), r=[b_cols], w=[b_cols16])

        qbT = A1[:, 0:8192].rearrange("p (h t) -> p h t", h=4)
        kbT = A1[:, 8192:16384].rearrange("p (h t) -> p h t", h=4)
        EpT = A3[:, 0:8192].rearrange("p (h t) -> p h t", h=4)
        EmT = A3[:, 8192:16384].rearrange("p (h t) -> p h t", h=4)
        y_aT = A3[:].rearrange("p (k t) -> p k t", k=KC)
        kdT = A4[:, 0:8192].rearrange("p (h t) -> p h t", h=4)
        alT = A4[0:32, 8192:10240]
        la_hi = [A4[:, 10240 + s * 512:10240 + (s + 1) * 512] for s in range(2)]
        la_lo = [A4[:, 11264 + s * 512:11264 + (s + 1) * 512] for s in range(2)]
        etmp = [A4[:, 12288 + s * 1024:12288 + (s + 1) * 1024].bitcast(F32) for s in range(2)]
        b_alT, b_Ep, b_Em, b_qb, b_kb, b_kd = [Buf() for _ in range(6)]
        b_la = [Buf() for _ in range(2)]
        b_et = [Buf() for _ in range(2)]
        b_EpT = [Buf() for _ in range(NT)]
        b_EmT = [Buf() for _ in range(NT)]

        b_alTb = [Buf() for _ in range(4)]
        DVE(lambda e: e.memset(alT[0:32, :], 1.0), w=b_alTb)
        DVE(lambda e: e.memset(ones_t[:, :], 1.0), w=[b_ones])

        def a0_blk(tb):
            bank = next_bank()
            for k in range(KC):
                PE(lambda e, k=k, tb=tb, bank=bank: e.matmul(
                    pb[bank][0:16, :], wal[:, k, :], xT[:, k, tb * 512:(tb + 1) * 512],
                    start=(k == 0), stop=(k == KC - 1)), r=[b_wal, b_xTb[tb]], w=[bb[bank]])
            DVE(lambda e, tb=tb, bank=bank: e.tensor_copy(alT[0:16, tb * 512:(tb + 1) * 512], pb[bank][0:16, :]),
                r=[bb[bank]], w=[b_alTb[tb]])

        a0_blk(0)

        cs = [A4[:, 10240 + s * 1024:10240 + (s + 1) * 1024].bitcast(F32) for s in range(2)]

        def a1_pre(u):
            s = u % 2
            tb, h = u // 4, u % 4
            bpre = 4 + s
            PE(lambda e, tb=tb, h=h, bpre=bpre: e.matmul(
                pb[bpre][:, :], wg_aug[0:17, h * 128:(h + 1) * 128], alT[0:17, tb * 512:(tb + 1) * 512],
                start=True, stop=True), r=[b_alTb[tb], b_wg, b_wg1], w=[bb[bpre]])
            ACT(lambda e, s=s, bpre=bpre: e.activation(etmp[s], pb[bpre][:, :], AF.Exp, scale=-1.0),
                r=[bb[bpre]], w=[b_et[s]])
            ACT(lambda e, s=s: e.activation(etmp[s], etmp[s], AF.Ln, bias=1.0), r=[b_et[s]], w=[b_et[s]])

        def a1_hilo(u):
            s = u % 2
            for c in range(4):
                DVE(lambda e, s=s, c=c: e.tensor_tensor_scan(
                    cs[s][:, c * 128:(c + 1) * 128], ones_t[:, :], etmp[s][:, c * 128:(c + 1) * 128], 0.0,
                    ALU.mult, ALU.add), r=[b_et[s], b_ones], w=[b_la[s]])

        def a1_cum(u):
            s = u % 2
            tb, h = u // 4, u % 4
            sl = slice(tb * 512, (tb + 1) * 512)
            ACT(lambda e, s=s, h=h, sl=sl: e.activation(EpT[:, h, sl], cs[s], AF.Exp, scale=-1.0 / 16.0),
                r=[b_la[s]], w=[b_EpT[u]])
            ACT(lambda e, s=s, h=h, sl=sl: e.activation(EmT[:, h, sl], cs[s], AF.Exp, scale=1.0 / 16.0),
                r=[b_la[s]], w=[b_EmT[u]])
            ACT(lambda e, s=s, h=h, tb=tb: e.activation(
                eblast[:, tb * 4:(tb + 1) * 4, h], cs[s].rearrange("p (c t) -> p c t", c=4)[:, :, 127],
                AF.Exp, scale=-1.0 / 16.0), r=[b_la[s]], w=[b_eblast[u]])

        b_blast = [Buf() for _ in range(NT)]
        b_eblast = [Buf() for _ in range(NT)]
        b_qbb = [[Buf() for _ in range(4)] for _ in range(4)]
        b_kbb = [[Buf() for _ in range(4)] for _ in range(4)]
        b_kdb = [[Buf() for _ in range(4)] for _ in range(4)]

        def proj_qk(kind, h, tb):
            W, B = wq if kind == 0 else wk
            dstT = qbT if kind == 0 else kbT
            dbuf = b_qbb if kind == 0 else b_kbb
            bank = next_bank()
            for k in range(KC):
                PE(lambda e, k=k, h=h, tb=tb, bank=bank, W=W: e.matmul(
                    pb[bank][:, :], W[:, k, h * 128:(h + 1) * 128], xT[:, k, tb * 512:(tb + 1) * 512],
                    start=(k == 0), stop=(k == KC - 1)), r=[B, b_xTb[tb]], w=[bb[bank]])
            if kind == 0:
                DVE(lambda e, h=h, tb=tb, bank=bank: e.tensor_scalar(
                    qbT[:, h, tb * 512:(tb + 1) * 512], pb[bank][:, :], DK ** -0.5, None, ALU.mult),
                    r=[bb[bank]], w=[dbuf[h][tb]])
            else:
                ACT(lambda e, h=h, tb=tb, bank=bank: e.copy(
                    kbT[:, h, tb * 512:(tb + 1) * 512], pb[bank][:, :]), r=[bb[bank]], w=[dbuf[h][tb]])

        fin_pending = []

        def finalize_one(tb, h):
            sl = slice(tb * 512, (tb + 1) * 512)
            DVE(lambda e, h=h, sl=sl: e.tensor_tensor(qbT[:, h, sl], qbT[:, h, sl], EpT[:, h, sl], ALU.mult),
                r=b_EpT[tb * 4:(tb + 1) * 4], w=[b_qbb[h][tb]])
            DVE(lambda e, h=h, sl=sl: e.tensor_tensor(kbT[:, h, sl], kbT[:, h, sl], EmT[:, h, sl], ALU.mult),
                r=b_EmT[tb * 4:(tb + 1) * 4], w=[b_kbb[h][tb]])
            POOL(lambda e, h=h, sl=sl, tb=tb: e.tensor_tensor(
                kdT[:, h, sl].rearrange("p (c t) -> p c t", c=4),
                kbT[:, h, sl].rearrange("p (c t) -> p c t", c=4),
                eblast[:, tb * 4:(tb + 1) * 4, h].unsqueeze(2).broadcast_to([128, 4, 128]), ALU.mult),
                r=[b_kbb[h][tb]] + b_eblast[tb * 4:(tb + 1) * 4], w=[b_kdb[h][tb]])

        def finalize_qk(tb):
            for h in range(4):
                fin_pending.append((tb, h))

        def drain_fin(k=1):
            for _ in range(k):
                if fin_pending:
                    finalize_one(*fin_pending.pop(0))

        fill = [(lambda kind=kind, h=h, tb=tb: proj_qk(kind, h, tb))
                for tb in range(4) for kind in range(2) for h in range(4)]
        for t in range(NT):
            if t + 4 < NT:
                p0_tile(t + 4)
            if t + 8 < NT:
                x_load(t + 8)
            if t == 7:
                vW[0] = ld(R[2], w_in, V0)
                gW[0] = ld(XA, w_in, G0)
                gW[1] = ld(XB, w_in, G0 + 512)
            if t == 11:
                vW[1] = ld(XC, w_in, V0 + 512, extra=[p0_last[15]])
            a1_pre(t)
            fill.pop(0)()
            a1_hilo(t)
            if t > 0:
                a1_cum(t - 1)
                if t % 4 == 0:
                    finalize_qk(t // 4 - 1)
            drain_fin(1)
            fill.pop(0)()
            if t + 4 < NT and (t + 4) % 4 == 3:
                a0_blk((t + 4) // 4)
        a1_cum(NT - 1)
        finalize_qk(3)
        assert not fill


        NVS = 2
        v_s = [A2[:, s * 1024:(s + 1) * 1024] for s in range(NVS)]
        sg_s = [A2[:, 2048 + s * 1024:2048 + (s + 1) * 1024] for s in range(NVS)]
        b_vs = [Buf() for _ in range(NVS)]
        b_sgs = [Buf() for _ in range(NVS)]

        TB = 8192
        kd_s = [A4[:, TB + s * 256:TB + (s + 1) * 256].rearrange("p (j t) -> p j t", j=2) for s in range(2)]
        sT_s = [A4[:, TB + 512 + s * 256:TB + 512 + (s + 1) * 256].rearrange("p (j t) -> p j t", j=2) for s in range(2)]
        yb_s = [A4[:, TB + 1024 + s * 512:TB + 1024 + (s + 1) * 512].rearrange("p (j f) -> p j f", j=2) for s in range(2)]
        yb_s.append(A4[:, TB + 6656:TB + 7168].rearrange("p (j f) -> p j f", j=2))
        Sbf_s = [[A4[:, TB + 2048 + (hp * 2 + s) * 512:TB + 2048 + (hp * 2 + s + 1) * 512].rearrange(
            "p (j f) -> p j f", j=2) for s in range(2)] for hp in range(2)]
        S_f = [A4[:, TB + 4096 + hp * 1024:TB + 4096 + (hp + 1) * 1024].bitcast(F32).rearrange(
            "p (j f) -> p j f", j=2) for hp in range(2)]
        junk = A4[:, TB + 6144:TB + 6656].bitcast(F32)
        ss_s = [small[:, s * 2:(s + 1) * 2] for s in range(2)]
        rs_s = [small[:, 4 + s * 2:4 + (s + 1) * 2] for s in range(2)]
        b_kds = [Buf() for _ in range(2)]
        b_sTs = [Buf() for _ in range(2)]
        b_ybs = [Buf() for _ in range(3)]
        b_Sbf = [[Buf() for _ in range(2)] for _ in range(2)]
        b_S = [Buf() for _ in range(2)]
        b_ss = [Buf() for _ in range(2)]
        b_rs = [Buf() for _ in range(2)]
        b_yaT = [Buf() for _ in range(NT)]

        ga0 = ld(R[0], w_in, GA0)
        wa0 = ld(R[1], w_a, 0)

        def proj_vg(c, kind, blk):
            slot = c % NVS
            W, B = (vW if kind == 0 else gW)[blk]
            bank = next_bank(2, 2)
            for k in range(KC):
                PE(lambda e, k=k, c=c, bank=bank, W=W: e.matmul(
                    pb[bank][:, :], xT[:, k, c * 128:(c + 1) * 128], W[:, k, :],
                    start=(k == 0), stop=(k == KC - 1)), r=[B, b_xT], w=[bb[bank]])
            if kind == 0:
                DVE(lambda e, slot=slot, blk=blk, bank=bank: e.tensor_copy(
                    v_s[slot][:, blk * 512:(blk + 1) * 512], pb[bank][:, :]), r=[bb[bank]], w=[b_vs[slot]])
            else:
                ACT(lambda e, slot=slot, blk=blk, bank=bank: e.activation(
                    sg_s[slot][:, blk * 512:(blk + 1) * 512], pb[bank][:, :], AF.Silu), r=[bb[bank]], w=[b_sgs[slot]])

        def st_T1S(n):
            c, hp = n // 2, n % 2
            s = n % 2
            tb = c // 4
            for j in range(2):
                h = hp * 2 + j
                PE(lambda e, c=c, j=j, h=h: e.transpose(
                    pbT[4][:, j * 128:(j + 1) * 128], kdT[:, h, c * 128:(c + 1) * 128], ident[:]),
                   r=[b_kdb[h][tb], b_ident], w=[bb[4]])
            for j in range(2):
                h = hp * 2 + j
                PE(lambda e, c=c, j=j, h=h: e.matmul(
                    pb[5][:, j * 128:(j + 1) * 128],
                    kbT[:, h, c * 128:(c + 1) * 128], qbT[:, h, c * 128:(c + 1) * 128], start=True, stop=True),
                   r=[b_kbb[h][tb], b_qbb[h][tb]], w=[bb[5]])
            ACT(lambda e, s=s: e.copy(kd_s[s], pbT[4][:, 0:256].rearrange("p (j t) -> p j t", j=2)),
                r=[bb[4]], w=[b_kds[s]])
            DVE(lambda e, s=s: e.tensor_tensor(
                sT_s[s], pb[5][:, 0:256].rearrange("p (j t) -> p j t", j=2),
                lt[:].unsqueeze(1).broadcast_to([128, 2, 128]), ALU.mult),
                r=[bb[5], b_lt], w=[b_sTs[s]])

        def st_UO1(n):
            c, hp = n // 2, n % 2
            s = n % 2
            slot = c % NVS
            for j in range(2):
                h = hp * 2 + j
                PE(lambda e, s=s, j=j, h=h, slot=slot: e.matmul(
                    pb[6][:, j * 256:(j + 1) * 256], kd_s[s][:, j, :], v_s[slot][:, h * 256:(h + 1) * 256],
                    start=True, stop=True), r=[b_kds[s], b_vs[slot]], w=[bb[6]])
            for j in range(2):
                h = hp * 2 + j
                PE(lambda e, c=c, s=s, j=j, h=h, slot=slot: e.matmul(
                    pb[s][:, j * 256:(j + 1) * 256], sT_s[s][:, j, :], v_s[slot][:, h * 256:(h + 1) * 256],
                    start=(j == 0), stop=(c == 0), skip_group_check=True), r=[b_sTs[s], b_vs[slot]], w=[bb[s]])

        def st_O2(n):
            c, hp = n // 2, n % 2
            s = n % 2
            if c == 0:
                return
            tb = c // 4
            for j in range(2):
                h = hp * 2 + j
                PE(lambda e, c=c, s=s, j=j, h=h, hp=hp: e.matmul(
                    pb[s][:, j * 256:(j + 1) * 256], qbT[:, h, c * 128:(c + 1) * 128], Sbf_s[hp][c % 2][:, j, :],
                    start=False, stop=True, skip_group_check=True),
                   r=[b_qbb[h][tb], b_Sbf[hp][c % 2]], w=[bb[s]])

        def st_state(n):
            c, hp = n // 2, n % 2
            s = n % 2
            if c == NT - 1:
                return
            for j in range(2):
                h = hp * 2 + j
                if c == 0:
                    DVE(lambda e, s=s, j=j, hp=hp: e.tensor_copy(S_f[hp][:, j, :], pb[6][:, j * 256:(j + 1) * 256]),
                        r=[bb[6]], w=[b_S[hp]])
                else:
                    DVE(lambda e, c=c, s=s, j=j, h=h, hp=hp: e.scalar_tensor_tensor(
                        S_f[hp][:, j, :], S_f[hp][:, j, :], eblast[:, c, h:h + 1], pb[6][:, j * 256:(j + 1) * 256],
                        ALU.mult, ALU.add), r=[bb[6], b_S[hp], b_eblast[(c // 4) * 4 + h]], w=[b_S[hp]])
            ACT(lambda e, c=c, hp=hp: e.copy(Sbf_s[hp][(c + 1) % 2], S_f[hp]), r=[b_S[hp]], w=[b_Sbf[hp][(c + 1) % 2]])

        def st_sq(n):
            c, hp = n // 2, n % 2
            s = n % 2
            for j in range(2):
                ACT(lambda e, s=s, j=j: e.activation(junk, pb[s][:, j * 256:(j + 1) * 256], AF.Square,
                                                    accum_out=ss_s[s][:, j:j + 1]),
                    r=[bb[s]], w=[b_ss[s]])
            POOL(lambda e, s=s: e.tensor_tensor(rs_s[s], ss_s[s], cpow[:, 4:6], ALU.add),
                 r=[b_ss[s], b_cpow], w=[b_rs[s]])
            POOL(lambda e, s=s: e.tensor_tensor(rs_s[s], rs_s[s], cpow[:, 0:2], ALU.pow),
                 r=[b_cpow], w=[b_rs[s]])

        def st_y(n):
            c, hp = n // 2, n % 2
            s = n % 2
            slot = c % NVS
            for j in range(2):
                h = hp * 2 + j
                DVE(lambda e, s=s, j=j, h=h, slot=slot, y3=n % 3: e.scalar_tensor_tensor(
                    yb_s[y3][:, j, :], pb[s][:, j * 256:(j + 1) * 256], rs_s[s][:, j:j + 1],
                    sg_s[slot][:, h * 256:(h + 1) * 256], ALU.mult, ALU.mult),
                    r=[bb[s], b_rs[s], b_sgs[slot]], w=[b_ybs[n % 3]])

        def st_Y(n):
            c, hp = n // 2, n % 2
            s = n % 2
            for j in range(2):
                for i in range(2):
                    q = j * 2 + i
                    PE(lambda e, y3=n % 3, j=j, i=i, q=q: e.transpose(
                        pbT[7][:, q * 128:(q + 1) * 128],
                        yb_s[y3][:, j, i * 128:(i + 1) * 128], ident[:]),
                       r=[b_ybs[n % 3], b_ident], w=[bb[7]])
            ACT(lambda e, c=c, hp=hp: e.copy(
                y_aT[:, hp * 4:(hp + 1) * 4, c * 128:(c + 1) * 128],
                pbT[7][:, 0:512].rearrange("p (q t) -> p q t", q=4)),
                r=[bb[7]], w=[b_yaT[c]])

        maT = A4[:].rearrange("p (k t) -> p k t", k=KC)
        b_ma = [[Buf() for _ in range(4)] for _ in range(KC)]
        sgm = [sgm_t[:, s, :] for s in range(2)]
        tmpf = [A2[:, 2048 + s * 1024:2048 + (s + 1) * 1024].bitcast(F32) for s in range(2)]
        b_sgm = [Buf() for _ in range(2)]
        b_tmpf = [Buf() for _ in range(2)]
        sctr = [0]

        def gate_piece(tb, dc, gWt, wWt, actT, act_bufs, accumulate, banks=None):
            m = dc % 4
            s = sctr[0] % 2
            sctr[0] += 1
            gWa, gB = gWt
            wWa, wB = wWt
            bank = banks[0] if banks else next_bank()
            for k in range(KC):
                PE(lambda e, k=k, m=m, tb=tb, bank=bank, gWa=gWa: e.matmul(
                    pb[bank][:, :], gWa[:, k, m * 128:(m + 1) * 128],
                    xT[:, k, tb * 512:(tb + 1) * 512], start=(k == 0), stop=(k == KC - 1)),
                   r=[gB, b_xT], w=[bb[bank]])
            ACT(lambda e, s=s, bank=bank: e.activation(sgm[s], pb[bank][:, :], AF.Sigmoid),
                r=[bb[bank]], w=[b_sgm[s]])
            bank2 = banks[1] if banks else next_bank()
            for k in range(KC):
                PE(lambda e, k=k, m=m, tb=tb, bank2=bank2, wWa=wWa: e.matmul(
                    pb[bank2][:, :], wWa[:, k, m * 128:(m + 1) * 128],
                    actT[:, k, tb * 512:(tb + 1) * 512], start=(k == 0), stop=(k == KC - 1)),
                   r=[wB] + act_bufs, w=[bb[bank2]])
            if not accumulate:
                DVE(lambda e, s=s, dc=dc, tb=tb, bank2=bank2: e.tensor_tensor(
                    maT[:, dc, tb * 512:(tb + 1) * 512], pb[bank2][:, :], sgm[s], ALU.mult),
                    r=[bb[bank2], b_sgm[s]], w=[b_ma[dc][tb]])
            else:
                DVE(lambda e, s=s, bank2=bank2: e.tensor_tensor(tmpf[s], pb[bank2][:, :], sgm[s], ALU.mult),
                    r=[bb[bank2], b_sgm[s]], w=[b_tmpf[s]])
                DVE(lambda e, s=s, dc=dc, tb=tb: e.tensor_tensor(
                    maT[:, dc, tb * 512:(tb + 1) * 512], tmpf[s], maT[:, dc, tb * 512:(tb + 1) * 512], ALU.add),
                    r=[b_tmpf[s], b_ma[dc][tb]], w=[b_ma[dc][tb]])


        for kind in range(2):
            for blk in range(2):
                proj_vg(0, kind, blk)
        NS = 2 * NT
        st_T1S(0)
        for n in range(NS):
            c, hp = n // 2, n % 2
            if c + 1 < NT:
                proj_vg(c + 1, hp, 0)
            else:
                gate_piece(hp, 0, ga0, wa0, y_aT, b_yaT[hp * 4:(hp + 1) * 4], False, banks=(2, 3))
            drain_fin(1)
            st_UO1(n)
            if n + 1 < NS:
                st_T1S(n + 1)
            st_state(n)
            if n > 0:
                st_y(n - 1)
            if n > 1:
                st_Y(n - 2)
            if c + 1 < NT:
                proj_vg(c + 1, hp, 1)
            else:
                tbx, dcx = ((2, 0), (0, 1))[hp]
                gate_piece(tbx, dcx, ga0, wa0, y_aT, b_yaT[tbx * 4:(tbx + 1) * 4], False, banks=(2, 3))
            st_O2(n)
            st_sq(n)
            if n == 8:
                scale_w(wa0, cols16[:, 0:8])
        ga1 = ld(R[2], w_in, GA0 + 512)
        wa1 = ld(XC, w_a, 512)
        u0w = ld(XA, w_in, U0)
        u1w = ld(XB, w_in, U0 + 512)

        NB[0] = 8
        if _en('B'):

            gaw = [ga0, ga1]
            waw = [wa0, wa1]
            done = {(0, 0), (0, 1), (0, 2), (1, 0)}
            for dc, tb in ((1, 1), (1, 2)):
                gate_piece(tb, dc, gaw[dc // 4], waw[dc // 4], y_aT, b_yaT[tb * 4:(tb + 1) * 4], False, banks=(2, 3))
                done.add((dc, tb))
            st_y(NS - 1)
            st_Y(NS - 2)
            st_Y(NS - 1)
            cnt = 0
            for dc in range(KC):
                for tb in range(4):
                    if (dc, tb) in done:
                        continue
                    gate_piece(tb, dc, gaw[dc // 4], waw[dc // 4], y_aT, b_yaT[tb * 4:(tb + 1) * 4], False)
                    cnt += 1
                    if cnt == 4:
                        scale_w(wa1, cols16[:, 0:8])

        if _en('C'):

            u_t = A1[:].rearrange("p (t f) -> p t f", t=NT)
            pT = A2[:].rearrange("p (k t) -> p k t", k=KC)
            szT = A3[:].rearrange("p (k t) -> p k t", k=KC)
            b_u = [[Buf() for _ in range(NT)] for _ in range(2)]
            b_pT = [[Buf() for _ in range(4)] for _ in range(KC)]
            b_sz = [[Buf() for _ in range(4)] for _ in range(KC)]
            ectr = [0]
            z0w = ld(R[0], w_in, Z0)
            z1w = ld(R[1], w_in, Z0 + 512)
            gb0 = ld(R[2], w_in, GB0)
            uT = A1[:].rearrange("p (k t) -> p k t", k=KC)
            S_t = xt3[:].bitcast(BF16)
            b_uT = [Buf() for _ in range(KC)]
            b_pTc = [Buf() for _ in range(KC)]
            b_S = Buf()
            b_pfix = Buf()
            last_u_mm = [None]
            for blk in range(2):
                uW, uB = (u0w, u1w)[blk]
                for m in range(4):
                    cc = blk * 4 + m
                    for tb in range(4):
                        bank = next_bank()
                        for k in range(KC):
                            last_u_mm[0] = PE(lambda e, k=k, m=m, tb=tb, bank=bank, uW=uW: e.matmul(
                                pb[bank][:, :], uW[:, k, m * 128:(m + 1) * 128], xT[:, k, tb * 512:(tb + 1) * 512],
                                start=(k == 0), stop=(k == KC - 1)), r=[uB, b_xT], w=[bb[bank]])
                        ACT(lambda e, cc=cc, tb=tb, bank=bank: e.copy(
                            uT[:, cc, tb * 512:(tb + 1) * 512], pb[bank][:, :]), r=[bb[bank]], w=[b_uT[cc]])

            def pool_chunk(cc):
                w = 2 ** (cc // 2 + 1)
                uc = uT[:, cc, :]
                ex = [last_u_mm[0]] if cc >= 4 else []
                DVE(lambda e, uc=uc, w=w: e.tensor_tensor_scan(
                    S_t[:, 0:w], ones_t[:, 0:w], uc[:, 0:w], 0.0, ALU.mult, ALU.add),
                    r=[b_uT[cc], b_ones], w=[b_S])
                DVE(lambda e, uc=uc, w=w: e.tensor_tensor_scan(
                    S_t[:, w:T], uc[:, w:T], uc[:, 0:T - w], S_t[:, w - 1:w], ALU.add, ALU.subtract),
                    r=[b_uT[cc]], w=[b_S])
                DVE(lambda e, cc=cc, uc=uc, w=w: e.scalar_tensor_tensor(
                    pT[:, cc, :], S_t[:, :], 1.0 / w, uc, ALU.mult, ALU.subtract),
                    r=[b_S, b_uT[cc]], w=[b_pTc[cc]], extra=ex)
                DVE(lambda e, w=w: e.tensor_tensor(pfix[:, 0:w - 1], S_t[:, 0:w - 1], invc[:, 0:w - 1], ALU.mult),
                    r=[b_S, b_invc], w=[b_pfix])
                DVE(lambda e, cc=cc, uc=uc, w=w: e.tensor_tensor(
                    pT[:, cc, 0:w - 1], pfix[:, 0:w - 1], uc[:, 0:w - 1], ALU.subtract),
                    r=[b_pfix, b_uT[cc]], w=[b_pTc[cc]])

            for cc in range(KC):
                pool_chunk(cc)
            pool_done = [P.last["dve"]]
            wb0 = ld(X123[0], w_b, 0, extra=pool_done)
            wb1 = ld(X123[1], w_b, 512, extra=pool_done)
            wo0 = ld(X123[2], w_o, 0, extra=pool_done)
            for blk in range(2):
                zW, zB = (z0w, z1w)[blk]
                for m in range(4):
                    zc = blk * 4 + m
                    for tb in range(4):
                        bank = next_bank()
                        for k in range(KC):
                            PE(lambda e, k=k, m=m, tb=tb, bank=bank, zW=zW: e.matmul(
                                pb[bank][:, :], zW[:, k, m * 128:(m + 1) * 128], xT[:, k, tb * 512:(tb + 1) * 512],
                                start=(k == 0), stop=(k == KC - 1)), r=[zB, b_xT], w=[bb[bank]])
                        ACT(lambda e, zc=zc, tb=tb, bank=bank: e.activation(
                            szT[:, zc, tb * 512:(tb + 1) * 512], pb[bank][:, :], AF.Silu),
                            r=[bb[bank]], w=[b_sz[zc][tb]])
            gb1 = ld(R[0], w_in, GB0 + 512)
            wo1 = ld(R[1], w_o, 512)
            scale_w(wb0, colsT[:, 16:24])
            scale_w(wb1, colsT[:, 16:24])
            for tb in range(4):
                for dc in range(KC):
                    g = dc // 2
                    j = dc % 2
                    bank = next_bank()
                    for i in range(2):
                        PE(lambda e, g=g, j=j, i=i, tb=tb, bank=bank: e.matmul(
                            pb[bank][:, :], poolw[:, g * 2 + i, j * 128:(j + 1) * 128],
                            pT[:, g * 2 + i, tb * 512:(tb + 1) * 512], start=(i == 0), stop=(i == 1)),
                           r=[b_poolw, b_pTc[g * 2 + i]], w=[bb[bank]])
                    DVE(lambda e, dc=dc, tb=tb, bank=bank: e.scalar_tensor_tensor(
                        szT[:, dc, tb * 512:(tb + 1) * 512], pb[bank][:, :], colsT[:, 8 + dc:9 + dc],
                        szT[:, dc, tb * 512:(tb + 1) * 512], ALU.add, ALU.mult),
                        r=[bb[bank], b_cols, b_sz[dc][tb]], w=[b_sz[dc][tb]])

        if _en('E'):
            lasts = [P.last[e] for e in ("pe", "act", "dve")]

            gbw = [gb0, gb1]
            wbw = [wb0, wb1]
            wow = [wo0, wo1]
            xt = [A2[:, 4096 + s * 2048:4096 + (s + 1) * 2048].bitcast(F32) for s in range(2)]
            rr = [A2[:, 8192 + s * 2048:8192 + (s + 1) * 2048].bitcast(F32) for s in range(2)]
            yo = [A2[:, 12288 + s * 2048:12288 + (s + 1) * 2048].bitcast(F32) for s in range(2)]
            lnw_t = A1[:, 12288:14336].bitcast(F32)
            lnb_t = A1[:, 14336:16384].bitcast(F32)
            xt.append(xt3[:])
            b_xt = [Buf() for _ in range(3)]
            b_rr = [Buf() for _ in range(2)]
            b_yo = [Buf() for _ in range(2)]
            b_ln = Buf()
            stats = [small[:, 8 + s * 12:8 + (s + 1) * 12] for s in range(2)]
            mv = [small[:, 32 + s * 2:32 + (s + 1) * 2] for s in range(2)]
            rstd = [small[:, 36 + s:37 + s] for s in range(2)]
            nmr = [small[:, 38 + s:39 + s] for s in range(2)]
            b_st = [Buf() for _ in range(2)]
            b_mv = [Buf() for _ in range(2)]
            b_rn = [Buf() for _ in range(2)]
            sums = [small[:, 40 + s * 2:40 + (s + 1) * 2] for s in range(2)]
            b_sums = [Buf() for _ in range(2)]
            for i in range(4):
                rr.append(A3[:, i * 2048:(i + 1) * 2048].bitcast(F32))
                yo.append(A3[:, 8192 + i * 2048:8192 + (i + 1) * 2048].bitcast(F32))
                mv.append(small2[:, i * 2:(i + 1) * 2])
                rstd.append(small2[:, 8 + i:9 + i])
                nmr.append(small2[:, 12 + i:13 + i])
                sums.append(small2[:, 16 + i * 2:16 + (i + 1) * 2])
                for lst in (b_rr, b_yo, b_mv, b_rn, b_sums):
                    lst.append(Buf())

            def eslot(t):
                return t % 2 if t < NT - 4 else 2 + (t - (NT - 4))
            P.add("sp", lambda e: e.dma_start(out=lnw_t, in_=ln_w[0:1, :].broadcast_to([128, D])),
                  writes=[b_ln], lane="c_ln", extra=lasts)
            P.add("sp", lambda e: e.dma_start(out=lnb_t, in_=ln_b[0:1, :].broadcast_to([128, D])),
                  writes=[b_ln], lane="c_ln", extra=lasts)
            stores = []

            def d_piece(tb, dc):
                blk = dc // 4
                m = dc % 4
                s = sctr[0] % 2
                sctr[0] += 1
                gW, gB = gbw[blk]
                wW, wB = wbw[blk]
                bank = next_bank()
                for k in range(KC):
                    PE(lambda e, k=k, m=m, tb=tb, bank=bank, gW=gW: e.matmul(
                        pb[bank][:, :], gW[:, k, m * 128:(m + 1) * 128],
                        xT[:, k, tb * 512:(tb + 1) * 512], start=(k == 0), stop=(k == KC - 1)),
                       r=[gB, b_xT], w=[bb[bank]])
                ACT(lambda e, s=s, bank=bank: e.activation(sgm[s], pb[bank][:, :], AF.Sigmoid),
                    r=[bb[bank]], w=[b_sgm[s]])
                bank2 = next_bank()
                for k in range(KC):
                    PE(lambda e, k=k, m=m, tb=tb, bank2=bank2, wW=wW: e.matmul(
                        pb[bank2][:, :], wW[:, k, m * 128:(m + 1) * 128],
                        szT[:, k, tb * 512:(tb + 1) * 512], start=(k == 0), stop=(k == KC - 1)),
                       r=[wB] + [b_sz[k][tb] for k in range(KC)], w=[bb[bank2]])
                DVE(lambda e, s=s, bank2=bank2: e.tensor_tensor(tmpf[s], pb[bank2][:, :], sgm[s], ALU.mult),
                    r=[bb[bank2], b_sgm[s]], w=[b_tmpf[s]])
                DVE(lambda e, s=s, dc=dc, tb=tb: e.tensor_tensor(
                    maT[:, dc, tb * 512:(tb + 1) * 512], tmpf[s], maT[:, dc, tb * 512:(tb + 1) * 512], ALU.add),
                    r=[b_tmpf[s], b_ma[dc][tb]], w=[b_ma[dc][tb]])

            def ld_x(t):
                s = t % 3
                P.add("sp", lambda e, t=t, s=s: e.dma_start(out=xt[s], in_=x[t * 128:(t + 1) * 128, :]),
                      writes=[b_xt[s]], lane=f"xt{s}", extra=lasts)

            def e_tile(t):
                s = eslot(t)
                x3 = t % 3
                if t + 2 < NT:
                    ld_x(t + 2)
                for half in range(2):
                    bank = next_bank()
                    oW, oB = wow[half]
                    for k in range(KC):
                        PE(lambda e, k=k, t=t, bank=bank, oW=oW: e.matmul(
                            pb[bank][:, :], maT[:, k, t * 128:(t + 1) * 128], oW[:, k, :],
                            start=(k == 0), stop=(k == KC - 1)),
                           r=[oB] + [b_ma[k][t // 4] for k in range(KC)], w=[bb[bank]])
                    DVE(lambda e, s=s, x3=x3, half=half, bank=bank: e.scalar_tensor_tensor(
                        rr[s][:, half * 512:(half + 1) * 512], xt[x3][:, half * 512:(half + 1) * 512], ALPHA,
                        pb[bank][:, :], ALU.mult, ALU.add), r=[bb[bank], b_xt[x3]], w=[b_rr[s]])
                ACT(lambda e, s=s: e.activation(junk2[:], rr[s], AF.Identity, accum_out=sums[s][:, 0:1]),
                    r=[b_rr[s]], w=[b_sums[s]])
                ACT(lambda e, s=s: e.activation(junk2[:], rr[s], AF.Square, accum_out=sums[s][:, 1:2]),
                    r=[b_rr[s]], w=[b_sums[s]])
                if t == NT - 1:
                    e_norm(t - 1)
                    e_fin(t - 2)
                    DVE(lambda e, s=s: e.tensor_tensor(mv[s], sums[s], cpow[:, 10:12], ALU.mult),
                        r=[b_sums[s], b_cpow], w=[b_mv[s]])
                    DVE(lambda e, s=s: e.scalar_tensor_tensor(rstd[s], mv[s][:, 0:1], mv[s][:, 0:1], mv[s][:, 1:2],
                                                             ALU.mult, ALU.subtract),
                        r=[b_mv[s]], w=[b_rn[s]])
                    ACT(lambda e, s=s: e.activation(rstd[s], rstd[s], AF.Sqrt, bias=EPS, scale=-1.0),
                        r=[b_rn[s]], w=[b_rn[s]])
                    DVE(lambda e, s=s: e.reciprocal(rstd[s], rstd[s]), r=[b_rn[s]], w=[b_rn[s]])
                    DVE(lambda e, s=s: e.tensor_tensor(nmr[s], mv[s][:, 0:1], rstd[s], ALU.mult),
                        r=[b_mv[s], b_rn[s]], w=[b_rn[s]])
                    return
                POOL(lambda e, s=s: e.tensor_tensor(mv[s], sums[s], cpow[:, 8:10], ALU.mult),
                     r=[b_sums[s], b_cpow], w=[b_mv[s]])
                POOL(lambda e, s=s: e.tensor_tensor(rstd[s], mv[s][:, 0:1], mv[s][:, 0:1], ALU.mult),
                     r=[b_mv[s]], w=[b_rn[s]])
                POOL(lambda e, s=s: e.tensor_tensor(rstd[s], mv[s][:, 1:2], rstd[s], ALU.subtract),
                     r=[b_mv[s]], w=[b_rn[s]])
                POOL(lambda e, s=s: e.tensor_tensor(rstd[s], rstd[s], cpow[:, 6:7], ALU.add),
                     r=[b_cpow], w=[b_rn[s]])
                POOL(lambda e, s=s: e.tensor_tensor(rstd[s], rstd[s], cpow[:, 0:1], ALU.pow),
                     r=[b_cpow], w=[b_rn[s]])
                POOL(lambda e, s=s: e.tensor_tensor(nmr[s], mv[s][:, 0:1], rstd[s], ALU.mult),
                     r=[b_mv[s]], w=[b_rn[s]])
                POOL(lambda e, s=s: e.tensor_tensor(nmr[s], nmr[s], cpow[:, 7:8], ALU.mult),
                     r=[b_cpow], w=[b_rn[s]])

            b_yoh = [Buf(), Buf()]

            def e_norm(t):
                s = eslot(t)
                if t == NT - 1:
                    for half in range(2):
                        cs_ = slice(half * 512, (half + 1) * 512)
                        ACT(lambda e, s=s, cs_=cs_: e.activation(yo[s][:, cs_], rr[s][:, cs_], AF.Identity,
                                                                bias=nmr[s], scale=rstd[s]),
                            r=[b_rr[s], b_rn[s]], w=[b_yoh[half]])
                    return
                ACT(lambda e, s=s: e.activation(yo[s], rr[s], AF.Identity, bias=nmr[s], scale=rstd[s]),
                    r=[b_rr[s], b_rn[s]], w=[b_yo[s]])

            def e_fin(t):
                s = eslot(t)
                if t == NT - 1:
                    for half in range(2):
                        cs_ = slice(half * 512, (half + 1) * 512)
                        bh = Buf()
                        DVE(lambda e, s=s, cs_=cs_: e.tensor_tensor(yo[s][:, cs_], yo[s][:, cs_], lnw_t[:, cs_], ALU.mult),
                            r=[b_yoh[half], b_ln], w=[bh])
                        DVE(lambda e, s=s, cs_=cs_: e.tensor_tensor(yo[s][:, cs_], yo[s][:, cs_], lnb_t[:, cs_], ALU.add),
                            r=[bh, b_ln], w=[bh])
                        stores.append(P.add("sp", lambda e, t=t, s=s, cs_=cs_: e.dma_start(
                            out=out[t * 128:(t + 1) * 128, cs_], in_=yo[s][:, cs_]), reads=[bh], lane=f"outh{half}"))
                    return
                DVE(lambda e, s=s: e.tensor_tensor(yo[s], yo[s], lnw_t, ALU.mult), r=[b_yo[s], b_ln], w=[b_yo[s]])
                if True:
                    DVE(lambda e, s=s: e.tensor_tensor(yo[s], yo[s], lnb_t, ALU.add), r=[b_yo[s], b_ln], w=[b_yo[s]])
                else:
                    POOL(lambda e, s=s: e.tensor_tensor(yo[s], yo[s], lnb_t, ALU.add), r=[b_yo[s], b_ln], w=[b_yo[s]])
                stores.append(P.add("sp", lambda e, t=t, s=s: e.dma_start(out=out[t * 128:(t + 1) * 128, :], in_=yo[s]),
                                    reads=[b_yo[s]], lane=f"out{s}"))

            ld_x(0)
            ld_x(1)
            for dc in range(KC):
                d_piece(0, dc)
            for tb in range(4):
                for i in range(4):
                    t = 4 * tb + i
                    if tb + 1 < 4:
                        d_piece(tb + 1, 2 * i)
                        if i == 3:
                            d_piece(tb + 1, 2 * i + 1)
                    if t == NT - 4:
                        ACT(lambda e: e.activation(small[:, 56:57], cpow[:, 6:7], AF.Sqrt), r=[b_cpow])
                    e_tile(t)
                    if t > 0 and t != NT - 1:
                        e_norm(t - 1)
                    if t > 1 and t != NT - 1:
                        e_fin(t - 2)
                    if tb + 1 < 4 and i != 3:
                        d_piece(tb + 1, 2 * i + 1)
            e_norm(NT - 1)
            e_fin(NT - 2)
            e_fin(NT - 1)

        if dbg:
            lasts = P.barrier()
        for name, (ap, shape, dt, bufs) in dbg_specs(locals(), dbg).items():
            dten = nc.dram_tensor("dbg_" + name, shape, dt, kind="ExternalOutput").ap()
            stores.append(P.add("sp", lambda e, dten=dten, ap=ap: e.dma_start(out=dten, in_=ap),
                                reads=bufs, lane="dbg_" + name, extra=lasts))
            dbg_out[name] = "dbg_" + name

        P.add("sp", lambda e: e.nop(), extra=stores)

        lanes = P.finalize()
        sems = {ln: es.enter_context(nc.semaphore("s_" + ln)) for ln in lanes}
        with nc.Block() as block:
            @block.tensor
            def _(e):
                P.replay("pe", e, sems)

            @block.scalar
            def _(e):
                P.replay("act", e, sems)

            @block.vector
            def _(e):
                P.replay("dve", e, sems)

            @block.gpsimd
            def _(e):
                P.replay("pool", e, sems)

            @block.sync
            def _(e):
                P.replay("sp", e, sems)
    return nc, dbg_out


def dbg_specs(loc, dbg):
    specs = {}
    for name in dbg:
        if name == "xT":
            specs[name] = (loc["xT"][:], [128, KC, T], BF16, [loc["b_xT"]])
        elif name == "y_aT":
            specs[name] = (loc["A3"][:], [128, 16384], BF16, [])
        elif name == "maT":
            specs[name] = (loc["A4"][:], [128, 16384], BF16, [])
        elif name == "A1":
            specs[name] = (loc["A1"][:], [128, 16384], BF16, [])
        elif name == "A2":
            specs[name] = (loc["A2"][:], [128, 16384], BF16, [])
        elif name == "colsT":
            specs[name] = (loc["colsT"][:], [128, 24], F32, [])
        elif name == "eblast":
            specs[name] = (loc["eblast"][:], [128, NT, 4], F32, [])
    return specs


_CACHE = {}


def _get_program():
    if "nc" not in _CACHE:
        _CACHE["nc"] = build_program()[0]
    return _CACHE["nc"]


def make_in_maps(inputs):
    c = _consts()
    f = lambda a: np.ascontiguousarray(np.asarray(a, dtype=np.float32))
    shared = {
        "w_in": f(inputs["w_in"][0]),
        "w_gate_up": f(inputs["w_gate_up"][0]),
        "b_gate": f(inputs["b_gate"]).reshape(1, 512),
        "gn_w": f(inputs["gn_w"]).reshape(8, 128),
        "pool_w": f(inputs["pool_w"][0]),
        "pool_b": f(inputs["pool_b"]).reshape(8, 128),
        "pool_scale": f(inputs["pool_scale"]).reshape(8, 128),
        "w_a": f(inputs["w_a"][0]),
        "w_b": f(inputs["w_b"][0]),
        "w_o": f(inputs["w_o"][0]),
        "ln_w": f(inputs["ln_w"]).reshape(1, D),
        "ln_b": f(inputs["ln_b"]).reshape(1, D),
    }
    shared.update(c)
    xs = f(inputs["x"])
    return [dict(shared, x=xs[b]) for b in range(N_CORES)]


def kernel(**inputs):
    nc = _get_program()
    in_maps = make_in_maps(inputs)
    res = run_bass_kernel_spmd(nc, in_maps, core_ids=list(range(N_CORES)))
    return np.stack([res.results[b]["out"] for b in range(N_CORES)], axis=0).astype(np.float32)
```
